# Optimizing a Trainium2 kernel written in Bass

```python
import math
import jax, jax.numpy as jnp
from jax import lax
import numpy as np

D_MODEL = 2048
BATCH = 2
SEQ = 4096
DEPTH = 2

CHUNK = 64
QBLOCK = 128
N_EVEN = (DEPTH + 1) // 2
N_ODD = DEPTH // 2
EPS = 1e-6

H_A = 8
DH_A = 128
H_B = 4
DK_B = 128
DV_B = 2 * DK_B
NUM_BUCKETS = 32
MAX_DISTANCE = 128
H_C = 4
DK_C = 128
DV_C = 2 * DK_C
ROPE_BASE = 10000.0
D_D = 1024
G_D = 4
SGU_LEN = 128
D_FF = 5632
CONV_WIDTH = 3

AB_IN = 3 * H_A * DH_A + 2 * H_B * 2 * DK_B + H_B * DV_B
AB_WIDTH = H_A * DH_A + H_B * DV_B
CD_IN = 2 * H_C * DK_C + 2 * H_C * DV_C + 2 * D_D
CD_WIDTH = H_C * DV_C + D_D

kernel_name = "hybrid_stickbreak_diffattn_retention_sgu_convffn"


def _rmsnorm(x, g):
    x32 = x.astype(jnp.float32)
    y = x32 * lax.rsqrt(jnp.mean(x32 * x32, axis=-1, keepdims=True) + EPS)
    return (y * g.astype(jnp.float32)).astype(x.dtype)


def _layernorm(x, g, b):
    x32 = x.astype(jnp.float32)
    mu = jnp.mean(x32, axis=-1, keepdims=True)
    var = jnp.mean(jnp.square(x32 - mu), axis=-1, keepdims=True)
    y = (x32 - mu) * lax.rsqrt(var + EPS)
    return (y * g.astype(jnp.float32) + b.astype(jnp.float32)).astype(x.dtype)


def _heads(t, n_heads, dh):
    b, s, _ = t.shape
    return t.reshape(b, s, n_heads, dh).transpose(0, 2, 1, 3)


def _merge_heads(t):
    b, h, s, d = t.shape
    return t.transpose(0, 2, 1, 3).reshape(b, s, h * d)


def _rel_bucket(rel):
    nb = NUM_BUCKETS // 2
    max_exact = nb // 2
    ret = jnp.where(rel > 0, nb, 0)
    n = jnp.abs(rel)
    n_f = jnp.maximum(n, 1).astype(jnp.float32)
    large = max_exact + (jnp.log(n_f / max_exact) / math.log(MAX_DISTANCE / max_exact)
                         * (nb - max_exact)).astype(jnp.int32)
    large = jnp.minimum(large, nb - 1)
    return ret + jnp.where(n < max_exact, n, large)


def _stick_breaking(q, k, v):
    b, h, s, dh = q.shape
    scale = dh ** -0.5
    kpos = jnp.arange(s)

    def block(i):
        start = i * QBLOCK
        qpos = start + jnp.arange(QBLOCK)
        q_blk = lax.dynamic_slice_in_dim(q, start, QBLOCK, axis=2)
        z = jnp.einsum('bhqd,bhkd->bhqk', q_blk, k).astype(jnp.float32) * scale
        earlier = kpos[None, :] < qpos[:, None]
        log_1m = jnp.where(earlier, jax.nn.log_sigmoid(-z), 0.0)
        after = lax.cumsum(log_1m, axis=3, reverse=True) - log_1m
        w = jnp.where(earlier, jnp.exp(jax.nn.log_sigmoid(z) + after), 0.0)
        return jnp.einsum('bhqk,bhkd->bhqd', w.astype(v.dtype), v)

    out = lax.map(block, jnp.arange(s // QBLOCK))
    return out.transpose(1, 2, 0, 3, 4).reshape(b, h, s, dh)


def _diff_attention(q1, q2, k1, k2, v, rel_bias, lam):
    b, h, s, dk = q1.shape
    dv = v.shape[-1]
    scale = dk ** -0.5
    kpos = jnp.arange(s)

    def block(i):
        start = i * QBLOCK
        qpos = start + jnp.arange(QBLOCK)
        bucket = _rel_bucket(kpos[None, :] - qpos[:, None])
        bias = jnp.transpose(rel_bias[bucket], (2, 0, 1)).astype(jnp.float32)
        allowed = (kpos[None, :] // CHUNK) <= (qpos[:, None] // CHUNK)

        def probs(qq, kk):
            q_blk = lax.dynamic_slice_in_dim(qq, start, QBLOCK, axis=2)
            logits = jnp.einsum('bhqd,bhkd->bhqk', q_blk, kk).astype(jnp.float32) * scale + bias
            return jax.nn.softmax(jnp.where(allowed, logits, -jnp.inf), axis=-1)

        a = probs(q1, k1) - lam * probs(q2, k2)
        return jnp.einsum('bhqk,bhkd->bhqd', a.astype(v.dtype), v)

    out = lax.map(block, jnp.arange(s // QBLOCK))
    return out.transpose(1, 2, 0, 3, 4).reshape(b, h, s, dv)


def _rotary(t):
    s, d = t.shape[2], t.shape[3]
    inv_freq = ROPE_BASE ** (-jnp.arange(0, d, 2, dtype=jnp.float32) / d)
    ang = jnp.arange(s, dtype=jnp.float32)[:, None] * inv_freq[None, :]
    cos = jnp.concatenate([jnp.cos(ang), jnp.cos(ang)], axis=-1)
    sin = jnp.concatenate([jnp.sin(ang), jnp.sin(ang)], axis=-1)
    t1, t2 = jnp.split(t, 2, axis=-1)
    return t * cos + jnp.concatenate([-t2, t1], axis=-1) * sin


def _retention(q, k, v):
    b, h, s, dk = q.shape
    dv = v.shape[-1]
    n = s // CHUNK
    log_g = jnp.log(1.0 - 2.0 ** (-5.0 - jnp.arange(h, dtype=jnp.float32)))
    idx = jnp.arange(CHUNK, dtype=jnp.float32)
    intra_decay = jnp.exp(log_g[:, None, None] * jnp.abs(idx[:, None] - idx[None, :]))
    q_decay = jnp.exp(log_g[:, None] * (idx + 1.0))
    k_decay = jnp.exp(log_g[:, None] * (CHUNK - 1.0 - idx))
    chunk_decay = jnp.exp(log_g * CHUNK)

    qc = q.reshape(b, h, n, CHUNK, dk)
    kc = k.reshape(b, h, n, CHUNK, dk)
    vc = v.reshape(b, h, n, CHUNK, dv)
    scores = jnp.einsum('bhncd,bhnmd->bhncm', qc, kc) * intra_decay[None, :, None]
    intra = jnp.einsum('bhncm,bhnme->bhnce', scores, vc)
    kv = jnp.einsum('bhncd,bhnce->bhnde', kc * k_decay[None, :, None, :, None], vc)

    def step(state, kv_n):
        return chunk_decay[None, :, None, None] * state + kv_n, state

    _, prev = lax.scan(step, jnp.zeros((b, h, dk, dv), jnp.float32),
                       kv.transpose(2, 0, 1, 3, 4))
    cross = jnp.einsum('bhncd,nbhde->bhnce', qc, prev) * q_decay[None, :, None, :, None]
    return (intra + cross).reshape(b, h, s, dv)


def _spatial_gate(v, w_s, b_s):
    b, s, _ = v.shape
    vg = v.reshape(b, s // SGU_LEN, SGU_LEN, G_D, D_D // G_D)
    pos = jnp.arange(SGU_LEN)
    mask = (pos[None, :] // CHUNK) <= (pos[:, None] // CHUNK)
    w = jnp.where(mask[None], w_s, 0.0)
    out = jnp.einsum('gij,bnjgc->bnigc', w, vg) + b_s.T[None, None, :, :, None]
    return out.reshape(b, s, D_D)


def _mixer_ab(h, w_in, w_out, rel_bias, lam_vecs, subln_g, lam_init):
    b, s, _ = h.shape
    p = h @ w_in
    sa = H_A * DH_A
    sbq = H_B * 2 * DK_B
    qa, ka, va, qb, kb, vb = jnp.split(
        p, np.cumsum([sa, sa, sa, sbq, sbq])[:5].tolist(), axis=-1)
    o_a = _stick_breaking(_heads(qa, H_A, DH_A), _heads(ka, H_A, DH_A), _heads(va, H_A, DH_A))
    qb = qb.reshape(b, s, H_B, 2, DK_B).transpose(0, 2, 3, 1, 4)
    kb = kb.reshape(b, s, H_B, 2, DK_B).transpose(0, 2, 3, 1, 4)
    lv = lam_vecs.astype(jnp.float32)
    lam = jnp.exp(jnp.sum(lv[0] * lv[1])) - jnp.exp(jnp.sum(lv[2] * lv[3])) + lam_init
    o_b = _diff_attention(qb[:, :, 0], qb[:, :, 1], kb[:, :, 0], kb[:, :, 1],
                          _heads(vb, H_B, DV_B), rel_bias, lam)
    o_b = _rmsnorm(o_b, subln_g) * (1.0 - lam_init)
    return jnp.concatenate([_merge_heads(o_a), _merge_heads(o_b)], axis=-1) @ w_out


def _mixer_cd(h, w_in, w_out, ret_norm_g, ln_g, ln_b, sgu_w, sgu_b):
    sq = H_C * DK_C
    sv = H_C * DV_C
    qc, kc, vc, gc, zd = jnp.split(
        h @ w_in, np.cumsum([sq, sq, sv, sv])[:4].tolist(), axis=-1)
    q = _rotary(_heads(qc, H_C, DK_C).astype(jnp.float32)) * (DK_C ** -0.5)
    k = _rotary(_heads(kc, H_C, DK_C).astype(jnp.float32))
    y = _retention(q, k, _heads(vc, H_C, DV_C).astype(jnp.float32))
    y = _rmsnorm(y, ret_norm_g).astype(h.dtype)
    o_c = jax.nn.silu(gc) * _merge_heads(y)
    u, v = jnp.split(jax.nn.gelu(zd), 2, axis=-1)
    o_d = u * _spatial_gate(_layernorm(v, ln_g, ln_b), sgu_w, sgu_b)
    return jnp.concatenate([o_c, o_d], axis=-1) @ w_out


def _conv_ffn(h, w_up, conv_w, conv_b, w_down):
    s = h.shape[1]
    up = h @ w_up
    padded = jnp.pad(up, ((0, 0), (CONV_WIDTH - 1, 0), (0, 0)))
    c = conv_b + sum(conv_w[j] * padded[:, j:j + s] for j in range(CONV_WIDTH))
    a, g = jnp.split(c, 2, axis=-1)
    return (jax.nn.silu(g) * a) @ w_down


def setup_inputs(seed: int = 0) -> dict:
    key = jax.random.key(seed)
    ks = jax.random.split(key, 20)
    nrm = jax.random.normal
    f32 = jnp.float32
    return {
        "x": nrm(ks[0], (BATCH, SEQ, D_MODEL), f32),
        "norm_mix_g": 1.0 + 0.02 * nrm(ks[1], (DEPTH, D_MODEL), f32),
        "norm_ffn_g": 1.0 + 0.02 * nrm(ks[2], (DEPTH, D_MODEL), f32),
        "final_norm_g": 1.0 + 0.02 * nrm(ks[3], (D_MODEL,), f32),
        "rel_bias": 0.2 * nrm(ks[4], (NUM_BUCKETS, H_B), f32),
        "ab_w_in": nrm(ks[5], (N_EVEN, D_MODEL, AB_IN), f32) * D_MODEL ** -0.5,
        "ab_w_out": nrm(ks[6], (N_EVEN, AB_WIDTH, D_MODEL), f32) * AB_WIDTH ** -0.5,
        "diff_lambda": 0.1 * nrm(ks[7], (N_EVEN, 4, DK_B), f32),
        "diff_subln_g": 1.0 + 0.02 * nrm(ks[8], (N_EVEN, DV_B), f32),
        "cd_w_in": nrm(ks[9], (N_ODD, D_MODEL, CD_IN), f32) * D_MODEL ** -0.5,
        "cd_w_out": nrm(ks[10], (N_ODD, CD_WIDTH, D_MODEL), f32) * CD_WIDTH ** -0.5,
        "ret_norm_g": 1.0 + 0.02 * nrm(ks[11], (N_ODD, DV_C), f32),
        "sgu_ln_g": 1.0 + 0.02 * nrm(ks[12], (N_ODD, D_D), f32),
        "sgu_ln_b": 0.02 * nrm(ks[13], (N_ODD, D_D), f32),
        "sgu_w": nrm(ks[14], (N_ODD, G_D, SGU_LEN, SGU_LEN), f32) * SGU_LEN ** -0.5,
        "sgu_b": 1.0 + 0.02 * nrm(ks[15], (N_ODD, G_D, SGU_LEN), f32),
        "ffn_w_up": nrm(ks[16], (DEPTH, D_MODEL, 2 * D_FF), f32) * D_MODEL ** -0.5,
        "ffn_conv_w": nrm(ks[17], (DEPTH, CONV_WIDTH, 2 * D_FF), f32) * CONV_WIDTH ** -0.5,
        "ffn_conv_b": 0.02 * nrm(ks[18], (DEPTH, 2 * D_FF), f32),
        "ffn_w_down": nrm(ks[19], (DEPTH, D_FF, D_MODEL), f32) * D_FF ** -0.5,
    }


def reference(x, norm_mix_g, norm_ffn_g, final_norm_g, rel_bias, ab_w_in, ab_w_out,
              diff_lambda, diff_subln_g, cd_w_in, cd_w_out, ret_norm_g, sgu_ln_g,
              sgu_ln_b, sgu_w, sgu_b, ffn_w_up, ffn_conv_w, ffn_conv_b, ffn_w_down):
    for layer in range(DEPTH):
        h = _rmsnorm(x, norm_mix_g[layer])
        j = layer // 2
        if layer % 2 == 0:
            lam_init = 0.8 - 0.6 * math.exp(-0.3 * layer)
            x = x + _mixer_ab(h, ab_w_in[j], ab_w_out[j], rel_bias, diff_lambda[j],
                              diff_subln_g[j], lam_init)
        else:
            x = x + _mixer_cd(h, cd_w_in[j], cd_w_out[j], ret_norm_g[j], sgu_ln_g[j],
                              sgu_ln_b[j], sgu_w[j], sgu_b[j])
        h = _rmsnorm(x, norm_ffn_g[layer])
        x = x + _conv_ffn(h, ffn_w_up[layer], ffn_conv_w[layer], ffn_conv_b[layer],
                          ffn_w_down[layer])
    return _rmsnorm(x, final_norm_g)
```

```python
import math
from contextlib import ExitStack
import numpy as np
import concourse.bass as bass
import concourse.mybir as mybir
from concourse.bass_utils import run_bass_kernel_spmd

F32 = mybir.dt.float32
BF16 = mybir.dt.bfloat16
AF = mybir.ActivationFunctionType
ALU = mybir.AluOpType

D = 2048
SEQ = 4096
NB = 2
TOK = 1024
TH = TOK + 2
KC = D // 128
DFF = 5632
NCH = DFF // 128
EPS = 1e-6
RG = [[0, 1, 2, 3], [4, 5, 6, 7]]
TT3 = [(0, 342), (342, 684), (684, 1026)]
STOP_ORDER = ['n0', 'projA', 'attA', 'attB', 'ag1', 'out0', 'norm0', 'ffn0', 'ag2', 'sgu', 'ret', 'ag3', 'out1', 'ffn1', 'final']


class Sched:
    ENGS = ('pe', 'act', 'dve', 'pool', 'sp')

    def __init__(self, nc, ndsem=8):
        self.nc = nc
        self.ndsem = ndsem
        self.streams = {e: [] for e in self.ENGS}
        self.cnt = {}
        self.seen = {e: {} for e in self.ENGS}
        self.lastw = {}
        self.readers = {}
        self.dma_idx = {e: 0 for e in self.ENGS}
        self.semkeys = []
        self.sems = {}

    def _semkey(self, k):
        if k not in self.cnt:
            self.cnt[k] = 0
            self.semkeys.append(k)
        return k

    def add(self, eng, fn, reads=(), writes=(), dma=False):
        deps = {}

        def need(d):
            for semk, val in d.items():
                if deps.get(semk, 0) < val:
                    deps[semk] = val
        for b in reads:
            need(self.lastw.get(b, {}))
        for b in writes:
            need(self.lastw.get(b, {}))
            need(self.readers.get(b, {}))
        if dma == 'cc':
            semk = self._semkey('CC')
            val = self.cnt[semk] + 1
            inc = 1
        elif dma:
            k = self.dma_idx[eng]
            self.dma_idx[eng] += 1
            semk = self._semkey('D_%s_%d' % (eng, k % self.ndsem))
            val = 16 * (k // self.ndsem + 1)
            if val > 16:
                need({semk: val - 16})
            inc = 16
        else:
            semk = self._semkey('E_' + eng)
            val = self.cnt[semk] + 1
            inc = 1
        self.cnt[semk] = val
        waits = []
        for sk, v in deps.items():
            if self.seen[eng].get(sk, 0) >= v:
                continue
            if sk == 'E_pe' and eng == 'pe':
                continue
            waits.append((sk, v))
            self.seen[eng][sk] = v
        self.streams[eng].append((waits, fn, semk, inc))
        for b in reads:
            r = self.readers.setdefault(b, {})
            if r.get(semk, 0) < val:
                r[semk] = val
        for b in writes:
            self.lastw[b] = {semk: val}
            self.readers[b] = {}
        return (semk, val)

    def barrier(self):
        for eng in self.ENGS:
            waits = []
            for sk, v in self.cnt.items():
                if v == 0 or self.seen[eng].get(sk, 0) >= v:
                    continue
                if sk == 'E_' + eng and eng in ('pe', 'sp'):
                    continue
                if sk == 'CC':
                    continue
                waits.append((sk, v))
                self.seen[eng][sk] = v
            if waits:
                self.streams[eng].append((waits, None, None, 0))

    def emit(self):
        nc = self.nc
        with ExitStack() as es:
            for k in self.semkeys:
                self.sems[k] = es.enter_context(nc.semaphore(k))
            with nc.Block() as block:
                def runner(name):
                    def run(e):
                        for waits, fn, semk, inc in self.streams[name]:
                            for sk, v in waits:
                                e.wait_ge(self.sems[sk], v)
                            if fn is not None:
                                ins = fn(e)
                                ins.then_inc(self.sems[semk], inc)
                    return run
                block.tensor(runner('pe'))
                block.scalar(runner('act'))
                block.vector(runner('dve'))
                block.gpsimd(runner('pool'))
                block.sync(runner('sp'))


def _rel_bucket_np(rel):
    nb = 16
    max_exact = 8
    ret = np.where(rel > 0, nb, 0)
    n = np.abs(rel)
    n_f = np.maximum(n, 1).astype(np.float32)
    large = max_exact + (np.log(n_f / np.float32(max_exact)) / np.float32(math.log(128 / max_exact))
                         * np.float32(nb - max_exact)).astype(np.int32)
    large = np.minimum(large, nb - 1)
    return ret + np.where(n < max_exact, n, large)


def _consts():
    c = {}
    j = np.arange(128)[:, None]
    s = np.arange(128)[None, :]
    tri = np.zeros((128, 3, 128), np.float32)
    tri[:, 0, :] = 1.0
    tri[:, 1, :] = (j > s)
    tri[:, 2, :] = (j <= s)
    c['tri'] = tri
    srow = np.arange(128)[:, None]
    t = np.arange(512)[None, :]
    mk = np.zeros((128, 8, 512), np.float32)
    for oi, o in enumerate((0, 128, 256, 384)):
        mk[:, oi, :] = ((o + srow) < t)
        mk[:, 4 + oi, :] = (((o + srow) // 64) <= (t // 64))
    c['masks'] = mk
    return c


def _bias_index():
    s = np.arange(128)[:, None]
    u = np.arange(1024)[None, :]
    return _rel_bucket_np((s - u + 384).astype(np.int32))


def _rot_tables(gamma):
    d = 128
    inv_freq = (10000.0 ** (-np.arange(0, d, 2, dtype=np.float32) / d)).astype(np.float32)
    ang = np.arange(SEQ, dtype=np.float32)[:, None] * inv_freq[None, :]
    cos = np.concatenate([np.cos(ang), np.cos(ang)], -1).astype(np.float32)
    sin = np.concatenate([-np.sin(ang), np.sin(ang)], -1).astype(np.float32)
    return cos, sin


def build(stop='final', dbg=()):
    nc = bass.Bass("TRN2", target_bir_lowering=False)
    S = Sched(nc)
    stop_i = STOP_ORDER.index(stop)

    def upto(name):
        return STOP_ORDER.index(name) <= stop_i

    used_inputs = []

    def din(name, shape, dt=F32, need='n0'):
        if not upto(need):
            return None
        used_inputs.append(name)
        return nc.dram_tensor(name, list(shape), dt, kind="ExternalInput").ap()

    def dout(name, shape, dt=F32):
        return nc.dram_tensor(name, list(shape), dt, kind="ExternalOutput").ap()

    def dint(name, shape, dt):
        return nc.dram_tensor(name, list(shape), dt).ap()

    xT_seq = din("xT_seq", [D, SEQ])
    xT_own = din("xT_own", [D, TH], need="out0")
    gains = din("gains", [128, 5, KC])
    tri_d = din("tri", [128, 3, 128])
    masks_d = din("masks", [128, 8, 512])
    w_inA = din("w_inA", [D, 768])
    w_inB = din("w_inB", [D, 768], need="attB")
    biasM_d = din("biasM", [128, 1024], need="attB")
    smallB_d = din("smallB", [128, 8])
    lamv_d = din("lamv", [128, 512], need="attB")
    w_out0 = din("w_out0", [D, D], need="out0")
    w_up = [din("w_up%d" % l, [D, NCH, 256], need="ffn%d" % l) for l in range(2)]
    convp = [din("convp%d" % l, [128, NCH, 8], need="ffn%d" % l) for l in range(2)]
    w_down = [din("w_down%d" % l, [DFF, D], need="ffn%d" % l) for l in range(2)]
    w_zd = din("w_zd", [D, 2048], need="sgu")
    sguw_d = din("sguw", [128, 4, 128], need="sgu")
    sgum_d = din("sgum", [128, 128], need="sgu")
    lngb_d = din("lngb", [128, 2, 1024], need="sgu")
    sgub_d = din("sgub", [128, 4, 512], need="sgu")
    w_inC = din("w_inC", [D, 1024], need="ret")
    retc_d = din("retc", [128, 264], need="ret")
    cosT_d = din("cosT", [128, SEQ], need="ret")
    sinT_d = din("sinT", [128, SEQ], need="ret")
    cstm_d = din("cstm", [128, 32, 256], need="ret")
    w_out1 = din("w_out1", [D, D], need="out1")
    yT = dout("yT", [D, TOK])
    dbg_out = {}
    for name, shape, dt in dbg:
        dbg_out[name] = dout("dbg_" + name, shape, dt)

    hseq = dint("hseq", [D, SEQ], BF16)
    cin1s = [dint("cin1s_%d" % t, [512, TOK], BF16) for t in range(4)]
    cout1a = dint("cout1a", [4 * 2048, TOK], BF16)
    cin2 = [dint("cin2_%d" % g, [1024, 512], BF16) for g in range(4)]
    cout2 = [dint("cout2_%d" % g, [4096, 512], BF16) for g in range(4)]
    cin3s = [dint("cin3s_%d" % t, [256, TOK], BF16) for t in range(4)]
    cout3a = dint("cout3a", [4 * 1024, TOK], BF16)
    cinH = dint("cinH", [128, 48], F32)
    coutH = dint("coutH", [512, 48], F32)
    x1s = dint("x1s", [D, TOK], F32)

    pid = nc.partition_id()
    rank = pid % 4

    def dma(q, out, in_, reads, writes):
        return S.add(q, lambda e: e.dma_start(out=out, in_=in_), reads, writes, dma=True)

    def act(out, in_, func, reads, writes, **kw):
        return S.add('act', lambda e: e.activation(out=out, in_=in_, func=func, **kw), reads, writes)

    def tt(out, in0, in1, op, reads, writes, eng='dve'):
        return S.add(eng, lambda e: e.tensor_tensor(out=out, in0=in0, in1=in1, op=op), reads, writes)

    def ts(out, in0, s1, s2, op0, op1, reads, writes, eng='dve'):
        if op1 is None:
            return S.add(eng, lambda e: e.tensor_scalar(out=out, in0=in0, scalar1=s1, scalar2=None, op0=op0), reads, writes)
        return S.add(eng, lambda e: e.tensor_scalar(out=out, in0=in0, scalar1=s1, scalar2=s2, op0=op0, op1=op1), reads, writes)

    def stt(out, in0, scalar, in1, op0, op1, reads, writes):
        return S.add('dve', lambda e: e.scalar_tensor_tensor(out=out, in0=in0, scalar=scalar, in1=in1, op0=op0, op1=op1), reads, writes)

    def cp(out, in_, reads, writes, eng='dve'):
        if eng == 'act':
            return act(out, in_, AF.Copy, reads, writes)
        return S.add(eng, lambda e: e.tensor_copy(out=out, in_=in_), reads, writes)

    def mms(lst, reads, writes):
        def fn(e):
            r = None
            for (o, l, rh, st, sp) in lst:
                r = e.matmul(o, lhsT=l, rhs=rh, start=st, stop=sp)
            return r
        return S.add('pe', fn, reads, writes)

    with ExitStack() as glob:
        uniq = [0]

        def sb(es, name, shape, dt):
            uniq[0] += 1
            return es.enter_context(nc.sbuf_tensor("s%d_%s" % (uniq[0], name), list(shape), dt))

        ps = glob.enter_context(nc.psum_tensor("ps", [128, 8, 512], F32))
        tri = sb(glob, "tri", [128, 3, 128], BF16)
        masks = sb(glob, "masks", [128, 8, 512], BF16)
        gn = sb(glob, "gn", [128, 5, KC], F32)
        smallB = sb(glob, "smallB", [128, 8], F32)
        with ExitStack() as es:
            trif = sb(es, "trif", [128, 3, 128], F32)
            mkf = sb(es, "mkf", [128, 8, 512], F32)
            dma('sp', trif[:], tri_d, [], ['trif'])
            dma('sp', mkf[:], masks_d, [], ['mkf'])
            dma('sp', gn[:], gains, [], ['gn'])
            dma('sp', smallB[:], smallB_d, [], ['smallB'])
            cp(tri[:], trif[:], ['trif'], ['tri'])
            cp(masks[:], mkf[:], ['mkf'], ['masks'])
            S.barrier()
        ones = tri[:, 0, :]
        Umat = tri[:, 1, :]
        Lmat = tri[:, 2, :]

        def rstd_from_ps(bank_ap, out_ap, n, reads, writes):
            act(out_ap, bank_ap, AF.Ln, reads, writes, scale=1.0 / n, bias=eps_t[:, 0:1])
            act(out_ap, out_ap, AF.Exp, writes, writes, scale=-0.5)

        eps_t = sb(glob, "eps_t", [128, 1], F32)
        S.add('dve', lambda e: e.memset(eps_t[:], EPS), [], ['eps'])
        S.barrier()

        xsv = xT_seq.rearrange("(kc p) t -> p kc t", p=128)
        hsv = hseq.rearrange("(kc p) t -> p kc t", p=128)
        wst = ExitStack()
        wA_t = sb(wst, "wA_t", [128, KC, 768], BF16)
        wB_t = sb(wst, "wB_t", [128, KC, 768], BF16)
        wvA = w_inA.rearrange("(kc p) n -> p kc n", p=128)
        for i in range(3):
            dma('pool', wA_t[:, :, 256 * i:256 * i + 256], wvA[:, :, 256 * i:256 * i + 256], [], ['wA%d' % i])
        mix = ExitStack()
        oT_all = sb(mix, "oT_all", [128, 4, SEQ], BF16)
        if upto('attB'):
            wvB = w_inB.rearrange("(kc p) n -> p kc n", p=128)
            for i in range(3):
                dma('pool', wB_t[:, :, 256 * i:256 * i + 256], wvB[:, :, 256 * i:256 * i + 256], ['hs16_15'], ['wB%d' % i])

        def in_proj(es, wA, QK, V, tagp):
            hbb = [sb(es, "hin%s%d" % (tagp, i), [128, KC, 512], BF16) for i in range(2)]
            wk = ['w%s%d' % (tagp, i) for i in range(3)]
            ev = 0
            for t8 in range(8):
                p = t8 % 2
                sl = slice(512 * t8, 512 * t8 + 512)
                dma('sp', hbb[p][:], hsv[:, :, sl], ['hs16_%d' % (2 * t8), 'hs16_%d' % (2 * t8 + 1)], ['hin%d' % p])
                for g in range(4):
                    bk = ev % 4
                    mms([(ps[:, bk, :], wA[:, kc, 128 * g:128 * g + 128], hbb[p][:, kc, :], kc == 0, kc == KC - 1) for kc in range(KC)],
                        ['hin%d' % p] + wk, ['ps%d' % bk])
                    cp(QK[:, g, sl], ps[:, bk, :], ['ps%d' % bk], ['QK%d_%d' % (g, t8)], eng=('act' if ev % 2 else 'dve'))
                    ev += 1
                for sub in range(4):
                    bk = ev % 4
                    mms([(ps[:, bk, 0:256], hbb[p][:, kc, 128 * sub:128 * sub + 128], wA[:, kc, 512:768], kc == 0, kc == KC - 1) for kc in range(KC)],
                        ['hin%d' % p] + wk, ['ps%d' % bk])
                    cp(V[:, 4 * t8 + sub, :], ps[:, bk, 0:256], ['ps%d' % bk], ['V%d' % (4 * t8 + sub)], eng=('act' if ev % 2 else 'dve'))
                    ev += 1

        scl = 128.0 ** -0.5

        def ag_slab(t, cin, cout_all, src, ng, tagp):
            rows = ng * 128 * 4
            dma('sp', cin[t].rearrange("(g p) t -> p g t", p=128), src[:, :, TOK * t:TOK * t + TOK],
                ['oT%d_%d' % (g, qt) for g in range(ng) for qt in (2 * t, 2 * t + 1)], ['%sin%d' % (tagp, t)])
            S.add('pool', lambda e: e.collective_compute("AllGather", ALU.bypass, replica_groups=RG, ins=[cin[t].opt()],
                                                         outs=[cout_all[rows * t:rows * t + rows, :].opt()]),
                  ['%sin%d' % (tagp, t)], ['%sout%d' % (tagp, t)], dma='cc')

        if upto('projA'):
            with ExitStack() as es:
                QK = sb(es, "QKa", [128, 4, SEQ], BF16)
                V = sb(es, "Va", [128, 32, 256], BF16)
                with ExitStack() as es2:
                    xs = [sb(es2, "xs%d" % i, [128, KC, 256], F32) for i in range(2)]
                    sq = sb(es2, "sq", [128, KC, 256], BF16)
                    hb = [sb(es2, "hb%d" % i, [128, KC, 256], BF16) for i in range(2)]
                    rs = [sb(es2, "rs%d" % i, [128, 256], F32) for i in range(2)]
                    wkA = ['wA%d' % i for i in range(3)]
                    dma('sp', xs[0][:], xsv[:, :, 0:256], [], ['xs0'])
                    ev = 0
                    for t16 in range(16):
                        p = t16 % 2
                        sl = slice(256 * t16, 256 * t16 + 256)
                        if t16 + 1 < 16:
                            dma('sp', xs[1 - p][:], xsv[:, :, 256 * (t16 + 1):256 * (t16 + 1) + 256], [], ['xs%d' % (1 - p)])
                        act(sq[:], xs[p][:], AF.Square, ['xs%d' % p], ['sq'])
                        mms([(ps[:, p, 0:256], ones, sq[:, kc, :], kc == 0, kc == KC - 1) for kc in range(KC)], ['sq', 'tri'], ['ps%d' % p])
                        rstd_from_ps(ps[:, p, 0:256], rs[p][:], D, ['ps%d' % p], ['rs%d' % p])
                        for kc in range(KC):
                            stt(hb[p][:, kc, :], xs[p][:, kc, :], gn[:, 0, kc:kc + 1], rs[p][:], ALU.mult, ALU.mult,
                                ['xs%d' % p, 'rs%d' % p, 'gn'], ['hb%d_%d' % (p, kc)])
                        hbk = ['hb%d_%d' % (p, kc) for kc in range(KC)]
                        dma('sp', hsv[:, :, sl], hb[p][:], hbk, ['hs16_%d' % t16])
                        for g in range(4):
                            bk = 2 + ev % 6
                            mms([(ps[:, bk, 0:256], wA_t[:, kc, 128 * g:128 * g + 128], hb[p][:, kc, :], kc == 0, kc == KC - 1) for kc in range(KC)],
                                hbk + wkA, ['ps%d' % bk])
                            cp(QK[:, g, sl], ps[:, bk, 0:256], ['ps%d' % bk], ['QK%d_%d' % (g, t16)], eng=('act' if ev % 2 else 'dve'))
                            ev += 1
                        for sub in range(2):
                            bk = 2 + ev % 6
                            mms([(ps[:, bk, 0:256], hb[p][:, kc, 128 * sub:128 * sub + 128], wA_t[:, kc, 512:768], kc == 0, kc == KC - 1) for kc in range(KC)],
                                hbk + wkA, ['ps%d' % bk])
                            cp(V[:, 2 * t16 + sub, :], ps[:, bk, 0:256], ['ps%d' % bk], ['V%d' % (2 * t16 + sub)], eng=('act' if ev % 2 else 'dve'))
                            ev += 1
                    S.barrier()
                if 'QKa' in dbg_out:
                    dma('sp', dbg_out['QKa'], QK[:], [], ['dbg1'])
                    dma('sp', dbg_out['Va'], V[:], [], ['dbg2'])
                if upto('attA'):
                    with ExitStack() as es2:
                        Eb = [[sb(es2, "Eb%d%d" % (h, q), [128, 512], F32) for q in range(2)] for h in range(2)]
                        SPf = [[sb(es2, "SPf%d%d" % (h, q), [128, 512], F32) for q in range(2)] for h in range(2)]
                        SPb = [[sb(es2, "SPb%d%d" % (h, q), [128, 512], BF16) for q in range(2)] for h in range(2)]
                        T1 = [[sb(es2, "T1%d%d" % (h, q), [128, 512], F32) for q in range(2)] for h in range(2)]
                        Wb = [[sb(es2, "Wb%d%d" % (h, q), [128, 512], BF16) for q in range(2)] for h in range(2)]
                        blocks = [(qt, bi, i) for qt in range(8) for bi, i in enumerate(range(4 * qt + 3, -1, -1))]

                        def A1(n):
                            qt, bi, i = blocks[n]
                            q = n % 2
                            qsl = slice(512 * qt, 512 * qt + 512)
                            ksl = slice(128 * i, 128 * i + 128)
                            for h in range(2):
                                zb = 2 * q + h
                                mms([(ps[:, zb, :], QK[:, 2 * h + 1, ksl], QK[:, 2 * h, qsl], True, True)], [], ['ps%d' % zb])
                            for h in range(2):
                                zb = 2 * q + h
                                kq = '%d%d' % (h, q)
                                act(Eb[h][q][:], ps[:, zb, :], AF.Exp, ['ps%d' % zb], ['Eb' + kq], scale=scl)
                                act(SPf[h][q][:], Eb[h][q][:], AF.Ln, ['Eb' + kq], ['SPf' + kq], bias=1.0)

                        def A2(n):
                            qt, bi, i = blocks[n]
                            q = n % 2
                            o = 128 * i - 512 * qt
                            diag = o >= 0
                            first = bi == 0
                            for h in range(2):
                                kq = '%d%d' % (h, q)
                                if diag:
                                    tt(SPb[h][q][:], SPf[h][q][:], masks[:, o // 128, :], ALU.mult, ['SPf' + kq, 'masks'], ['SPb' + kq])
                                else:
                                    cp(SPb[h][q][:], SPf[h][q][:], ['SPf' + kq], ['SPb' + kq])
                            for h in range(2):
                                kq = '%d%d' % (h, q)
                                mms([(ps[:, 4 + h, :], Umat, SPb[h][q][:], first, True)], ['SPb' + kq, 'tri'], ['ps%d' % (4 + h)])
                            for h in range(2):
                                zb = 2 * q + h
                                kq = '%d%d' % (h, q)
                                stt(T1[h][q][:], ps[:, zb, :], scl, SPf[h][q][:], ALU.mult, ALU.subtract, ['ps%d' % zb, 'SPf' + kq], ['T1' + kq])
                            for h in range(2):
                                kq = '%d%d' % (h, q)
                                tt(T1[h][q][:], T1[h][q][:], ps[:, 4 + h, :], ALU.subtract, ['T1' + kq, 'ps%d' % (4 + h)], ['T1' + kq])
                                mms([(ps[:, 4 + h, :], Lmat, SPb[h][q][:], False, True)], ['SPb' + kq, 'tri'], ['ps%d' % (4 + h)])

                        def A3(n):
                            qt, bi, i = blocks[n]
                            q = n % 2
                            qsl = slice(512 * qt, 512 * qt + 512)
                            o = 128 * i - 512 * qt
                            diag = o >= 0
                            first = bi == 0
                            last = i == 0
                            for h in range(2):
                                kq = '%d%d' % (h, q)
                                act(Wb[h][q][:], T1[h][q][:], AF.Exp, ['T1' + kq], ['Wb' + kq])
                                if diag:
                                    tt(Wb[h][q][:], Wb[h][q][:], masks[:, o // 128, :], ALU.mult, ['Wb' + kq, 'masks'], ['Wb' + kq])
                            for h in range(2):
                                kq = '%d%d' % (h, q)
                                mms([(ps[:, 6 + h, :], V[:, i, 128 * h:128 * h + 128], Wb[h][q][:], first, last)], ['Wb' + kq], ['ps%d' % (6 + h)])
                            if last:
                                for h in range(2):
                                    cp(oT_all[:, h, qsl], ps[:, 6 + h, :], ['ps%d' % (6 + h)], ['oT%d_%d' % (h, qt)], eng='act')

                        NBk = len(blocks)
                        for n in range(NBk + 2):
                            if n < NBk:
                                A1(n)
                            if 0 <= n - 1 < NBk:
                                A2(n - 1)
                            if 0 <= n - 2 < NBk:
                                A3(n - 2)
                        S.barrier()
        if 'oT' in dbg_out and not upto('attB'):
            dma('sp', dbg_out['oT'], oT_all[:], ['oT%d_%d' % (h, qt) for h in range(2) for qt in range(8)], ['dbg3'])

        if upto('attB'):
            with ExitStack() as es:
                QK = sb(es, "QKb", [128, 4, SEQ], BF16)
                V = sb(es, "Vb", [128, 32, 256], BF16)
                with ExitStack() as es2:
                    in_proj(es2, wB_t, QK, V, "B")
                    S.barrier()
                with ExitStack() as es2:
                    biasM = sb(es2, "biasM", [128, 1024], F32)
                    lamv = sb(es2, "lamv", [128, 512], F32)
                    lt = sb(es2, "lt", [128, 8], F32)
                    dma('sp', biasM[:], biasM_d, [], ['biasM'])
                    dma('sp', lamv[:], lamv_d, [], ['lamv'])
                    lam_init = 0.8 - 0.6 * math.exp(-0.3 * 0)
                    lp = sb(es2, "lp", [128, 256], F32)
                    tt(lp[:, 0:128], lamv[:, 0:128], lamv[:, 128:256], ALU.mult, ['lamv'], ['lp0'])
                    tt(lp[:, 128:256], lamv[:, 256:384], lamv[:, 384:512], ALU.mult, ['lamv'], ['lp1'])
                    S.add('dve', lambda e: e.reduce_sum(out=lt[:, 0:1], in_=lp[:, 0:128], axis=mybir.AxisListType.X), ['lp0'], ['lt0'])
                    S.add('dve', lambda e: e.reduce_sum(out=lt[:, 1:2], in_=lp[:, 128:256], axis=mybir.AxisListType.X), ['lp1'], ['lt1'])
                    act(lt[:, 2:4], lt[:, 0:2], AF.Exp, ['lt0', 'lt1'], ['lt23'])
                    tt(lt[:, 4:5], lt[:, 3:4], lt[:, 2:3], ALU.subtract, ['lt23'], ['lt4'])
                    ts(lt[:, 4:5], lt[:, 4:5], -lam_init, None, ALU.add, None, ['lt4'], ['lt4'])
                    neglam = lt[:, 4:5]
                    ts(lt[:, 5:7], smallB[:, 1:3], 1.0 - lam_init, None, ALU.mult, None, ['smallB'], ['lt56'])
                    gsub = lt[:, 5:7]
                    bconst = smallB[:, 0:1]
                    Tn = [sb(es2, "Tn%d%d" % (c, q), [128, 512], F32) for c in range(2) for q in range(2)]
                    Ebf = [sb(es2, "Ebf%d%d" % (c, q), [128, 512], BF16) for c in range(2) for q in range(2)]
                    Esum = [sb(es2, "Esum%d" % c, [128, 512], F32) for c in range(2)]
                    Esb = [sb(es2, "Esb%d" % c, [128, 512], BF16) for c in range(2)]
                    rD = [sb(es2, "rD%d" % c, [128, 512], F32) for c in range(2)]
                    oh = [sb(es2, "oh%d" % c, [128, 512], F32) for c in range(2)]
                    tA = sb(es2, "tA", [128, 512], F32)
                    sqh = [sb(es2, "sqh%d" % c, [128, 512], BF16) for c in range(2)]
                    rsb = sb(es2, "rsb", [128, 512], F32)
                    blocksB = []
                    for qt in range(8):
                        allb = list(range(4 * qt + 3, -1, -1))
                        nearb = [i for i in allb if 128 * i - 512 * qt >= -128]
                        farb = [i for i in allb if 128 * i - 512 * qt < -128]
                        order = []
                        while nearb or farb:
                            if farb:
                                order.append(farb.pop(0))
                            if nearb:
                                order.append(nearb.pop(0))
                        for pos, i in enumerate(order):
                            blocksB.append((qt, pos, i, len(order)))

                    def B1(n):
                        qt, bi, i, nbq = blocksB[n]
                        q = n % 2
                        qsl = slice(512 * qt, 512 * qt + 512)
                        o = 128 * i - 512 * qt
                        near = o >= -128
                        diag = o >= 0
                        ksl = slice(128 * i, 128 * i + 128)
                        for c in range(2):
                            zb = 2 * q + c
                            mms([(ps[:, zb, :], QK[:, 2 + c, ksl], QK[:, c, qsl], True, True)], [], ['ps%d' % zb])
                        for c in range(2):
                            zb = 2 * q + c
                            kq = '%d%d' % (c, q)
                            E = Ebf[2 * c + q]
                            if near:
                                T = Tn[2 * c + q]
                                u0 = 384 - o
                                stt(T[:], ps[:, zb, :], scl, biasM[:, u0:u0 + 512], ALU.mult, ALU.add, ['ps%d' % zb, 'biasM'], ['Tn' + kq])
                                act(E[:], T[:], AF.Exp, ['Tn' + kq], ['Ebf' + kq])
                                if diag:
                                    tt(E[:], E[:], masks[:, 4 + o // 128, :], ALU.mult, ['Ebf' + kq, 'masks'], ['Ebf' + kq])
                            else:
                                act(E[:], ps[:, zb, :], AF.Exp, ['ps%d' % zb, 'smallB'], ['Ebf' + kq], scale=scl, bias=bconst)

                    def B2(n):
                        qt, bi, i, nbq = blocksB[n]
                        q = n % 2
                        qsl = slice(512 * qt, 512 * qt + 512)
                        first = bi == 0
                        last = bi == nbq - 1
                        for c in range(2):
                            kq = '%d%d' % (c, q)
                            E = Ebf[2 * c + q]
                            mms([(ps[:, 4 + 2 * c + hf, :], V[:, i, 128 * hf:128 * hf + 128], E[:], first, last) for hf in range(2)],
                                ['Ebf' + kq], ['ps%d' % (4 + 2 * c), 'ps%d' % (5 + 2 * c)])
                            eg = 'dve'
                            if first:
                                cp(Esum[c][:], E[:], ['Ebf' + kq], ['Esum%d' % c], eng=eg)
                            else:
                                tt(Esum[c][:], Esum[c][:], E[:], ALU.add, ['Ebf' + kq, 'Esum%d' % c], ['Esum%d' % c], eng=eg)
                        if last:
                            for c in range(2):
                                cp(Esb[c][:], Esum[c][:], ['Esum%d' % c], ['Esb%d' % c])
                                mms([(ps[:, c, :], ones, Esb[c][:], True, True)], ['Esb%d' % c, 'tri'], ['ps%d' % c])
                                S.add('dve', lambda e, c=c: e.reciprocal(out=rD[c][:], in_=ps[:, c, :]), ['ps%d' % c], ['rD%d' % c])
                            for hf in range(2):
                                tt(tA[:], ps[:, 4 + hf, :], rD[0][:], ALU.mult, ['ps%d' % (4 + hf), 'rD0'], ['tA'])
                                tt(oh[hf][:], ps[:, 6 + hf, :], rD[1][:], ALU.mult, ['ps%d' % (6 + hf), 'rD1'], ['oh%d' % hf])
                                stt(oh[hf][:], oh[hf][:], neglam, tA[:], ALU.mult, ALU.add, ['oh%d' % hf, 'tA', 'lt4'], ['oh%d' % hf])
                                act(sqh[hf][:], oh[hf][:], AF.Square, ['oh%d' % hf], ['sqh%d' % hf])
                            mms([(ps[:, 2, :], ones, sqh[hf][:], hf == 0, hf == 1) for hf in range(2)], ['sqh0', 'sqh1', 'tri'], ['ps2'])
                            rstd_from_ps(ps[:, 2, :], rsb[:], 256, ['ps2'], ['rsb'])
                            for hf in range(2):
                                stt(oT_all[:, 2 + hf, qsl], oh[hf][:], gsub[:, hf:hf + 1], rsb[:], ALU.mult, ALU.mult,
                                    ['oh%d' % hf, 'rsb', 'lt56'], ['oT%d_%d' % (2 + hf, qt)])
                            if qt % 2 == 1 and upto('ag1'):
                                ag_slab(qt // 2, cin1s, cout1a, oT_all, 4, 'c1')

                    for n in range(len(blocksB) + 1):
                        if n < len(blocksB):
                            B1(n)
                        if n > 0:
                            B2(n - 1)
                    S.barrier()
        if 'oT' in dbg_out and upto('attB'):
            dma('sp', dbg_out['oT'], oT_all[:], ['oT%d_%d' % (h, qt) for h in range(4) for qt in range(8)], ['dbg3'])

        mix.close()
        wst.close()
        S.barrier()

        hpk = sb(glob, "hpk", [128, 48], F32)
        hh = sb(glob, "hh", [128, 48], F32)
        odT = sb(glob, "odT", [128, 8, TOK], BF16)
        tok = ExitStack()
        xT = sb(tok, "xT", [128, KC, TH], F32)
        aT = sb(tok, "aT", [128, KC, TH], BF16)
        rs1 = sb(tok, "rs1", [128, TH], F32)

        def out_proj(w_d, lname):
            with ExitStack() as es:
                wo = [sb(es, "wo%d" % i, [128, KC, 256], BF16) for i in range(3)]
                wv = w_d.rearrange("(kc p) n -> p kc n", p=128)
                n = 0
                for c2 in range(8):
                    wb = c2 % 3
                    dma('pool', wo[wb][:], wv[:, :, 256 * c2:256 * c2 + 256], [], ['wo%d' % wb])
                    for half in range(2):
                        cc = 2 * c2 + half
                        for (a, b) in TT3:
                            bk = n % 8
                            n += 1
                            mms([(ps[:, bk, 0:b - a], wo[wb][:, kc, 128 * half:128 * half + 128], aT[:, kc, a:b], kc == 0, kc == KC - 1) for kc in range(KC)],
                                ['wo%d' % wb] + ['aT%d' % kc for kc in range(KC)], ['ps%d' % bk])
                            tt(xT[:, cc, a:b], xT[:, cc, a:b], ps[:, bk, 0:b - a], ALU.add, ['ps%d' % bk, 'xT%d' % cc], ['xT%d' % cc])
                S.barrier()

        def norm_to_aT(gi, t0, t1):
            with ExitStack() as es:
                sq = sb(es, "sqn", [128, KC, 342], BF16)
                tiles = [(a, min(a + 342, t1)) for a in range(t0, t1, 342)]
                for ti, (a, b) in enumerate(tiles):
                    bk = ti % 8
                    act(sq[:, :, 0:b - a], xT[:, :, a:b], AF.Square, ['xT%d' % kc for kc in range(KC)], ['sqn'])
                    mms([(ps[:, bk, 0:b - a], ones, sq[:, kc, 0:b - a], kc == 0, kc == KC - 1) for kc in range(KC)], ['sqn', 'tri'], ['ps%d' % bk])
                    rstd_from_ps(ps[:, bk, 0:b - a], rs1[:, a:b], D, ['ps%d' % bk], ['rs1_%d' % ti])
                    for kc in range(KC):
                        stt(aT[:, kc, a:b], xT[:, kc, a:b], gn[:, gi, kc:kc + 1], rs1[:, a:b], ALU.mult, ALU.mult,
                            ['xT%d' % kc, 'rs1_%d' % ti, 'gn'], ['aT%d' % kc])
                S.barrier()

        def conv_ffn(l):
            GK = 4
            with ExitStack() as es:
                wu = [sb(es, "wu%d" % i, [128, KC, 256], BF16) for i in range(3)]
                cpar = sb(es, "cpar", [128, NCH, 8], F32)
                Ur = [sb(es, "Ur%d" % i, [128, TH], F32) for i in range(2)]
                Cc = [sb(es, "Cc%d" % i, [128, TOK], F32) for i in range(2)]
                gat = [sb(es, "gat%d" % i, [128, GK, TOK], BF16) for i in range(2)]
                wd = [sb(es, "wd%d" % i, [128, GK, 256], BF16) for i in range(3)]
                dma('sp', cpar[:], convp[l], [], ['cpar'])
                wdv = w_down[l].rearrange("(c p) n -> p c n", p=128)
                aTk = ['aT%d' % kc for kc in range(KC)]
                nw = 0
                nd = 0
                for kg in range(NCH // GK):
                    gp = kg % 2
                    for ci in range(GK):
                        c = kg * GK + ci
                        wb = nw % 3
                        nw += 1
                        dma('pool', wu[wb][:], w_up[l][:, c, :].rearrange("(kc p) n -> p kc n", p=128), [], ['wu%d' % wb])
                        for ag in range(2):
                            b0 = 3 * ag
                            for ti, (a, b) in enumerate(TT3):
                                mms([(ps[:, b0 + ti, 0:b - a], wu[wb][:, kc, 128 * ag:128 * ag + 128], aT[:, kc, a:b], kc == 0, kc == KC - 1) for kc in range(KC)],
                                    ['wu%d' % wb] + aTk, ['ps%d' % (b0 + ti)])
                            pk = ['ps%d' % (b0 + ti) for ti in range(3)]
                            act(Ur[ag][:].rearrange("p (a b) -> p a b", a=3), ps[:, b0:b0 + 3, 0:342], AF.Copy, pk, ['Ur%d' % ag])
                            pb = 4 * ag
                            ts(Cc[ag][:], Ur[ag][:, 2:TH], cpar[:, c, pb + 2:pb + 3], cpar[:, c, pb + 3:pb + 4], ALU.mult, ALU.add,
                               ['Ur%d' % ag, 'cpar'], ['Cc%d' % ag])
                            stt(Cc[ag][:], Ur[ag][:, 1:TH - 1], cpar[:, c, pb + 1:pb + 2], Cc[ag][:], ALU.mult, ALU.add, ['Ur%d' % ag, 'Cc%d' % ag, 'cpar'], ['Cc%d' % ag])
                            stt(Cc[ag][:], Ur[ag][:, 0:TH - 2], cpar[:, c, pb + 0:pb + 1], Cc[ag][:], ALU.mult, ALU.add, ['Ur%d' % ag, 'Cc%d' % ag, 'cpar'], ['Cc%d' % ag])
                        act(Cc[1][:], Cc[1][:], AF.Silu, ['Cc1'], ['Cc1'])
                        tt(gat[gp][:, ci, :], Cc[0][:], Cc[1][:], ALU.mult, ['Cc0', 'Cc1'], ['gat%d_%d' % (gp, ci)])
                    gk = ['gat%d_%d' % (gp, ci) for ci in range(GK)]
                    for c2 in range(8):
                        wb = nd % 3
                        nd += 1
                        dma('pool', wd[wb][:], wdv[:, kg * GK:kg * GK + GK, 256 * c2:256 * c2 + 256], [], ['wd%d' % wb])
                        for half in range(2):
                            cc = 2 * c2 + half
                            for th in range(2):
                                bk = 6 + th
                                mms([(ps[:, bk, :], wd[wb][:, ci, 128 * half:128 * half + 128], gat[gp][:, ci, 512 * th:512 * th + 512], ci == 0, ci == GK - 1) for ci in range(GK)],
                                    ['wd%d' % wb] + gk, ['ps%d' % bk])
                                sl = slice(2 + 512 * th, 2 + 512 * th + 512)
                                tt(xT[:, cc, sl], xT[:, cc, sl], ps[:, bk, :], ALU.add, ['ps%d' % bk, 'xT%d' % cc], ['xT%d' % cc])
                S.barrier()

        xTk = ['xT%d' % kc for kc in range(KC)]
        aTk = ['aT%d' % kc for kc in range(KC)]
        if upto('out0'):
            dma('sp', xT[:], xT_own.rearrange("(kc p) t -> p kc t", p=128), [], xTk)
            cv = cout1a.rearrange("(sk p) t -> p sk t", p=128)
            c1k = ['c1out%d' % t for t in range(4)]
            dma('sp', aT[:, :, 2:TH], cv[:, bass.ds(rank * 16, 16), :], c1k, aTk)
            dma('sp', aT[:, :, 0:2], cv[:, bass.ds(((rank + 3) % 4) * 16, 16), TOK - 2:TOK], c1k, ['aTh'])
            ts(aT[:, :, 0:2], aT[:, :, 0:2], smallB[:, 3:4], None, ALU.mult, None, ['aTh', 'smallB'] + aTk, aTk)
            out_proj(w_out0, 'o0')
            if 'xmid0' in dbg_out:
                dma('sp', dbg_out['xmid0'].rearrange("(kc p) t -> p kc t", p=128), xT[:], xTk, ['dbg4'])
        if upto('norm0'):
            norm_to_aT(2, 0, TH)
        if upto('ffn0'):
            conv_ffn(0)
            if 'x1' in dbg_out:
                dma('sp', dbg_out['x1'].rearrange("(kc p) t -> p kc t", p=128), xT[:], xTk, ['dbg5'])


        def final_out(normed):
            with ExitStack() as es:
                yo = sb(es, "yo", [128, KC, TOK], F32)
                if normed:
                    sq = sb(es, "sqf", [128, KC, 512], BF16)
                    for th in range(2):
                        sl = slice(2 + 512 * th, 2 + 512 * th + 512)
                        act(sq[:], xT[:, :, sl], AF.Square, xTk, ['sqf'])
                        mms([(ps[:, th, :], ones, sq[:, kc, :], kc == 0, kc == KC - 1) for kc in range(KC)], ['sqf', 'tri'], ['ps%d' % th])
                        rstd_from_ps(ps[:, th, :], rs1[:, sl], D, ['ps%d' % th], ['rsf%d' % th])
                        for kc in range(KC):
                            stt(yo[:, kc, 512 * th:512 * th + 512], xT[:, kc, sl], gn[:, 4, kc:kc + 1], rs1[:, sl], ALU.mult, ALU.mult,
                                ['xT%d' % kc, 'rsf%d' % th, 'gn'], ['yo%d_%d' % (kc, th)])
                    yk = ['yo%d_%d' % (kc, th) for kc in range(KC) for th in range(2)]
                else:
                    for kc in range(KC):
                        cp(yo[:, kc, :], xT[:, kc, 2:TH], ['xT%d' % kc], ['yo%d' % kc], eng=('act' if kc % 2 else 'dve'))
                    yk = ['yo%d' % kc for kc in range(KC)]
                dma('sp', yT.rearrange("(kc p) t -> p kc t", p=128), yo[:], yk, ['yT'])
                S.barrier()

        if not upto('ag2'):
            final_out(False)
            tok.close()
            S.barrier()
        else:
            norm_to_aT(1, 2, TH)
            dma('sp', x1s.rearrange("(kc p) t -> p kc t", p=128), xT[:, :, 2:TH], xTk, ['x1s'])
            cp(hpk[:, 0:32].rearrange("p (k t) -> p k t", t=2), xT[:, :, TH - 2:TH], xTk, ['hpk0'])
            for q in range(4):
                th_, h_ = divmod(q, 2)
                dma('sp', cin2[q].rearrange("(k p) t -> p k t", p=128), aT[:, 8 * h_:8 * h_ + 8, 2 + 512 * th_:2 + 512 * th_ + 512], aTk, ['c2in%d' % q])
                S.add('pool', lambda e, q=q: e.collective_compute("AllGather", ALU.bypass, replica_groups=RG, ins=[cin2[q].opt()], outs=[cout2[q].opt()]),
                      ['c2in%d' % q], ['c2out%d' % q], dma='cc')
            S.barrier()
            tok.close()
            S.barrier()

            if upto('sgu'):
                with ExitStack() as es:
                    hown = sb(es, "hown", [128, KC, TOK], BF16)
                    for q in range(4):
                        th_, h_ = divmod(q, 2)
                        dma('sp', hown[:, 8 * h_:8 * h_ + 8, 512 * th_:512 * th_ + 512], cin2[q].rearrange("(k p) t -> p k t", p=128), ['c2in%d' % q], ['hown%d' % q])
                    hk = ['hown%d' % q for q in range(4)]
                    lng = sb(es, "lng", [128, 2, 1024], F32)
                    WT = sb(es, "WTs", [128, 4, 128], BF16)
                    bsb = sb(es, "bsb", [128, 4, 512], F32)
                    sgm = sb(es, "sgm", [128, 128], F32)
                    with ExitStack() as es2:
                        WTf = sb(es2, "WTf", [128, 4, 128], F32)
                        dma('sp', WTf[:], sguw_d, [], ['WTf'])
                        dma('sp', sgm[:], sgum_d, [], ['sgm'])
                        dma('sp', lng[:], lngb_d, [], ['lng'])
                        dma('sp', bsb[:], sgub_d, [], ['bsb'])
                        for g in range(4):
                            tt(WT[:, g, :], WTf[:, g, :], sgm[:], ALU.mult, ['WTf', 'sgm'], ['WT%d' % g])
                        S.barrier()
                    wzv = w_zd.rearrange("(kc p) n -> p kc n", p=128)
                    uT = sb(es, "uT", [128, 8, TOK], BF16)
                    vn = sb(es, "vn", [128, 8, 1024], BF16)
                    g1 = [sb(es, "g1_%d" % i, [128, 512], F32) for i in range(2)]
                    g2 = [sb(es, "g2_%d" % i, [128, 512], F32) for i in range(2)]
                    GC = 2.0 * math.sqrt(2.0 / math.pi)
                    gcount = [0]

                    def gelu_from_ps(bank, out_ap, pk, wk):
                        i = gcount[0] % 2
                        gcount[0] += 1
                        act(g1[i][:], bank, AF.Square, [pk], ['g1_%d' % i])
                        ts(g1[i][:], g1[i][:], 0.044715, 1.0, ALU.mult, ALU.add, ['g1_%d' % i], ['g1_%d' % i])
                        tt(g1[i][:], g1[i][:], bank, ALU.mult, ['g1_%d' % i, pk], ['g1_%d' % i])
                        act(g2[i][:], g1[i][:], AF.Sigmoid, ['g1_%d' % i], ['g2_%d' % i], scale=GC)
                        tt(out_ap, g2[i][:], bank, ALU.mult, ['g2_%d' % i, pk], [wk])
                    with ExitStack() as es2:
                        wz = [sb(es2, "wz%d" % i, [128, KC, 256], BF16) for i in range(3)]
                        n = 0
                        for c2 in range(4):
                            wb = c2 % 3
                            dma('pool', wz[wb][:], wzv[:, :, 256 * c2:256 * c2 + 256], [], ['wz%d' % wb])
                            for half in range(2):
                                ch = 2 * c2 + half
                                for th in range(2):
                                    bk = n % 4
                                    n += 1
                                    mms([(ps[:, bk, :], wz[wb][:, kc, 128 * half:128 * half + 128], hown[:, kc, 512 * th:512 * th + 512], kc == 0, kc == KC - 1) for kc in range(KC)],
                                        ['wz%d' % wb] + hk, ['ps%d' % bk])
                                    gelu_from_ps(ps[:, bk, :], uT[:, ch, 512 * th:512 * th + 512], 'ps%d' % bk, 'uT%d_%d' % (ch, th))
                        S.barrier()
                    with ExitStack() as es2:
                        wv2 = [sb(es2, "wv2_%d" % i, [128, KC, 512], BF16) for i in range(2)]
                        for i in range(2):
                            for hhf in range(2):
                                dma('pool', wv2[i][:, :, 256 * hhf:256 * hhf + 256], wzv[:, :, 1024 + 512 * i + 256 * hhf:1024 + 512 * i + 256 * hhf + 256], [], ['wv2_%d_%d' % (i, hhf)])
                        wvk = ['wv2_%d_%d' % (i, hhf) for i in range(2) for hhf in range(2)]
                        vg = [sb(es2, "vg%d" % i, [128, 1024], F32) for i in range(2)]
                        st = sb(es2, "lnst", [128, 16], F32)
                        junk = sb(es2, "lnjunk", [128, 1024], BF16)
                        n = 0
                        for t8 in range(8):
                            p = t8 % 2
                            for i in range(2):
                                bk = 4 + n % 4
                                n += 1
                                mms([(ps[:, bk, :], hown[:, kc, 128 * t8:128 * t8 + 128], wv2[i][:, kc, :], kc == 0, kc == KC - 1) for kc in range(KC)],
                                    wvk + hk, ['ps%d' % bk])
                                gelu_from_ps(ps[:, bk, :], vg[p][:, 512 * i:512 * i + 512], 'ps%d' % bk, 'vg%d_%d' % (p, i))
                            vk = ['vg%d_0' % p, 'vg%d_1' % p]
                            c0 = 8 * p
                            S.add('dve', lambda e, p=p, c0=c0: e.reduce_sum(out=st[:, c0:c0 + 1], in_=vg[p][:], axis=mybir.AxisListType.X), vk, ['st%d_0' % p])
                            act(junk[:], vg[p][:], AF.Square, vk, ['junk', 'st%d_1' % p], accum_out=st[:, c0 + 1:c0 + 2])
                            ts(st[:, c0 + 2:c0 + 3], st[:, c0:c0 + 1], 1.0 / 1024, None, ALU.mult, None, ['st%d_0' % p], ['st%d_2' % p])
                            tt(st[:, c0 + 3:c0 + 4], st[:, c0 + 2:c0 + 3], st[:, c0 + 2:c0 + 3], ALU.mult, ['st%d_2' % p], ['st%d_3' % p])
                            stt(st[:, c0 + 4:c0 + 5], st[:, c0 + 1:c0 + 2], 1.0 / 1024, st[:, c0 + 3:c0 + 4], ALU.mult, ALU.subtract, ['st%d_1' % p, 'st%d_3' % p], ['st%d_4' % p])
                            act(st[:, c0 + 5:c0 + 6], st[:, c0 + 4:c0 + 5], AF.Ln, ['st%d_4' % p], ['st%d_5' % p], bias=eps_t[:, 0:1])
                            act(st[:, c0 + 5:c0 + 6], st[:, c0 + 5:c0 + 6], AF.Exp, ['st%d_5' % p], ['st%d_5' % p], scale=-0.5)
                            ts(vg[p][:], vg[p][:], st[:, c0 + 2:c0 + 3], st[:, c0 + 5:c0 + 6], ALU.subtract, ALU.mult, vk + ['st%d_2' % p, 'st%d_5' % p], ['vgn%d' % p])
                            tt(vg[p][:], vg[p][:], lng[:, 0, :], ALU.mult, ['vgn%d' % p, 'lng'], ['vgn%d' % p])
                            tt(vn[:, t8, :], vg[p][:], lng[:, 1, :], ALU.add, ['vgn%d' % p, 'lng'], ['vn%d' % t8] + vk)
                        S.barrier()
                    n = 0
                    for ch in range(8):
                        g = ch // 2
                        for th in range(2):
                            bk = n % 4
                            n += 1
                            for t4 in range(4):
                                t8 = 4 * th + t4
                                mms([(ps[:, bk, 128 * t4:128 * t4 + 128], vn[:, t8, 128 * ch:128 * ch + 128], WT[:, g, :], True, True)],
                                    ['vn%d' % t8, 'WT%d' % g], ['ps%d_%d' % (bk, t4)])
                            i = n % 2
                            pk4 = ['ps%d_%d' % (bk, t4) for t4 in range(4)]
                            tt(g1[i][:], ps[:, bk, :], bsb[:, g, :], ALU.add, pk4 + ['bsb'], ['g1_%d' % i])
                            tt(odT[:, ch, 512 * th:512 * th + 512], g1[i][:], uT[:, ch, 512 * th:512 * th + 512], ALU.mult, ['g1_%d' % i, 'uT%d_%d' % (ch, th)], ['odT%d_%d' % (ch, th)] + pk4)
                    cp(hpk[:, 32:48].rearrange("p (k t) -> p k t", t=2), odT[:, :, TOK - 2:TOK], ['odT%d_1' % ch for ch in range(8)], ['hpk1'])
                    S.barrier()
                if 'odT' in dbg_out:
                    dma('sp', dbg_out['odT'], odT[:], [], ['dbg6'])

            if upto('ret'):
                with ExitStack() as es:
                    ocT = sb(es, "ocT", [128, 2, SEQ], BF16)
                    QKr = sb(es, "QKr", [128, 2, SEQ], BF16)
                    gate = sb(es, "gate", [128, 2, SEQ], BF16)
                    Vr = sb(es, "Vr", [128, 32, 256], BF16)
                    Ktm = sb(es, "Ktm", [128, 32, 128], BF16)
                    wC = sb(es, "wC", [128, KC, 1024], BF16)
                    hin = [sb(es, "hinC%d" % i, [128, KC, 512], BF16) for i in range(2)]
                    rc = sb(es, "retc", [128, 264], F32)
                    cs = [sb(es, "cs%d" % i, [128, 2, 512], F32) for i in range(2)]
                    cstm = [sb(es, "cstm%d" % i, [128, 4, 256], F32) for i in range(2)]
                    ra = sb(es, "ra", [128, 512], F32)
                    rb = sb(es, "rb", [128, 512], F32)
                    SD = [sb(es, "SD%d" % i, [128, 128], BF16) for i in range(2)]
                    Qs = [sb(es, "Qs%d" % i, [128, 128], BF16) for i in range(2)]
                    Sf = sb(es, "Sf", [128, 256], F32)
                    Sb = [sb(es, "Sb%d" % i, [128, 256], BF16) for i in range(2)]
                    yb = sb(es, "yb", [128, 2, 512], F32)
                    sqy = sb(es, "sqy", [128, 2, 512], BF16)
                    rsy = sb(es, "rsy", [128, 512], F32)
                    wcv = w_inC.rearrange("(kc p) n -> p kc n", p=128)
                    for i in range(4):
                        dma('pool', wC[:, :, 256 * i:256 * i + 256], wcv[:, :, 256 * i:256 * i + 256], [], ['wC%d' % i])
                    wck = ['wC%d' % i for i in range(4)]
                    dma('sp', rc[:], retc_d, [], ['retc'])
                    Dblk = rc[:, 0:128]
                    qdec = rc[:, 128:256]
                    kdec = rc[:, 256:257]
                    g128 = rc[:, 257:258]
                    retg = rc[:, 258:260]
                    S.add('dve', lambda e: e.memset(Sf[:], 0.0), [], ['Sf'])
                    ra2 = sb(es, "ra2", [128, 512], F32)
                    rk = [sb(es, "rk%d" % i, [128, 128], F32) for i in range(2)]

                    def loads(t8):
                        p = t8 % 2
                        r = t8 // 2
                        csl = slice(512 * (t8 % 2), 512 * (t8 % 2) + 512)
                        sl = slice(512 * t8, 512 * t8 + 512)
                        for q in range(2):
                            qq = 2 * (t8 % 2) + q
                            dma('sp', hin[p][:, 8 * q:8 * q + 8, :], cout2[qq][1024 * r:1024 * r + 1024, :].rearrange("(k p) t -> p k t", p=128), ['c2out%d' % qq], ['hinC%d_%d' % (p, q)])
                        dma('sp', cs[p][:, 0, :], cosT_d[:, sl], [], ['cs%d_0' % p])
                        dma('sp', cs[p][:, 1, :], sinT_d[:, sl], [], ['cs%d_1' % p])
                        dma('sp', cstm[p][:], cstm_d[:, 4 * t8:4 * t8 + 4, :], [], ['cstm%d' % p])

                    def inproj_groups(t8):
                        p = t8 % 2
                        sl = slice(512 * t8, 512 * t8 + 512)
                        hk = ['hinC%d_%d' % (p, q) for q in range(2)]
                        G = []

                        def g_qk(qk):
                            b0 = 2 * qk
                            for w in range(2):
                                g = 2 * qk + w
                                mms([(ps[:, b0 + w, :], wC[:, kc, 128 * g:128 * g + 128], hin[p][:, kc, :], kc == 0, kc == KC - 1) for kc in range(KC)], wck + hk, ['ps%d' % (b0 + w)])
                            tt(ra[:], ps[:, b0, :], cs[p][:, 0, :], ALU.mult, ['ps%d' % b0, 'cs%d_0' % p], ['ra'])
                            tt(rb[:], ps[:, b0 + 1, :], cs[p][:, 1, :], ALU.mult, ['ps%d' % (b0 + 1), 'cs%d_1' % p], ['rb'])
                            tt(QKr[:, qk, sl], ra[:], rb[:], ALU.add, ['ra', 'rb'], ['QKr%d_%d' % (qk, t8)])

                        def g_gate(hf):
                            g = 6 + hf
                            mms([(ps[:, hf, :], wC[:, kc, 128 * g:128 * g + 128], hin[p][:, kc, :], kc == 0, kc == KC - 1) for kc in range(KC)], wck + hk, ['ps%d' % hf])
                            act(gate[:, hf, sl], ps[:, hf, :], AF.Silu, ['ps%d' % hf], ['gate%d_%d' % (hf, t8)])

                        def g_v(sub):
                            blk = 4 * t8 + sub
                            tsl = slice(128 * sub, 128 * sub + 128)
                            mms([(ps[:, 2, 0:256], hin[p][:, kc, tsl], wC[:, kc, 512:768], kc == 0, kc == KC - 1) for kc in range(KC)], wck + hk, ['ps2'])
                            cp(Vr[:, blk, :], ps[:, 2, 0:256], ['ps2'], ['Vr%d' % blk], eng='act')

                        def g_k(sub):
                            blk = 4 * t8 + sub
                            tsl = slice(128 * sub, 128 * sub + 128)
                            mms([(ps[:, 3, 0:256], hin[p][:, kc, tsl], wC[:, kc, 256:512], kc == 0, kc == KC - 1) for kc in range(KC)], wck + hk, ['ps3'])
                            tt(rk[0][:], ps[:, 3, 0:128], cstm[p][:, sub, 0:128], ALU.mult, ['ps3', 'cstm%d' % p], ['rk0'])
                            tt(rk[1][:], ps[:, 3, 128:256], cstm[p][:, sub, 128:256], ALU.mult, ['ps3', 'cstm%d' % p], ['rk1'])
                            tt(rk[0][:], rk[0][:], rk[1][:], ALU.add, ['rk0', 'rk1'], ['rk0'])
                            ts(Ktm[:, blk, :], rk[0][:], kdec, None, ALU.mult, None, ['rk0', 'retc'], ['Ktm%d' % blk])
                        G.append(lambda: g_qk(0))
                        G.append(lambda: g_qk(1))
                        G.append(lambda: g_gate(0))
                        G.append(lambda: g_gate(1))
                        for sub in range(4):
                            G.append(lambda sub=sub: g_v(sub))
                            G.append(lambda sub=sub: g_k(sub))
                        return G

                    def rec_steps(t8):
                        R = []
                        for sub in range(4):
                            blk = 4 * t8 + sub
                            bsl = slice(128 * blk, 128 * blk + 128)
                            pb = blk % 2

                            def r_a(blk=blk, bsl=bsl, pb=pb):
                                mms([(ps[:, 4, 0:128], QKr[:, 1, bsl], QKr[:, 0, bsl], True, True)], ['QKr0_%d' % t8, 'QKr1_%d' % t8], ['ps4'])
                                tt(SD[pb][:], ps[:, 4, 0:128], Dblk, ALU.mult, ['ps4', 'retc'], ['SD%d' % pb])
                                if blk > 0:
                                    tt(Qs[pb][:], QKr[:, 0, bsl], qdec, ALU.mult, ['QKr0_%d' % t8, 'retc'], ['Qs%d' % pb])

                            def r_b(blk=blk, pb=pb, sub=sub):
                                lst = []
                                for hf in range(2):
                                    lst.append((ps[:, 5, 128 * hf:128 * hf + 128], Vr[:, blk, 128 * hf:128 * hf + 128], SD[pb][:], True, blk == 0))
                                rd = ['SD%d' % pb, 'Vr%d' % blk]
                                if blk > 0:
                                    for hf in range(2):
                                        lst.append((ps[:, 5, 128 * hf:128 * hf + 128], Sb[(blk - 1) % 2][:, 128 * hf:128 * hf + 128], Qs[pb][:], False, True))
                                    rd += ['Qs%d' % pb, 'Sb%d' % ((blk - 1) % 2)]
                                    lst = [lst[0], lst[2], lst[1], lst[3]]
                                mms(lst, rd, ['ps5'])
                                cp(yb[:, :, 128 * sub:128 * sub + 128], ps[:, 5, 0:256].rearrange("p (h t) -> p h t", h=2), ['ps5'], ['yb%d' % sub], eng='act')

                            def r_c(blk=blk):
                                mms([(ps[:, 6, 0:256], Ktm[:, blk, :], Vr[:, blk, :], True, True)], ['Ktm%d' % blk, 'Vr%d' % blk], ['ps6'])
                                stt(Sf[:], Sf[:], g128, ps[:, 6, 0:256], ALU.mult, ALU.add, ['Sf', 'ps6', 'retc'], ['Sf'])
                                cp(Sb[blk % 2][:], Sf[:], ['Sf'], ['Sb%d' % (blk % 2)])
                            R += [r_a, r_b, r_c]
                        return R

                    def norm_gate(t8):
                        sl = slice(512 * t8, 512 * t8 + 512)
                        ybk = ['yb%d' % sub for sub in range(4)]
                        act(sqy[:], yb[:], AF.Square, ybk, ['sqy'])
                        mms([(ps[:, 7, :], ones, sqy[:, hf, :], hf == 0, hf == 1) for hf in range(2)], ['sqy', 'tri'], ['ps7'])
                        rstd_from_ps(ps[:, 7, :], rsy[:], 256, ['ps7'], ['rsy'])
                        for hf in range(2):
                            stt(ra2[:], yb[:, hf, :], retg[:, hf:hf + 1], rsy[:], ALU.mult, ALU.mult, ybk + ['rsy', 'retc'], ['ra2'])
                            tt(ocT[:, hf, sl], ra2[:], gate[:, hf, sl], ALU.mult, ['ra2', 'gate%d_%d' % (hf, t8)], ['oT%d_%d' % (hf, t8)])
                        if t8 % 2 == 1 and upto('ag3'):
                            ag_slab(t8 // 2, cin3s, cout3a, ocT, 2, 'c3')

                    loads(0)
                    for t8 in range(9):
                        if t8 + 1 < 8:
                            loads(t8 + 1)
                        G = inproj_groups(t8) if t8 < 8 else []
                        R = rec_steps(t8 - 1) if t8 >= 1 else []
                        for k in range(max(len(G), len(R))):
                            if k < len(G):
                                G[k]()
                            if k < len(R):
                                R[k]()
                        if t8 >= 1:
                            norm_gate(t8 - 1)
                    if 'ocT' in dbg_out:
                        dma('sp', dbg_out['ocT'], ocT[:], ['oT%d_%d' % (hf, t8) for hf in range(2) for t8 in range(8)], ['dbg7'])
                    S.barrier()

            if upto('ag3'):
                dma('sp', cinH, hpk[:], ['hpk0', 'hpk1'], ['cinH'])
                S.add('pool', lambda e: e.collective_compute("AllGather", ALU.bypass, replica_groups=RG, ins=[cinH.opt()], outs=[coutH.opt()]),
                      ['cinH'], ['coutH'], dma='cc')
                dma('sp', hh[:], coutH[bass.ds(((rank + 3) % 4) * 128, 128), :], ['coutH'], ['hh'])
                ts(hh[:], hh[:], smallB[:, 3:4], None, ALU.mult, None, ['hh', 'smallB'], ['hh'])

            if upto('out1'):
                tok = ExitStack()
                xT = sb(tok, "xTb", [128, KC, TH], F32)
                aT = sb(tok, "aTb", [128, KC, TH], BF16)
                rs1 = sb(tok, "rs1b", [128, TH], F32)
                dma('sp', xT[:, :, 2:TH], x1s.rearrange("(kc p) t -> p kc t", p=128), ['x1s'], ['xTown'])
                cp(xT[:, :, 0:2], hh[:, 0:32].rearrange("p (k t) -> p k t", t=2), ['hh', 'xTown'], xTk)
                cv3 = cout3a.rearrange("(sk p) t -> p sk t", p=128)
                c3k = ['c3out%d' % t for t in range(4)]
                dma('sp', aT[:, 0:8, 2:TH], cv3[:, bass.ds(rank * 8, 8), :], c3k, ['aT%d' % kc for kc in range(8)])
                dma('sp', aT[:, 0:8, 0:2], cv3[:, bass.ds(((rank + 3) % 4) * 8, 8), TOK - 2:TOK], c3k, ['aTh3'])
                ts(aT[:, 0:8, 0:2], aT[:, 0:8, 0:2], smallB[:, 3:4], None, ALU.mult, None, ['aTh3', 'smallB'] + ['aT%d' % kc for kc in range(8)], ['aT%d' % kc for kc in range(8)])
                for ch in range(8):
                    cp(aT[:, 8 + ch, 2:TH], odT[:, ch, :], [], ['aT%d' % (8 + ch)], eng=('act' if ch % 2 else 'dve'))
                cp(aT[:, 8:16, 0:2], hh[:, 32:48].rearrange("p (k t) -> p k t", t=2), ['hh'] + ['aT%d' % (8 + ch) for ch in range(8)], ['aT%d' % (8 + ch) for ch in range(8)])
                S.barrier()
                out_proj(w_out1, 'o1')
                if 'xmid1' in dbg_out:
                    dma('sp', dbg_out['xmid1'].rearrange("(kc p) t -> p kc t", p=128), xT[:], xTk, ['dbg8'])
                if upto('ffn1'):
                    norm_to_aT(3, 0, TH)
                    conv_ffn(1)
                final_out(upto('final'))
                tok.close()
                S.barrier()
        S.emit()
    nc.used_inputs = used_inputs
    return nc


def _pp(v, n=KC):
    return np.ascontiguousarray(np.asarray(v, np.float32).reshape(n, 128).T)


def prep_inputs(inp):
    f = lambda k: np.asarray(inp[k], np.float32)
    x = f('x')
    cst = _consts()
    bidx = _bias_index()
    ab_in = f('ab_w_in')[0]
    ab_out = f('ab_w_out')[0]
    rel_bias = f('rel_bias')
    lam = f('diff_lambda')[0]
    subln = f('diff_subln_g')[0]
    gains = np.stack([_pp(f('norm_mix_g')[0]), _pp(f('norm_mix_g')[1]), _pp(f('norm_ffn_g')[0]), _pp(f('norm_ffn_g')[1]),
                      _pp(f('final_norm_g'))], axis=1)
    w_up_r, convp_r, w_down_r = [], [], []
    for l in range(2):
        wu = f('ffn_w_up')[l]
        w_up_r.append(np.ascontiguousarray(np.concatenate([wu[:, :DFF].reshape(D, NCH, 128), wu[:, DFF:].reshape(D, NCH, 128)], axis=2)))
        cw = f('ffn_conv_w')[l]
        cb = f('ffn_conv_b')[l]
        cp_ = np.zeros((128, NCH, 8), np.float32)
        for ag in range(2):
            for jj in range(3):
                cp_[:, :, 4 * ag + jj] = cw[jj, ag * DFF:(ag + 1) * DFF].reshape(NCH, 128).T
            cp_[:, :, 4 * ag + 3] = cb[ag * DFF:(ag + 1) * DFF].reshape(NCH, 128).T
        convp_r.append(cp_)
        w_down_r.append(np.ascontiguousarray(f('ffn_w_down')[l]))
    perm = []
    for r in range(4):
        for g in range(4):
            base = [128 * (2 * r), 128 * (2 * r + 1), 1024 + 256 * r, 1024 + 256 * r + 128][g]
            perm += list(range(base, base + 128))
    w_out0 = np.ascontiguousarray(ab_out[perm, :])
    cd_in = f('cd_w_in')[0]
    cd_out = f('cd_w_out')[0]
    sgw = f('sgu_w')[0]
    sguw = np.ascontiguousarray(sgw.transpose(2, 0, 1))
    jj = np.arange(128)[:, None]
    ii = np.arange(128)[None, :]
    sgum = ((jj // 64) <= (ii // 64)).astype(np.float32)
    lngb = np.ascontiguousarray(np.broadcast_to(np.stack([f('sgu_ln_g')[0], f('sgu_ln_b')[0]])[None], (128, 2, 1024)))
    sgub = np.ascontiguousarray(np.broadcast_to(np.tile(f('sgu_b')[0], (1, 4))[None], (128, 4, 512)))
    cos, sin = _rot_tables(None)
    cosT = np.ascontiguousarray(cos.T)
    sinT = np.ascontiguousarray(sin.T)
    cstm = np.ascontiguousarray(np.concatenate([cos, sin], -1).reshape(32, 128, 256).transpose(1, 0, 2))
    retg = f('ret_norm_g')[0]
    perm1 = list(range(2048))
    w_out1 = np.ascontiguousarray(cd_out[perm1, :])
    maps = []
    for c in range(8):
        b, j = divmod(c, 4)
        m = {}
        m['w_zd'] = np.ascontiguousarray(cd_in[:, 3072:5120])
        m['sguw'] = sguw
        m['sgum'] = sgum
        m['lngb'] = lngb
        m['sgub'] = sgub
        sw = (np.arange(128) + 64) % 128
        qcols = np.arange(128 * j, 128 * j + 128)
        kcols = 512 + qcols
        m['w_inC'] = np.ascontiguousarray(np.concatenate([cd_in[:, qcols], cd_in[:, qcols[sw]], cd_in[:, kcols], cd_in[:, kcols[sw]],
                                                          cd_in[:, 1024 + 256 * j:1024 + 256 * j + 256], cd_in[:, 2048 + 256 * j:2048 + 256 * j + 256]], axis=1))
        log_g = np.log(np.float32(1.0) - np.float32(2.0) ** np.float32(-5.0 - j)).astype(np.float32)
        sidx = np.arange(128, dtype=np.float32)
        rcst = np.zeros((128, 264), np.float32)
        sc = np.float32(128.0 ** -0.5)
        rcst[:, 0:128] = np.exp(log_g * np.abs(sidx[None, :] - sidx[:, None])) * ((jj // 64) <= (ii // 64)) * sc
        rcst[:, 128:256] = (np.exp(log_g * (sidx + 1.0)) * sc)[None, :]
        rcst[:, 256] = np.exp(log_g * (127.0 - sidx))
        rcst[:, 257] = np.exp(log_g * np.float32(128.0))
        rcst[:, 258] = retg[0:128]
        rcst[:, 259] = retg[128:256]
        m['retc'] = rcst
        m['cosT'] = cosT
        m['sinT'] = sinT
        m['cstm'] = cstm
        m['w_out1'] = w_out1
        m['xT_seq'] = np.ascontiguousarray(x[b].T)
        xo = np.zeros((D, TH), np.float32)
        xo[:, 2:] = x[b, TOK * j:TOK * j + TOK].T
        if j > 0:
            xo[:, 0:2] = x[b, TOK * j - 2:TOK * j].T
        m['xT_own'] = xo
        m['gains'] = gains
        m['tri'] = cst['tri']
        m['masks'] = cst['masks']
        h0, h1 = 2 * j, 2 * j + 1
        colsA = []
        for h in (h0, h1):
            colsA += list(range(128 * h, 128 * h + 128)) + list(range(1024 + 128 * h, 1024 + 128 * h + 128))
        for h in (h0, h1):
            colsA += list(range(2048 + 128 * h, 2048 + 128 * h + 128))
        m['w_inA'] = np.ascontiguousarray(ab_in[:, colsA])
        colsB = []
        for base in (3072, 4096):
            for cc in range(2):
                colsB += list(range(base + 256 * j + 128 * cc, base + 256 * j + 128 * cc + 128))
        colsB += list(range(5120 + 256 * j, 5120 + 256 * j + 256))
        m['w_inB'] = np.ascontiguousarray(ab_in[:, colsB])
        m['biasM'] = np.ascontiguousarray(rel_bias[bidx, j]).astype(np.float32)
        sB = np.zeros((128, 8), np.float32)
        sB[:, 0] = rel_bias[15, j]
        sB[:, 1] = subln[0:128]
        sB[:, 2] = subln[128:256]
        sB[:, 3] = 0.0 if j == 0 else 1.0
        m['smallB'] = sB
        m['lamv'] = np.ascontiguousarray(np.broadcast_to(lam.reshape(1, 512), (128, 512)))
        m['w_out0'] = w_out0
        for l in range(2):
            m['w_up%d' % l] = w_up_r[l]
            m['convp%d' % l] = convp_r[l]
            m['w_down%d' % l] = w_down_r[l]
        maps.append(m)
    return maps


_NC_CACHE = {}


def kernel(**inputs):
    maps = prep_inputs(inputs)
    if 'nc' not in _NC_CACHE:
        _NC_CACHE['nc'] = build()
    ui = _NC_CACHE['nc'].used_inputs
    maps = [{k: m[k] for k in ui} for m in maps]
    res = run_bass_kernel_spmd(_NC_CACHE['nc'], maps, core_ids=list(range(8)))
    out = np.zeros((NB, SEQ, D), np.float32)
    for c in range(8):
        b, j = divmod(c, 4)
        out[b, TOK * j:TOK * j + TOK, :] = res.results[c]['yT'].T
    return out
```

```python
import math
from contextlib import ExitStack
import numpy as np
import concourse.bass as bass
import concourse.mybir as mybir
from concourse.bass_utils import run_bass_kernel_spmd

F32 = mybir.dt.float32
BF16 = mybir.dt.bfloat16
AF = mybir.ActivationFunctionType
ALU = mybir.AluOpType

D = 2048
SEQ = 4096
NB = 2
TOK = 1024
TH = TOK + 2
KC = D // 128
DFF = 5632
NCH = DFF // 128
EPS = 1e-6
RG = [[0, 1, 2, 3], [4, 5, 6, 7]]
TT3 = [(0, 342), (342, 684), (684, 1026)]
STOP_ORDER = ['n0', 'projA', 'attA', 'attB', 'ag1', 'out0', 'norm0', 'ffn0', 'ag2', 'sgu', 'ret', 'ag3', 'out1', 'ffn1', 'final']


class Sched:
    ENGS = ('pe', 'act', 'dve', 'pool', 'sp')

    def __init__(self, nc, ndsem=8):
        self.nc = nc
        self.ndsem = ndsem
        self.streams = {e: [] for e in self.ENGS}
        self.cnt = {}
        self.seen = {e: {} for e in self.ENGS}
        self.lastw = {}
        self.readers = {}
        self.dma_idx = {e: 0 for e in self.ENGS}
        self.semkeys = []
        self.sems = {}

    def _semkey(self, k):
        if k not in self.cnt:
            self.cnt[k] = 0
            self.semkeys.append(k)
        return k

    def add(self, eng, fn, reads=(), writes=(), dma=False):
        deps = {}

        def need(d):
            for semk, val in d.items():
                if deps.get(semk, 0) < val:
                    deps[semk] = val
        for b in reads:
            need(self.lastw.get(b, {}))
        for b in writes:
            need(self.lastw.get(b, {}))
            need(self.readers.get(b, {}))
        if dma == 'cc':
            semk = self._semkey('CC')
            val = self.cnt[semk] + 1
            inc = 1
        elif dma:
            k = self.dma_idx[eng]
            self.dma_idx[eng] += 1
            semk = self._semkey('D_%s_%d' % (eng, k % self.ndsem))
            val = 16 * (k // self.ndsem + 1)
            if val > 16:
                need({semk: val - 16})
            inc = 16
        else:
            semk = self._semkey('E_' + eng)
            val = self.cnt[semk] + 1
            inc = 1
        self.cnt[semk] = val
        waits = []
        for sk, v in deps.items():
            if self.seen[eng].get(sk, 0) >= v:
                continue
            if sk == 'E_pe' and eng == 'pe':
                continue
            waits.append((sk, v))
            self.seen[eng][sk] = v
        self.streams[eng].append((waits, fn, semk, inc))
        for b in reads:
            r = self.readers.setdefault(b, {})
            if r.get(semk, 0) < val:
                r[semk] = val
        for b in writes:
            self.lastw[b] = {semk: val}
            self.readers[b] = {}
        return (semk, val)

    def barrier(self):
        for eng in self.ENGS:
            waits = []
            for sk, v in self.cnt.items():
                if v == 0 or self.seen[eng].get(sk, 0) >= v:
                    continue
                if sk == 'E_' + eng and eng in ('pe', 'sp'):
                    continue
                if sk == 'CC':
                    continue
                waits.append((sk, v))
                self.seen[eng][sk] = v
            if waits:
                self.streams[eng].append((waits, None, None, 0))

    def emit(self):
        nc = self.nc
        with ExitStack() as es:
            for k in self.semkeys:
                self.sems[k] = es.enter_context(nc.semaphore(k))
            with nc.Block() as block:
                def runner(name):
                    def run(e):
                        for waits, fn, semk, inc in self.streams[name]:
                            for sk, v in waits:
                                e.wait_ge(self.sems[sk], v)
                            if fn is not None:
                                ins = fn(e)
                                ins.then_inc(self.sems[semk], inc)
                    return run
                block.tensor(runner('pe'))
                block.scalar(runner('act'))
                block.vector(runner('dve'))
                block.gpsimd(runner('pool'))
                block.sync(runner('sp'))


def _rel_bucket_np(rel):
    nb = 16
    max_exact = 8
    ret = np.where(rel > 0, nb, 0)
    n = np.abs(rel)
    n_f = np.maximum(n, 1).astype(np.float32)
    large = max_exact + (np.log(n_f / np.float32(max_exact)) / np.float32(math.log(128 / max_exact))
                         * np.float32(nb - max_exact)).astype(np.int32)
    large = np.minimum(large, nb - 1)
    return ret + np.where(n < max_exact, n, large)


def _consts():
    c = {}
    j = np.arange(128)[:, None]
    s = np.arange(128)[None, :]
    tri = np.zeros((128, 3, 128), np.float32)
    tri[:, 0, :] = 1.0
    tri[:, 1, :] = (j > s)
    tri[:, 2, :] = (j <= s)
    c['tri'] = tri
    srow = np.arange(128)[:, None]
    t = np.arange(512)[None, :]
    mk = np.zeros((128, 8, 512), np.float32)
    for oi, o in enumerate((0, 128, 256, 384)):
        mk[:, oi, :] = ((o + srow) < t)
        mk[:, 4 + oi, :] = (((o + srow) // 64) <= (t // 64))
    c['masks'] = mk
    return c


def _bias_index():
    s = np.arange(128)[:, None]
    u = np.arange(1024)[None, :]
    return _rel_bucket_np((s - u + 384).astype(np.int32))


def _rot_tables(gamma):
    d = 128
    inv_freq = (10000.0 ** (-np.arange(0, d, 2, dtype=np.float32) / d)).astype(np.float32)
    ang = np.arange(SEQ, dtype=np.float32)[:, None] * inv_freq[None, :]
    cos = np.concatenate([np.cos(ang), np.cos(ang)], -1).astype(np.float32)
    sin = np.concatenate([-np.sin(ang), np.sin(ang)], -1).astype(np.float32)
    return cos, sin


def build(stop='final', dbg=()):
    nc = bass.Bass("TRN2", target_bir_lowering=False)
    S = Sched(nc)
    stop_i = STOP_ORDER.index(stop)

    def upto(name):
        return STOP_ORDER.index(name) <= stop_i

    used_inputs = []

    def din(name, shape, dt=F32, need='n0'):
        if not upto(need):
            return None
        used_inputs.append(name)
        return nc.dram_tensor(name, list(shape), dt, kind="ExternalInput").ap()

    def dout(name, shape, dt=F32):
        return nc.dram_tensor(name, list(shape), dt, kind="ExternalOutput").ap()

    def dint(name, shape, dt):
        return nc.dram_tensor(name, list(shape), dt).ap()

    xT_seq = din("xT_seq", [D, SEQ])
    xT_own = din("xT_own", [D, TH], need="out0")
    gains = din("gains", [128, 5, KC])
    tri_d = din("tri", [128, 3, 128])
    masks_d = din("masks", [128, 8, 512])
    w_inA = din("w_inA", [D, 768])
    w_inB = din("w_inB", [D, 768], need="attB")
    biasM_d = din("biasM", [128, 1024], need="attB")
    smallB_d = din("smallB", [128, 8])
    lamv_d = din("lamv", [128, 512], need="attB")
    w_out0 = din("w_out0", [D, D], need="out0")
    w_up = [din("w_up%d" % l, [D, NCH, 256], need="ffn%d" % l) for l in range(2)]
    convp = [din("convp%d" % l, [128, NCH, 8], need="ffn%d" % l) for l in range(2)]
    w_down = [din("w_down%d" % l, [DFF, D], need="ffn%d" % l) for l in range(2)]
    w_zd = din("w_zd", [D, 2048], need="sgu")
    sguw_d = din("sguw", [128, 4, 128], need="sgu")
    sgum_d = din("sgum", [128, 128], need="sgu")
    lngb_d = din("lngb", [128, 2, 1024], need="sgu")
    sgub_d = din("sgub", [128, 4, 512], need="sgu")
    w_inC = din("w_inC", [D, 1024], need="ret")
    retc_d = din("retc", [128, 264], need="ret")
    cosT_d = din("cosT", [128, SEQ], need="ret")
    sinT_d = din("sinT", [128, SEQ], need="ret")
    cstm_d = din("cstm", [128, 32, 256], need="ret")
    w_out1 = din("w_out1", [D, D], need="out1")
    yT = dout("yT", [D, TOK])
    dbg_out = {}
    for name, shape, dt in dbg:
        dbg_out[name] = dout("dbg_" + name, shape, dt)

    hseq = dint("hseq", [D, SEQ], BF16)
    cin1s = [dint("cin1s_%d" % t, [512, TOK], BF16) for t in range(4)]
    cout1a = dint("cout1a", [4 * 2048, TOK], BF16)
    cin2 = [dint("cin2_%d" % g, [1024, 512], BF16) for g in range(4)]
    cout2 = [dint("cout2_%d" % g, [4096, 512], BF16) for g in range(4)]
    cin3s = [dint("cin3s_%d" % t, [256, TOK], BF16) for t in range(4)]
    cout3a = dint("cout3a", [4 * 1024, TOK], BF16)
    cinH = dint("cinH", [128, 48], F32)
    coutH = dint("coutH", [512, 48], F32)
    x1s = dint("x1s", [D, TOK], F32)

    pid = nc.partition_id()
    rank = pid % 4

    def dma(q, out, in_, reads, writes):
        return S.add(q, lambda e: e.dma_start(out=out, in_=in_), reads, writes, dma=True)

    def act(out, in_, func, reads, writes, **kw):
        return S.add('act', lambda e: e.activation(out=out, in_=in_, func=func, **kw), reads, writes)

    def tt(out, in0, in1, op, reads, writes, eng='dve'):
        return S.add(eng, lambda e: e.tensor_tensor(out=out, in0=in0, in1=in1, op=op), reads, writes)

    def ts(out, in0, s1, s2, op0, op1, reads, writes, eng='dve'):
        if op1 is None:
            return S.add(eng, lambda e: e.tensor_scalar(out=out, in0=in0, scalar1=s1, scalar2=None, op0=op0), reads, writes)
        return S.add(eng, lambda e: e.tensor_scalar(out=out, in0=in0, scalar1=s1, scalar2=s2, op0=op0, op1=op1), reads, writes)

    def stt(out, in0, scalar, in1, op0, op1, reads, writes):
        return S.add('dve', lambda e: e.scalar_tensor_tensor(out=out, in0=in0, scalar=scalar, in1=in1, op0=op0, op1=op1), reads, writes)

    def cp(out, in_, reads, writes, eng='dve'):
        if eng == 'act':
            return act(out, in_, AF.Copy, reads, writes)
        return S.add(eng, lambda e: e.tensor_copy(out=out, in_=in_), reads, writes)

    def mms(lst, reads, writes):
        def fn(e):
            r = None
            for (o, l, rh, st, sp) in lst:
                r = e.matmul(o, lhsT=l, rhs=rh, start=st, stop=sp)
            return r
        return S.add('pe', fn, reads, writes)

    with ExitStack() as glob:
        uniq = [0]

        def sb(es, name, shape, dt):
            uniq[0] += 1
            return es.enter_context(nc.sbuf_tensor("s%d_%s" % (uniq[0], name), list(shape), dt))

        ps = glob.enter_context(nc.psum_tensor("ps", [128, 8, 512], F32))
        tri = sb(glob, "tri", [128, 3, 128], BF16)
        masks = sb(glob, "masks", [128, 8, 512], BF16)
        gn = sb(glob, "gn", [128, 5, KC], F32)
        smallB = sb(glob, "smallB", [128, 8], F32)
        with ExitStack() as es:
            trif = sb(es, "trif", [128, 3, 128], F32)
            mkf = sb(es, "mkf", [128, 8, 512], F32)
            dma('sp', trif[:], tri_d, [], ['trif'])
            dma('sp', mkf[:], masks_d, [], ['mkf'])
            dma('sp', gn[:], gains, [], ['gn'])
            dma('sp', smallB[:], smallB_d, [], ['smallB'])
            cp(tri[:], trif[:], ['trif'], ['tri'])
            cp(masks[:], mkf[:], ['mkf'], ['masks'])
            S.barrier()
        ones = tri[:, 0, :]
        Umat = tri[:, 1, :]
        Lmat = tri[:, 2, :]

        def rstd_from_ps(bank_ap, out_ap, n, reads, writes):
            act(out_ap, bank_ap, AF.Ln, reads, writes, scale=1.0 / n, bias=eps_t[:, 0:1])
            act(out_ap, out_ap, AF.Exp, writes, writes, scale=-0.5)

        eps_t = sb(glob, "eps_t", [128, 1], F32)
        S.add('dve', lambda e: e.memset(eps_t[:], EPS), [], ['eps'])
        S.barrier()

        xsv = xT_seq.rearrange("(kc p) t -> p kc t", p=128)
        hsv = hseq.rearrange("(kc p) t -> p kc t", p=128)
        wst = ExitStack()
        wA_t = sb(wst, "wA_t", [128, KC, 768], BF16)
        wB_t = sb(wst, "wB_t", [128, KC, 768], BF16)
        wvA = w_inA.rearrange("(kc p) n -> p kc n", p=128)
        for i in range(3):
            dma('pool', wA_t[:, :, 256 * i:256 * i + 256], wvA[:, :, 256 * i:256 * i + 256], [], ['wA%d' % i])
        mix = ExitStack()
        oT_all = sb(mix, "oT_all", [128, 4, SEQ], BF16)
        if upto('attB'):
            wvB = w_inB.rearrange("(kc p) n -> p kc n", p=128)
            for i in range(3):
                dma('pool', wB_t[:, :, 256 * i:256 * i + 256], wvB[:, :, 256 * i:256 * i + 256], ['hs16_15'], ['wB%d' % i])

        def in_proj(es, wA, QK, V, tagp):
            hbb = [sb(es, "hin%s%d" % (tagp, i), [128, KC, 512], BF16) for i in range(2)]
            wk = ['w%s%d' % (tagp, i) for i in range(3)]
            ev = 0
            for t8 in range(8):
                p = t8 % 2
                sl = slice(512 * t8, 512 * t8 + 512)
                dma('sp', hbb[p][:], hsv[:, :, sl], ['hs16_%d' % (2 * t8), 'hs16_%d' % (2 * t8 + 1)], ['hin%d' % p])
                for g in range(4):
                    bk = ev % 4
                    mms([(ps[:, bk, :], wA[:, kc, 128 * g:128 * g + 128], hbb[p][:, kc, :], kc == 0, kc == KC - 1) for kc in range(KC)],
                        ['hin%d' % p] + wk, ['ps%d' % bk])
                    cp(QK[:, g, sl], ps[:, bk, :], ['ps%d' % bk], ['QK%d_%d' % (g, t8)], eng=('act' if ev % 2 else 'dve'))
                    ev += 1
                for sub in range(4):
                    bk = ev % 4
                    mms([(ps[:, bk, 0:256], hbb[p][:, kc, 128 * sub:128 * sub + 128], wA[:, kc, 512:768], kc == 0, kc == KC - 1) for kc in range(KC)],
                        ['hin%d' % p] + wk, ['ps%d' % bk])
                    cp(V[:, 4 * t8 + sub, :], ps[:, bk, 0:256], ['ps%d' % bk], ['V%d' % (4 * t8 + sub)], eng=('act' if ev % 2 else 'dve'))
                    ev += 1

        scl = 128.0 ** -0.5

        def ag_slab(t, cin, cout_all, src, ng, tagp):
            rows = ng * 128 * 4
            dma('sp', cin[t].rearrange("(g p) t -> p g t", p=128), src[:, :, TOK * t:TOK * t + TOK],
                ['oT%d_%d' % (g, qt) for g in range(ng) for qt in (2 * t, 2 * t + 1)], ['%sin%d' % (tagp, t)])
            S.add('pool', lambda e: e.collective_compute("AllGather", ALU.bypass, replica_groups=RG, ins=[cin[t].opt()],
                                                         outs=[cout_all[rows * t:rows * t + rows, :].opt()]),
                  ['%sin%d' % (tagp, t)], ['%sout%d' % (tagp, t)], dma='cc')

        if upto('projA'):
            with ExitStack() as es:
                QK = sb(es, "QKa", [128, 4, SEQ], BF16)
                V = sb(es, "Va", [128, 32, 256], BF16)
                with ExitStack() as es2:
                    xs = [sb(es2, "xs%d" % i, [128, KC, 256], F32) for i in range(2)]
                    sq = sb(es2, "sq", [128, KC, 256], BF16)
                    hb = [sb(es2, "hb%d" % i, [128, KC, 256], BF16) for i in range(2)]
                    rs = [sb(es2, "rs%d" % i, [128, 256], F32) for i in range(2)]
                    wkA = ['wA%d' % i for i in range(3)]
                    dma('sp', xs[0][:], xsv[:, :, 0:256], [], ['xs0'])
                    dma('sp', xs[1][:], xsv[:, :, 256:512], [], ['xs1'])
                    evc = [0]

                    def stageN(t16):
                        p = t16 % 2
                        sl = slice(256 * t16, 256 * t16 + 256)
                        act(sq[:], xs[p][:], AF.Square, ['xs%d' % p], ['sq'])
                        mms([(ps[:, p, 0:256], ones, sq[:, kc, :], kc == 0, kc == KC - 1) for kc in range(KC)], ['sq', 'tri'], ['ps%d' % p])
                        rstd_from_ps(ps[:, p, 0:256], rs[p][:], D, ['ps%d' % p], ['rs%d' % p])
                        for kc in range(KC):
                            stt(hb[p][:, kc, :], xs[p][:, kc, :], gn[:, 0, kc:kc + 1], rs[p][:], ALU.mult, ALU.mult,
                                ['xs%d' % p, 'rs%d' % p, 'gn'], ['hb%d_%d' % (p, kc)])
                        if t16 + 2 < 16:
                            dma('sp', xs[p][:], xsv[:, :, 256 * (t16 + 2):256 * (t16 + 2) + 256], [], ['xs%d' % p])
                        dma('sp', hsv[:, :, sl], hb[p][:], ['hb%d_%d' % (p, kc) for kc in range(KC)], ['hs16_%d' % t16])

                    def stageP(t16):
                        p = t16 % 2
                        sl = slice(256 * t16, 256 * t16 + 256)
                        hbk = ['hb%d_%d' % (p, kc) for kc in range(KC)]
                        for g in range(4):
                            ev = evc[0]
                            bk = 2 + ev % 6
                            mms([(ps[:, bk, 0:256], wA_t[:, kc, 128 * g:128 * g + 128], hb[p][:, kc, :], kc == 0, kc == KC - 1) for kc in range(KC)],
                                hbk + wkA, ['ps%d' % bk])
                            cp(QK[:, g, sl], ps[:, bk, 0:256], ['ps%d' % bk], ['QK%d_%d' % (g, t16)], eng=('act' if ev % 2 else 'dve'))
                            evc[0] += 1
                        for sub in range(2):
                            ev = evc[0]
                            bk = 2 + ev % 6
                            mms([(ps[:, bk, 0:256], hb[p][:, kc, 128 * sub:128 * sub + 128], wA_t[:, kc, 512:768], kc == 0, kc == KC - 1) for kc in range(KC)],
                                hbk + wkA, ['ps%d' % bk])
                            cp(V[:, 2 * t16 + sub, :], ps[:, bk, 0:256], ['ps%d' % bk], ['V%d' % (2 * t16 + sub)], eng=('act' if ev % 2 else 'dve'))
                            evc[0] += 1

                    stageN(0)
                    for t16 in range(16):
                        if t16 + 1 < 16:
                            stageN(t16 + 1)
                        stageP(t16)
                    S.barrier()
                if 'QKa' in dbg_out:
                    dma('sp', dbg_out['QKa'], QK[:], [], ['dbg1'])
                    dma('sp', dbg_out['Va'], V[:], [], ['dbg2'])
                if upto('attA'):
                    with ExitStack() as es2:
                        Eb = [[sb(es2, "Eb%d%d" % (h, q), [128, 512], F32) for q in range(2)] for h in range(2)]
                        SPf = [[sb(es2, "SPf%d%d" % (h, q), [128, 512], F32) for q in range(2)] for h in range(2)]
                        SPb = [[sb(es2, "SPb%d%d" % (h, q), [128, 512], BF16) for q in range(2)] for h in range(2)]
                        T1 = [[sb(es2, "T1%d%d" % (h, q), [128, 512], F32) for q in range(2)] for h in range(2)]
                        Wb = [[sb(es2, "Wb%d%d" % (h, q), [128, 512], BF16) for q in range(2)] for h in range(2)]
                        blocks = [(qt, bi, i) for qt in range(8) for bi, i in enumerate(range(4 * qt + 3, -1, -1))]

                        def A1(n):
                            qt, bi, i = blocks[n]
                            q = n % 2
                            qsl = slice(512 * qt, 512 * qt + 512)
                            ksl = slice(128 * i, 128 * i + 128)
                            for h in range(2):
                                zb = 2 * q + h
                                mms([(ps[:, zb, :], QK[:, 2 * h + 1, ksl], QK[:, 2 * h, qsl], True, True)], [], ['ps%d' % zb])
                            for h in range(2):
                                zb = 2 * q + h
                                kq = '%d%d' % (h, q)
                                act(Eb[h][q][:], ps[:, zb, :], AF.Exp, ['ps%d' % zb], ['Eb' + kq], scale=scl)
                                act(SPf[h][q][:], Eb[h][q][:], AF.Ln, ['Eb' + kq], ['SPf' + kq], bias=1.0)

                        def A2(n):
                            qt, bi, i = blocks[n]
                            q = n % 2
                            o = 128 * i - 512 * qt
                            diag = o >= 0
                            first = bi == 0
                            for h in range(2):
                                kq = '%d%d' % (h, q)
                                if diag:
                                    tt(SPb[h][q][:], SPf[h][q][:], masks[:, o // 128, :], ALU.mult, ['SPf' + kq, 'masks'], ['SPb' + kq])
                                else:
                                    cp(SPb[h][q][:], SPf[h][q][:], ['SPf' + kq], ['SPb' + kq])
                            for h in range(2):
                                kq = '%d%d' % (h, q)
                                mms([(ps[:, 4 + h, :], Umat, SPb[h][q][:], first, True)], ['SPb' + kq, 'tri'], ['ps%d' % (4 + h)])
                            for h in range(2):
                                zb = 2 * q + h
                                kq = '%d%d' % (h, q)
                                stt(T1[h][q][:], ps[:, zb, :], scl, SPf[h][q][:], ALU.mult, ALU.subtract, ['ps%d' % zb, 'SPf' + kq], ['T1' + kq])
                            for h in range(2):
                                kq = '%d%d' % (h, q)
                                tt(T1[h][q][:], T1[h][q][:], ps[:, 4 + h, :], ALU.subtract, ['T1' + kq, 'ps%d' % (4 + h)], ['T1' + kq])
                                mms([(ps[:, 4 + h, :], Lmat, SPb[h][q][:], False, True)], ['SPb' + kq, 'tri'], ['ps%d' % (4 + h)])

                        def A3(n):
                            qt, bi, i = blocks[n]
                            q = n % 2
                            qsl = slice(512 * qt, 512 * qt + 512)
                            o = 128 * i - 512 * qt
                            diag = o >= 0
                            first = bi == 0
                            last = i == 0
                            for h in range(2):
                                kq = '%d%d' % (h, q)
                                act(Wb[h][q][:], T1[h][q][:], AF.Exp, ['T1' + kq], ['Wb' + kq])
                                if diag:
                                    tt(Wb[h][q][:], Wb[h][q][:], masks[:, o // 128, :], ALU.mult, ['Wb' + kq, 'masks'], ['Wb' + kq])
                            for h in range(2):
                                kq = '%d%d' % (h, q)
                                mms([(ps[:, 6 + h, :], V[:, i, 128 * h:128 * h + 128], Wb[h][q][:], first, last)], ['Wb' + kq], ['ps%d' % (6 + h)])
                            if last:
                                for h in range(2):
                                    cp(oT_all[:, h, qsl], ps[:, 6 + h, :], ['ps%d' % (6 + h)], ['oT%d_%d' % (h, qt)], eng='act')

                        NBk = len(blocks)
                        for n in range(NBk + 2):
                            if n < NBk:
                                A1(n)
                            if 0 <= n - 1 < NBk:
                                A2(n - 1)
                            if 0 <= n - 2 < NBk:
                                A3(n - 2)
                        S.barrier()
        if 'oT' in dbg_out and not upto('attB'):
            dma('sp', dbg_out['oT'], oT_all[:], ['oT%d_%d' % (h, qt) for h in range(2) for qt in range(8)], ['dbg3'])

        if upto('attB'):
            with ExitStack() as es:
                QK = sb(es, "QKb", [128, 4, SEQ], BF16)
                V = sb(es, "Vb", [128, 32, 256], BF16)
                with ExitStack() as es2:
                    in_proj(es2, wB_t, QK, V, "B")
                    S.barrier()
                with ExitStack() as es2:
                    biasM = sb(es2, "biasM", [128, 1024], F32)
                    lamv = sb(es2, "lamv", [128, 512], F32)
                    lt = sb(es2, "lt", [128, 8], F32)
                    dma('sp', biasM[:], biasM_d, [], ['biasM'])
                    dma('sp', lamv[:], lamv_d, [], ['lamv'])
                    lam_init = 0.8 - 0.6 * math.exp(-0.3 * 0)
                    lp = sb(es2, "lp", [128, 256], F32)
                    tt(lp[:, 0:128], lamv[:, 0:128], lamv[:, 128:256], ALU.mult, ['lamv'], ['lp0'])
                    tt(lp[:, 128:256], lamv[:, 256:384], lamv[:, 384:512], ALU.mult, ['lamv'], ['lp1'])
                    S.add('dve', lambda e: e.reduce_sum(out=lt[:, 0:1], in_=lp[:, 0:128], axis=mybir.AxisListType.X), ['lp0'], ['lt0'])
                    S.add('dve', lambda e: e.reduce_sum(out=lt[:, 1:2], in_=lp[:, 128:256], axis=mybir.AxisListType.X), ['lp1'], ['lt1'])
                    act(lt[:, 2:4], lt[:, 0:2], AF.Exp, ['lt0', 'lt1'], ['lt23'])
                    tt(lt[:, 4:5], lt[:, 3:4], lt[:, 2:3], ALU.subtract, ['lt23'], ['lt4'])
                    ts(lt[:, 4:5], lt[:, 4:5], -lam_init, None, ALU.add, None, ['lt4'], ['lt4'])
                    neglam = lt[:, 4:5]
                    ts(lt[:, 5:7], smallB[:, 1:3], 1.0 - lam_init, None, ALU.mult, None, ['smallB'], ['lt56'])
                    gsub = lt[:, 5:7]
                    bconst = smallB[:, 0:1]
                    Tn = [sb(es2, "Tn%d%d" % (c, q), [128, 512], F32) for c in range(2) for q in range(2)]
                    Ebf = [sb(es2, "Ebf%d%d" % (c, q), [128, 512], BF16) for c in range(2) for q in range(2)]
                    Esum = [sb(es2, "Esum%d" % c, [128, 512], F32) for c in range(2)]
                    Esb = [sb(es2, "Esb%d" % c, [128, 512], BF16) for c in range(2)]
                    rD = [sb(es2, "rD%d" % c, [128, 512], F32) for c in range(2)]
                    oh = [sb(es2, "oh%d" % c, [128, 512], F32) for c in range(2)]
                    tA = sb(es2, "tA", [128, 512], F32)
                    sqh = [sb(es2, "sqh%d" % c, [128, 512], BF16) for c in range(2)]
                    rsb = sb(es2, "rsb", [128, 512], F32)
                    blocksB = []
                    for qt in range(8):
                        allb = list(range(4 * qt + 3, -1, -1))
                        nearb = [i for i in allb if 128 * i - 512 * qt >= -128]
                        farb = [i for i in allb if 128 * i - 512 * qt < -128]
                        order = []
                        while nearb or farb:
                            if farb:
                                order.append(farb.pop(0))
                            if nearb:
                                order.append(nearb.pop(0))
                        for pos, i in enumerate(order):
                            blocksB.append((qt, pos, i, len(order)))

                    def B1(n):
                        qt, bi, i, nbq = blocksB[n]
                        q = n % 2
                        qsl = slice(512 * qt, 512 * qt + 512)
                        o = 128 * i - 512 * qt
                        near = o >= -128
                        diag = o >= 0
                        ksl = slice(128 * i, 128 * i + 128)
                        for c in range(2):
                            zb = 2 * q + c
                            mms([(ps[:, zb, :], QK[:, 2 + c, ksl], QK[:, c, qsl], True, True)], [], ['ps%d' % zb])
                        for c in range(2):
                            zb = 2 * q + c
                            kq = '%d%d' % (c, q)
                            E = Ebf[2 * c + q]
                            if near:
                                T = Tn[2 * c + q]
                                u0 = 384 - o
                                stt(T[:], ps[:, zb, :], scl, biasM[:, u0:u0 + 512], ALU.mult, ALU.add, ['ps%d' % zb, 'biasM'], ['Tn' + kq])
                                act(E[:], T[:], AF.Exp, ['Tn' + kq], ['Ebf' + kq])
                                if diag:
                                    tt(E[:], E[:], masks[:, 4 + o // 128, :], ALU.mult, ['Ebf' + kq, 'masks'], ['Ebf' + kq])
                            else:
                                act(E[:], ps[:, zb, :], AF.Exp, ['ps%d' % zb, 'smallB'], ['Ebf' + kq], scale=scl, bias=bconst)

                    def B2(n):
                        qt, bi, i, nbq = blocksB[n]
                        q = n % 2
                        qsl = slice(512 * qt, 512 * qt + 512)
                        first = bi == 0
                        last = bi == nbq - 1
                        for c in range(2):
                            kq = '%d%d' % (c, q)
                            E = Ebf[2 * c + q]
                            mms([(ps[:, 4 + 2 * c + hf, :], V[:, i, 128 * hf:128 * hf + 128], E[:], first, last) for hf in range(2)],
                                ['Ebf' + kq], ['ps%d' % (4 + 2 * c), 'ps%d' % (5 + 2 * c)])
                            eg = 'dve'
                            if first:
                                cp(Esum[c][:], E[:], ['Ebf' + kq], ['Esum%d' % c], eng=eg)
                            else:
                                tt(Esum[c][:], Esum[c][:], E[:], ALU.add, ['Ebf' + kq, 'Esum%d' % c], ['Esum%d' % c], eng=eg)
                        if last:
                            for c in range(2):
                                cp(Esb[c][:], Esum[c][:], ['Esum%d' % c], ['Esb%d' % c])
                                mms([(ps[:, c, :], ones, Esb[c][:], True, True)], ['Esb%d' % c, 'tri'], ['ps%d' % c])
                                S.add('dve', lambda e, c=c: e.reciprocal(out=rD[c][:], in_=ps[:, c, :]), ['ps%d' % c], ['rD%d' % c])
                            for hf in range(2):
                                tt(tA[:], ps[:, 4 + hf, :], rD[0][:], ALU.mult, ['ps%d' % (4 + hf), 'rD0'], ['tA'])
                                tt(oh[hf][:], ps[:, 6 + hf, :], rD[1][:], ALU.mult, ['ps%d' % (6 + hf), 'rD1'], ['oh%d' % hf])
                                stt(oh[hf][:], oh[hf][:], neglam, tA[:], ALU.mult, ALU.add, ['oh%d' % hf, 'tA', 'lt4'], ['oh%d' % hf])
                                act(sqh[hf][:], oh[hf][:], AF.Square, ['oh%d' % hf], ['sqh%d' % hf])
                            mms([(ps[:, 2, :], ones, sqh[hf][:], hf == 0, hf == 1) for hf in range(2)], ['sqh0', 'sqh1', 'tri'], ['ps2'])
                            rstd_from_ps(ps[:, 2, :], rsb[:], 256, ['ps2'], ['rsb'])
                            for hf in range(2):
                                stt(oT_all[:, 2 + hf, qsl], oh[hf][:], gsub[:, hf:hf + 1], rsb[:], ALU.mult, ALU.mult,
                                    ['oh%d' % hf, 'rsb', 'lt56'], ['oT%d_%d' % (2 + hf, qt)])
                            if qt % 2 == 1 and upto('ag1'):
                                ag_slab(qt // 2, cin1s, cout1a, oT_all, 4, 'c1')

                    for n in range(len(blocksB) + 1):
                        if n < len(blocksB):
                            B1(n)
                        if n > 0:
                            B2(n - 1)
                    S.barrier()
        if 'oT' in dbg_out and upto('attB'):
            dma('sp', dbg_out['oT'], oT_all[:], ['oT%d_%d' % (h, qt) for h in range(4) for qt in range(8)], ['dbg3'])

        mix.close()
        wst.close()
        S.barrier()

        hpk = sb(glob, "hpk", [128, 48], F32)
        hh = sb(glob, "hh", [128, 48], F32)
        odT = sb(glob, "odT", [128, 8, TOK], BF16)
        tok = ExitStack()
        xT = sb(tok, "xT", [128, KC, TH], F32)
        aT = sb(tok, "aT", [128, KC, TH], BF16)
        rs1 = sb(tok, "rs1", [128, TH], F32)

        def out_proj(w_d, lname):
            with ExitStack() as es:
                wo = [sb(es, "wo%d" % i, [128, KC, 256], BF16) for i in range(3)]
                wv = w_d.rearrange("(kc p) n -> p kc n", p=128)
                n = 0
                for c2 in range(8):
                    wb = c2 % 3
                    dma('pool', wo[wb][:], wv[:, :, 256 * c2:256 * c2 + 256], [], ['wo%d' % wb])
                    for half in range(2):
                        cc = 2 * c2 + half
                        for (a, b) in TT3:
                            bk = n % 8
                            n += 1
                            mms([(ps[:, bk, 0:b - a], wo[wb][:, kc, 128 * half:128 * half + 128], aT[:, kc, a:b], kc == 0, kc == KC - 1) for kc in range(KC)],
                                ['wo%d' % wb] + ['aT%d' % kc for kc in range(KC)], ['ps%d' % bk])
                            tt(xT[:, cc, a:b], xT[:, cc, a:b], ps[:, bk, 0:b - a], ALU.add, ['ps%d' % bk, 'xT%d' % cc], ['xT%d' % cc])
                S.barrier()

        def norm_to_aT(gi, t0, t1):
            with ExitStack() as es:
                sq = sb(es, "sqn", [128, KC, 342], BF16)
                tiles = [(a, min(a + 342, t1)) for a in range(t0, t1, 342)]
                for ti, (a, b) in enumerate(tiles):
                    bk = ti % 8
                    act(sq[:, :, 0:b - a], xT[:, :, a:b], AF.Square, ['xT%d' % kc for kc in range(KC)], ['sqn'])
                    mms([(ps[:, bk, 0:b - a], ones, sq[:, kc, 0:b - a], kc == 0, kc == KC - 1) for kc in range(KC)], ['sqn', 'tri'], ['ps%d' % bk])
                    rstd_from_ps(ps[:, bk, 0:b - a], rs1[:, a:b], D, ['ps%d' % bk], ['rs1_%d' % ti])
                    for kc in range(KC):
                        stt(aT[:, kc, a:b], xT[:, kc, a:b], gn[:, gi, kc:kc + 1], rs1[:, a:b], ALU.mult, ALU.mult,
                            ['xT%d' % kc, 'rs1_%d' % ti, 'gn'], ['aT%d' % kc])
                S.barrier()

        def conv_ffn(l):
            GK = 4
            with ExitStack() as es:
                wu = [sb(es, "wu%d" % i, [128, KC, 256], BF16) for i in range(3)]
                cpar = sb(es, "cpar", [128, NCH, 8], F32)
                Ur = [sb(es, "Ur%d" % i, [128, TH], F32) for i in range(2)]
                Cc = [sb(es, "Cc%d" % i, [128, TOK], F32) for i in range(2)]
                gat = [sb(es, "gat%d" % i, [128, GK, TOK], BF16) for i in range(2)]
                wd = [sb(es, "wd%d" % i, [128, GK, 256], BF16) for i in range(3)]
                dma('sp', cpar[:], convp[l], [], ['cpar'])
                wdv = w_down[l].rearrange("(c p) n -> p c n", p=128)
                aTk = ['aT%d' % kc for kc in range(KC)]
                nw = 0
                nd = 0
                for kg in range(NCH // GK):
                    gp = kg % 2
                    for ci in range(GK):
                        c = kg * GK + ci
                        wb = nw % 3
                        nw += 1
                        dma('pool', wu[wb][:], w_up[l][:, c, :].rearrange("(kc p) n -> p kc n", p=128), [], ['wu%d' % wb])
                        for ag in range(2):
                            b0 = 3 * ag
                            for ti, (a, b) in enumerate(TT3):
                                mms([(ps[:, b0 + ti, 0:b - a], wu[wb][:, kc, 128 * ag:128 * ag + 128], aT[:, kc, a:b], kc == 0, kc == KC - 1) for kc in range(KC)],
                                    ['wu%d' % wb] + aTk, ['ps%d' % (b0 + ti)])
                            pk = ['ps%d' % (b0 + ti) for ti in range(3)]
                            act(Ur[ag][:].rearrange("p (a b) -> p a b", a=3), ps[:, b0:b0 + 3, 0:342], AF.Copy, pk, ['Ur%d' % ag])
                            pb = 4 * ag
                            ts(Cc[ag][:], Ur[ag][:, 2:TH], cpar[:, c, pb + 2:pb + 3], cpar[:, c, pb + 3:pb + 4], ALU.mult, ALU.add,
                               ['Ur%d' % ag, 'cpar'], ['Cc%d' % ag])
                            stt(Cc[ag][:], Ur[ag][:, 1:TH - 1], cpar[:, c, pb + 1:pb + 2], Cc[ag][:], ALU.mult, ALU.add, ['Ur%d' % ag, 'Cc%d' % ag, 'cpar'], ['Cc%d' % ag])
                            stt(Cc[ag][:], Ur[ag][:, 0:TH - 2], cpar[:, c, pb + 0:pb + 1], Cc[ag][:], ALU.mult, ALU.add, ['Ur%d' % ag, 'Cc%d' % ag, 'cpar'], ['Cc%d' % ag])
                        act(Cc[1][:], Cc[1][:], AF.Silu, ['Cc1'], ['Cc1'])
                        tt(gat[gp][:, ci, :], Cc[0][:], Cc[1][:], ALU.mult, ['Cc0', 'Cc1'], ['gat%d_%d' % (gp, ci)])
                    gk = ['gat%d_%d' % (gp, ci) for ci in range(GK)]
                    for c2 in range(8):
                        wb = nd % 3
                        nd += 1
                        dma('pool', wd[wb][:], wdv[:, kg * GK:kg * GK + GK, 256 * c2:256 * c2 + 256], [], ['wd%d' % wb])
                        for half in range(2):
                            cc = 2 * c2 + half
                            for th in range(2):
                                bk = 6 + th
                                mms([(ps[:, bk, :], wd[wb][:, ci, 128 * half:128 * half + 128], gat[gp][:, ci, 512 * th:512 * th + 512], ci == 0, ci == GK - 1) for ci in range(GK)],
                                    ['wd%d' % wb] + gk, ['ps%d' % bk])
                                sl = slice(2 + 512 * th, 2 + 512 * th + 512)
                                tt(xT[:, cc, sl], xT[:, cc, sl], ps[:, bk, :], ALU.add, ['ps%d' % bk, 'xT%d' % cc], ['xT%d' % cc])
                S.barrier()

        xTk = ['xT%d' % kc for kc in range(KC)]
        aTk = ['aT%d' % kc for kc in range(KC)]
        if upto('out0'):
            dma('sp', xT[:], xT_own.rearrange("(kc p) t -> p kc t", p=128), [], xTk)
            cv = cout1a.rearrange("(sk p) t -> p sk t", p=128)
            c1k = ['c1out%d' % t for t in range(4)]
            dma('sp', aT[:, :, 2:TH], cv[:, bass.ds(rank * 16, 16), :], c1k, aTk)
            dma('sp', aT[:, :, 0:2], cv[:, bass.ds(((rank + 3) % 4) * 16, 16), TOK - 2:TOK], c1k, ['aTh'])
            ts(aT[:, :, 0:2], aT[:, :, 0:2], smallB[:, 3:4], None, ALU.mult, None, ['aTh', 'smallB'] + aTk, aTk)
            out_proj(w_out0, 'o0')
            if 'xmid0' in dbg_out:
                dma('sp', dbg_out['xmid0'].rearrange("(kc p) t -> p kc t", p=128), xT[:], xTk, ['dbg4'])
        if upto('norm0'):
            norm_to_aT(2, 0, TH)
        if upto('ffn0'):
            conv_ffn(0)
            if 'x1' in dbg_out:
                dma('sp', dbg_out['x1'].rearrange("(kc p) t -> p kc t", p=128), xT[:], xTk, ['dbg5'])


        def final_out(normed):
            with ExitStack() as es:
                yo = sb(es, "yo", [128, KC, TOK], F32)
                if normed:
                    sq = sb(es, "sqf", [128, KC, 512], BF16)
                    for th in range(2):
                        sl = slice(2 + 512 * th, 2 + 512 * th + 512)
                        act(sq[:], xT[:, :, sl], AF.Square, xTk, ['sqf'])
                        mms([(ps[:, th, :], ones, sq[:, kc, :], kc == 0, kc == KC - 1) for kc in range(KC)], ['sqf', 'tri'], ['ps%d' % th])
                        rstd_from_ps(ps[:, th, :], rs1[:, sl], D, ['ps%d' % th], ['rsf%d' % th])
                        for kc in range(KC):
                            stt(yo[:, kc, 512 * th:512 * th + 512], xT[:, kc, sl], gn[:, 4, kc:kc + 1], rs1[:, sl], ALU.mult, ALU.mult,
                                ['xT%d' % kc, 'rsf%d' % th, 'gn'], ['yo%d_%d' % (kc, th)])
                    yk = ['yo%d_%d' % (kc, th) for kc in range(KC) for th in range(2)]
                else:
                    for kc in range(KC):
                        cp(yo[:, kc, :], xT[:, kc, 2:TH], ['xT%d' % kc], ['yo%d' % kc], eng=('act' if kc % 2 else 'dve'))
                    yk = ['yo%d' % kc for kc in range(KC)]
                dma('sp', yT.rearrange("(kc p) t -> p kc t", p=128), yo[:], yk, ['yT'])
                S.barrier()

        if not upto('ag2'):
            final_out(False)
            tok.close()
            S.barrier()
        else:
            norm_to_aT(1, 2, TH)
            dma('sp', x1s.rearrange("(kc p) t -> p kc t", p=128), xT[:, :, 2:TH], xTk, ['x1s'])
            cp(hpk[:, 0:32].rearrange("p (k t) -> p k t", t=2), xT[:, :, TH - 2:TH], xTk, ['hpk0'])
            for q in range(4):
                th_, h_ = divmod(q, 2)
                dma('sp', cin2[q].rearrange("(k p) t -> p k t", p=128), aT[:, 8 * h_:8 * h_ + 8, 2 + 512 * th_:2 + 512 * th_ + 512], aTk, ['c2in%d' % q])
                S.add('pool', lambda e, q=q: e.collective_compute("AllGather", ALU.bypass, replica_groups=RG, ins=[cin2[q].opt()], outs=[cout2[q].opt()]),
                      ['c2in%d' % q], ['c2out%d' % q], dma='cc')
            S.barrier()
            tok.close()
            S.barrier()

            if upto('sgu'):
                with ExitStack() as es:
                    hown = sb(es, "hown", [128, KC, TOK], BF16)
                    for q in range(4):
                        th_, h_ = divmod(q, 2)
                        dma('sp', hown[:, 8 * h_:8 * h_ + 8, 512 * th_:512 * th_ + 512], cin2[q].rearrange("(k p) t -> p k t", p=128), ['c2in%d' % q], ['hown%d' % q])
                    hk = ['hown%d' % q for q in range(4)]
                    lng = sb(es, "lng", [128, 2, 1024], F32)
                    WT = sb(es, "WTs", [128, 4, 128], BF16)
                    bsb = sb(es, "bsb", [128, 4, 512], F32)
                    sgm = sb(es, "sgm", [128, 128], F32)
                    with ExitStack() as es2:
                        WTf = sb(es2, "WTf", [128, 4, 128], F32)
                        dma('sp', WTf[:], sguw_d, [], ['WTf'])
                        dma('sp', sgm[:], sgum_d, [], ['sgm'])
                        dma('sp', lng[:], lngb_d, [], ['lng'])
                        dma('sp', bsb[:], sgub_d, [], ['bsb'])
                        for g in range(4):
                            tt(WT[:, g, :], WTf[:, g, :], sgm[:], ALU.mult, ['WTf', 'sgm'], ['WT%d' % g])
                        S.barrier()
                    wzv = w_zd.rearrange("(kc p) n -> p kc n", p=128)
                    uT = sb(es, "uT", [128, 8, TOK], BF16)
                    vn = sb(es, "vn", [128, 8, 1024], BF16)
                    g1 = [sb(es, "g1_%d" % i, [128, 512], F32) for i in range(2)]
                    g2 = [sb(es, "g2_%d" % i, [128, 512], F32) for i in range(2)]
                    GC = 2.0 * math.sqrt(2.0 / math.pi)
                    gcount = [0]

                    def gelu_from_ps(bank, out_ap, pk, wk):
                        i = gcount[0] % 2
                        gcount[0] += 1
                        act(g1[i][:], bank, AF.Square, [pk], ['g1_%d' % i])
                        ts(g1[i][:], g1[i][:], 0.044715, 1.0, ALU.mult, ALU.add, ['g1_%d' % i], ['g1_%d' % i])
                        tt(g1[i][:], g1[i][:], bank, ALU.mult, ['g1_%d' % i, pk], ['g1_%d' % i])
                        act(g2[i][:], g1[i][:], AF.Sigmoid, ['g1_%d' % i], ['g2_%d' % i], scale=GC)
                        tt(out_ap, g2[i][:], bank, ALU.mult, ['g2_%d' % i, pk], [wk])
                    with ExitStack() as es2:
                        wz = [sb(es2, "wz%d" % i, [128, KC, 256], BF16) for i in range(3)]
                        n = 0
                        for c2 in range(4):
                            wb = c2 % 3
                            dma('pool', wz[wb][:], wzv[:, :, 256 * c2:256 * c2 + 256], [], ['wz%d' % wb])
                            for half in range(2):
                                ch = 2 * c2 + half
                                for th in range(2):
                                    bk = n % 4
                                    n += 1
                                    mms([(ps[:, bk, :], wz[wb][:, kc, 128 * half:128 * half + 128], hown[:, kc, 512 * th:512 * th + 512], kc == 0, kc == KC - 1) for kc in range(KC)],
                                        ['wz%d' % wb] + hk, ['ps%d' % bk])
                                    gelu_from_ps(ps[:, bk, :], uT[:, ch, 512 * th:512 * th + 512], 'ps%d' % bk, 'uT%d_%d' % (ch, th))
                        S.barrier()
                    with ExitStack() as es2:
                        wv2 = [sb(es2, "wv2_%d" % i, [128, KC, 512], BF16) for i in range(2)]
                        for i in range(2):
                            for hhf in range(2):
                                dma('pool', wv2[i][:, :, 256 * hhf:256 * hhf + 256], wzv[:, :, 1024 + 512 * i + 256 * hhf:1024 + 512 * i + 256 * hhf + 256], [], ['wv2_%d_%d' % (i, hhf)])
                        wvk = ['wv2_%d_%d' % (i, hhf) for i in range(2) for hhf in range(2)]
                        vg = [sb(es2, "vg%d" % i, [128, 1024], F32) for i in range(2)]
                        st = sb(es2, "lnst", [128, 16], F32)
                        junk = sb(es2, "lnjunk", [128, 1024], BF16)
                        n = 0
                        for t8 in range(8):
                            p = t8 % 2
                            for i in range(2):
                                bk = 4 + n % 4
                                n += 1
                                mms([(ps[:, bk, :], hown[:, kc, 128 * t8:128 * t8 + 128], wv2[i][:, kc, :], kc == 0, kc == KC - 1) for kc in range(KC)],
                                    wvk + hk, ['ps%d' % bk])
                                gelu_from_ps(ps[:, bk, :], vg[p][:, 512 * i:512 * i + 512], 'ps%d' % bk, 'vg%d_%d' % (p, i))
                            vk = ['vg%d_0' % p, 'vg%d_1' % p]
                            c0 = 8 * p
                            S.add('dve', lambda e, p=p, c0=c0: e.reduce_sum(out=st[:, c0:c0 + 1], in_=vg[p][:], axis=mybir.AxisListType.X), vk, ['st%d_0' % p])
                            act(junk[:], vg[p][:], AF.Square, vk, ['junk', 'st%d_1' % p], accum_out=st[:, c0 + 1:c0 + 2])
                            ts(st[:, c0 + 2:c0 + 3], st[:, c0:c0 + 1], 1.0 / 1024, None, ALU.mult, None, ['st%d_0' % p], ['st%d_2' % p])
                            tt(st[:, c0 + 3:c0 + 4], st[:, c0 + 2:c0 + 3], st[:, c0 + 2:c0 + 3], ALU.mult, ['st%d_2' % p], ['st%d_3' % p])
                            stt(st[:, c0 + 4:c0 + 5], st[:, c0 + 1:c0 + 2], 1.0 / 1024, st[:, c0 + 3:c0 + 4], ALU.mult, ALU.subtract, ['st%d_1' % p, 'st%d_3' % p], ['st%d_4' % p])
                            act(st[:, c0 + 5:c0 + 6], st[:, c0 + 4:c0 + 5], AF.Ln, ['st%d_4' % p], ['st%d_5' % p], bias=eps_t[:, 0:1])
                            act(st[:, c0 + 5:c0 + 6], st[:, c0 + 5:c0 + 6], AF.Exp, ['st%d_5' % p], ['st%d_5' % p], scale=-0.5)
                            ts(vg[p][:], vg[p][:], st[:, c0 + 2:c0 + 3], st[:, c0 + 5:c0 + 6], ALU.subtract, ALU.mult, vk + ['st%d_2' % p, 'st%d_5' % p], ['vgn%d' % p])
                            tt(vg[p][:], vg[p][:], lng[:, 0, :], ALU.mult, ['vgn%d' % p, 'lng'], ['vgn%d' % p])
                            tt(vn[:, t8, :], vg[p][:], lng[:, 1, :], ALU.add, ['vgn%d' % p, 'lng'], ['vn%d' % t8] + vk)
                        S.barrier()
                    n = 0
                    for ch in range(8):
                        g = ch // 2
                        for th in range(2):
                            bk = n % 4
                            n += 1
                            for t4 in range(4):
                                t8 = 4 * th + t4
                                mms([(ps[:, bk, 128 * t4:128 * t4 + 128], vn[:, t8, 128 * ch:128 * ch + 128], WT[:, g, :], True, True)],
                                    ['vn%d' % t8, 'WT%d' % g], ['ps%d_%d' % (bk, t4)])
                            i = n % 2
                            pk4 = ['ps%d_%d' % (bk, t4) for t4 in range(4)]
                            tt(g1[i][:], ps[:, bk, :], bsb[:, g, :], ALU.add, pk4 + ['bsb'], ['g1_%d' % i])
                            tt(odT[:, ch, 512 * th:512 * th + 512], g1[i][:], uT[:, ch, 512 * th:512 * th + 512], ALU.mult, ['g1_%d' % i, 'uT%d_%d' % (ch, th)], ['odT%d_%d' % (ch, th)] + pk4)
                    cp(hpk[:, 32:48].rearrange("p (k t) -> p k t", t=2), odT[:, :, TOK - 2:TOK], ['odT%d_1' % ch for ch in range(8)], ['hpk1'])
                    S.barrier()
                if 'odT' in dbg_out:
                    dma('sp', dbg_out['odT'], odT[:], [], ['dbg6'])

            if upto('ret'):
                with ExitStack() as es:
                    ocT = sb(es, "ocT", [128, 2, SEQ], BF16)
                    QKr = sb(es, "QKr", [128, 2, SEQ], BF16)
                    gate = sb(es, "gate", [128, 2, SEQ], BF16)
                    Vr = sb(es, "Vr", [128, 32, 256], BF16)
                    Ktm = sb(es, "Ktm", [128, 32, 128], BF16)
                    wC = sb(es, "wC", [128, KC, 1024], BF16)
                    hin = [sb(es, "hinC%d" % i, [128, KC, 512], BF16) for i in range(2)]
                    rc = sb(es, "retc", [128, 264], F32)
                    cs = [sb(es, "cs%d" % i, [128, 2, 512], F32) for i in range(2)]
                    cstm = [sb(es, "cstm%d" % i, [128, 4, 256], F32) for i in range(2)]
                    ra = sb(es, "ra", [128, 512], F32)
                    rb = sb(es, "rb", [128, 512], F32)
                    SD = [sb(es, "SD%d" % i, [128, 128], BF16) for i in range(2)]
                    Qs = [sb(es, "Qs%d" % i, [128, 128], BF16) for i in range(2)]
                    Sf = sb(es, "Sf", [128, 256], F32)
                    Sb = [sb(es, "Sb%d" % i, [128, 256], BF16) for i in range(2)]
                    yb = sb(es, "yb", [128, 2, 512], F32)
                    sqy = sb(es, "sqy", [128, 2, 512], BF16)
                    rsy = sb(es, "rsy", [128, 512], F32)
                    wcv = w_inC.rearrange("(kc p) n -> p kc n", p=128)
                    for i in range(4):
                        dma('pool', wC[:, :, 256 * i:256 * i + 256], wcv[:, :, 256 * i:256 * i + 256], [], ['wC%d' % i])
                    wck = ['wC%d' % i for i in range(4)]
                    dma('sp', rc[:], retc_d, [], ['retc'])
                    Dblk = rc[:, 0:128]
                    qdec = rc[:, 128:256]
                    kdec = rc[:, 256:257]
                    g128 = rc[:, 257:258]
                    retg = rc[:, 258:260]
                    S.add('dve', lambda e: e.memset(Sf[:], 0.0), [], ['Sf'])
                    ra2 = sb(es, "ra2", [128, 512], F32)
                    rk = [sb(es, "rk%d" % i, [128, 128], F32) for i in range(2)]

                    def loads(t8):
                        p = t8 % 2
                        r = t8 // 2
                        csl = slice(512 * (t8 % 2), 512 * (t8 % 2) + 512)
                        sl = slice(512 * t8, 512 * t8 + 512)
                        for q in range(2):
                            qq = 2 * (t8 % 2) + q
                            dma('sp', hin[p][:, 8 * q:8 * q + 8, :], cout2[qq][1024 * r:1024 * r + 1024, :].rearrange("(k p) t -> p k t", p=128), ['c2out%d' % qq], ['hinC%d_%d' % (p, q)])
                        dma('sp', cs[p][:, 0, :], cosT_d[:, sl], [], ['cs%d_0' % p])
                        dma('sp', cs[p][:, 1, :], sinT_d[:, sl], [], ['cs%d_1' % p])
                        dma('sp', cstm[p][:], cstm_d[:, 4 * t8:4 * t8 + 4, :], [], ['cstm%d' % p])

                    def inproj_groups(t8):
                        p = t8 % 2
                        sl = slice(512 * t8, 512 * t8 + 512)
                        hk = ['hinC%d_%d' % (p, q) for q in range(2)]
                        G = []

                        def g_qk(qk):
                            b0 = 2 * qk
                            for w in range(2):
                                g = 2 * qk + w
                                mms([(ps[:, b0 + w, :], wC[:, kc, 128 * g:128 * g + 128], hin[p][:, kc, :], kc == 0, kc == KC - 1) for kc in range(KC)], wck + hk, ['ps%d' % (b0 + w)])
                            tt(ra[:], ps[:, b0, :], cs[p][:, 0, :], ALU.mult, ['ps%d' % b0, 'cs%d_0' % p], ['ra'])
                            tt(rb[:], ps[:, b0 + 1, :], cs[p][:, 1, :], ALU.mult, ['ps%d' % (b0 + 1), 'cs%d_1' % p], ['rb'])
                            tt(QKr[:, qk, sl], ra[:], rb[:], ALU.add, ['ra', 'rb'], ['QKr%d_%d' % (qk, t8)])

                        def g_gate(hf):
                            g = 6 + hf
                            mms([(ps[:, hf, :], wC[:, kc, 128 * g:128 * g + 128], hin[p][:, kc, :], kc == 0, kc == KC - 1) for kc in range(KC)], wck + hk, ['ps%d' % hf])
                            act(gate[:, hf, sl], ps[:, hf, :], AF.Silu, ['ps%d' % hf], ['gate%d_%d' % (hf, t8)])

                        def g_v(sub):
                            blk = 4 * t8 + sub
                            tsl = slice(128 * sub, 128 * sub + 128)
                            mms([(ps[:, 2, 0:256], hin[p][:, kc, tsl], wC[:, kc, 512:768], kc == 0, kc == KC - 1) for kc in range(KC)], wck + hk, ['ps2'])
                            cp(Vr[:, blk, :], ps[:, 2, 0:256], ['ps2'], ['Vr%d' % blk], eng='act')

                        def g_k(sub):
                            blk = 4 * t8 + sub
                            tsl = slice(128 * sub, 128 * sub + 128)
                            mms([(ps[:, 3, 0:256], hin[p][:, kc, tsl], wC[:, kc, 256:512], kc == 0, kc == KC - 1) for kc in range(KC)], wck + hk, ['ps3'])
                            tt(rk[0][:], ps[:, 3, 0:128], cstm[p][:, sub, 0:128], ALU.mult, ['ps3', 'cstm%d' % p], ['rk0'])
                            tt(rk[1][:], ps[:, 3, 128:256], cstm[p][:, sub, 128:256], ALU.mult, ['ps3', 'cstm%d' % p], ['rk1'])
                            tt(rk[0][:], rk[0][:], rk[1][:], ALU.add, ['rk0', 'rk1'], ['rk0'])
                            ts(Ktm[:, blk, :], rk[0][:], kdec, None, ALU.mult, None, ['rk0', 'retc'], ['Ktm%d' % blk])
                        G.append(lambda: g_qk(0))
                        G.append(lambda: g_qk(1))
                        G.append(lambda: g_gate(0))
                        G.append(lambda: g_gate(1))
                        for sub in range(4):
                            G.append(lambda sub=sub: g_v(sub))
                            G.append(lambda sub=sub: g_k(sub))
                        return G

                    def rec_steps(t8):
                        R = []
                        for sub in range(4):
                            blk = 4 * t8 + sub
                            bsl = slice(128 * blk, 128 * blk + 128)
                            pb = blk % 2

                            def r_a(blk=blk, bsl=bsl, pb=pb):
                                mms([(ps[:, 4, 0:128], QKr[:, 1, bsl], QKr[:, 0, bsl], True, True)], ['QKr0_%d' % t8, 'QKr1_%d' % t8], ['ps4'])
                                tt(SD[pb][:], ps[:, 4, 0:128], Dblk, ALU.mult, ['ps4', 'retc'], ['SD%d' % pb])
                                if blk > 0:
                                    tt(Qs[pb][:], QKr[:, 0, bsl], qdec, ALU.mult, ['QKr0_%d' % t8, 'retc'], ['Qs%d' % pb])

                            def r_b(blk=blk, pb=pb, sub=sub):
                                lst = []
                                for hf in range(2):
                                    lst.append((ps[:, 5, 128 * hf:128 * hf + 128], Vr[:, blk, 128 * hf:128 * hf + 128], SD[pb][:], True, blk == 0))
                                rd = ['SD%d' % pb, 'Vr%d' % blk]
                                if blk > 0:
                                    for hf in range(2):
                                        lst.append((ps[:, 5, 128 * hf:128 * hf + 128], Sb[(blk - 1) % 2][:, 128 * hf:128 * hf + 128], Qs[pb][:], False, True))
                                    rd += ['Qs%d' % pb, 'Sb%d' % ((blk - 1) % 2)]
                                    lst = [lst[0], lst[2], lst[1], lst[3]]
                                mms(lst, rd, ['ps5'])
                                cp(yb[:, :, 128 * sub:128 * sub + 128], ps[:, 5, 0:256].rearrange("p (h t) -> p h t", h=2), ['ps5'], ['yb%d' % sub], eng='act')

                            def r_c(blk=blk):
                                mms([(ps[:, 6, 0:256], Ktm[:, blk, :], Vr[:, blk, :], True, True)], ['Ktm%d' % blk, 'Vr%d' % blk], ['ps6'])
                                stt(Sf[:], Sf[:], g128, ps[:, 6, 0:256], ALU.mult, ALU.add, ['Sf', 'ps6', 'retc'], ['Sf'])
                                cp(Sb[blk % 2][:], Sf[:], ['Sf'], ['Sb%d' % (blk % 2)])
                            R += [r_a, r_b, r_c]
                        return R

                    def norm_gate(t8):
                        sl = slice(512 * t8, 512 * t8 + 512)
                        ybk = ['yb%d' % sub for sub in range(4)]
                        act(sqy[:], yb[:], AF.Square, ybk, ['sqy'])
                        mms([(ps[:, 7, :], ones, sqy[:, hf, :], hf == 0, hf == 1) for hf in range(2)], ['sqy', 'tri'], ['ps7'])
                        rstd_from_ps(ps[:, 7, :], rsy[:], 256, ['ps7'], ['rsy'])
                        for hf in range(2):
                            stt(ra2[:], yb[:, hf, :], retg[:, hf:hf + 1], rsy[:], ALU.mult, ALU.mult, ybk + ['rsy', 'retc'], ['ra2'])
                            tt(ocT[:, hf, sl], ra2[:], gate[:, hf, sl], ALU.mult, ['ra2', 'gate%d_%d' % (hf, t8)], ['oT%d_%d' % (hf, t8)])
                        if t8 % 2 == 1 and upto('ag3'):
                            ag_slab(t8 // 2, cin3s, cout3a, ocT, 2, 'c3')

                    loads(0)
                    for t8 in range(9):
                        if t8 + 1 < 8:
                            loads(t8 + 1)
                        G = inproj_groups(t8) if t8 < 8 else []
                        R = rec_steps(t8 - 1) if t8 >= 1 else []
                        for k in range(max(len(G), len(R))):
                            if k < len(G):
                                G[k]()
                            if k < len(R):
                                R[k]()
                        if t8 >= 1:
                            norm_gate(t8 - 1)
                    if 'ocT' in dbg_out:
                        dma('sp', dbg_out['ocT'], ocT[:], ['oT%d_%d' % (hf, t8) for hf in range(2) for t8 in range(8)], ['dbg7'])
                    S.barrier()

            if upto('ag3'):
                dma('sp', cinH, hpk[:], ['hpk0', 'hpk1'], ['cinH'])
                S.add('pool', lambda e: e.collective_compute("AllGather", ALU.bypass, replica_groups=RG, ins=[cinH.opt()], outs=[coutH.opt()]),
                      ['cinH'], ['coutH'], dma='cc')
                dma('sp', hh[:], coutH[bass.ds(((rank + 3) % 4) * 128, 128), :], ['coutH'], ['hh'])
                ts(hh[:], hh[:], smallB[:, 3:4], None, ALU.mult, None, ['hh', 'smallB'], ['hh'])

            if upto('out1'):
                tok = ExitStack()
                xT = sb(tok, "xTb", [128, KC, TH], F32)
                aT = sb(tok, "aTb", [128, KC, TH], BF16)
                rs1 = sb(tok, "rs1b", [128, TH], F32)
                dma('sp', xT[:, :, 2:TH], x1s.rearrange("(kc p) t -> p kc t", p=128), ['x1s'], ['xTown'])
                cp(xT[:, :, 0:2], hh[:, 0:32].rearrange("p (k t) -> p k t", t=2), ['hh', 'xTown'], xTk)
                cv3 = cout3a.rearrange("(sk p) t -> p sk t", p=128)
                c3k = ['c3out%d' % t for t in range(4)]
                dma('sp', aT[:, 0:8, 2:TH], cv3[:, bass.ds(rank * 8, 8), :], c3k, ['aT%d' % kc for kc in range(8)])
                dma('sp', aT[:, 0:8, 0:2], cv3[:, bass.ds(((rank + 3) % 4) * 8, 8), TOK - 2:TOK], c3k, ['aTh3'])
                ts(aT[:, 0:8, 0:2], aT[:, 0:8, 0:2], smallB[:, 3:4], None, ALU.mult, None, ['aTh3', 'smallB'] + ['aT%d' % kc for kc in range(8)], ['aT%d' % kc for kc in range(8)])
                for ch in range(8):
                    cp(aT[:, 8 + ch, 2:TH], odT[:, ch, :], [], ['aT%d' % (8 + ch)], eng=('act' if ch % 2 else 'dve'))
                cp(aT[:, 8:16, 0:2], hh[:, 32:48].rearrange("p (k t) -> p k t", t=2), ['hh'] + ['aT%d' % (8 + ch) for ch in range(8)], ['aT%d' % (8 + ch) for ch in range(8)])
                S.barrier()
                out_proj(w_out1, 'o1')
                if 'xmid1' in dbg_out:
                    dma('sp', dbg_out['xmid1'].rearrange("(kc p) t -> p kc t", p=128), xT[:], xTk, ['dbg8'])
                if upto('ffn1'):
                    norm_to_aT(3, 0, TH)
                    conv_ffn(1)
                final_out(upto('final'))
                tok.close()
                S.barrier()
        S.emit()
    nc.used_inputs = used_inputs
    return nc


def _pp(v, n=KC):
    return np.ascontiguousarray(np.asarray(v, np.float32).reshape(n, 128).T)


def prep_inputs(inp):
    f = lambda k: np.asarray(inp[k], np.float32)
    x = f('x')
    cst = _consts()
    bidx = _bias_index()
    ab_in = f('ab_w_in')[0]
    ab_out = f('ab_w_out')[0]
    rel_bias = f('rel_bias')
    lam = f('diff_lambda')[0]
    subln = f('diff_subln_g')[0]
    gains = np.stack([_pp(f('norm_mix_g')[0]), _pp(f('norm_mix_g')[1]), _pp(f('norm_ffn_g')[0]), _pp(f('norm_ffn_g')[1]),
                      _pp(f('final_norm_g'))], axis=1)
    w_up_r, convp_r, w_down_r = [], [], []
    for l in range(2):
        wu = f('ffn_w_up')[l]
        w_up_r.append(np.ascontiguousarray(np.concatenate([wu[:, :DFF].reshape(D, NCH, 128), wu[:, DFF:].reshape(D, NCH, 128)], axis=2)))
        cw = f('ffn_conv_w')[l]
        cb = f('ffn_conv_b')[l]
        cp_ = np.zeros((128, NCH, 8), np.float32)
        for ag in range(2):
            for jj in range(3):
                cp_[:, :, 4 * ag + jj] = cw[jj, ag * DFF:(ag + 1) * DFF].reshape(NCH, 128).T
            cp_[:, :, 4 * ag + 3] = cb[ag * DFF:(ag + 1) * DFF].reshape(NCH, 128).T
        convp_r.append(cp_)
        w_down_r.append(np.ascontiguousarray(f('ffn_w_down')[l]))
    perm = []
    for r in range(4):
        for g in range(4):
            base = [128 * (2 * r), 128 * (2 * r + 1), 1024 + 256 * r, 1024 + 256 * r + 128][g]
            perm += list(range(base, base + 128))
    w_out0 = np.ascontiguousarray(ab_out[perm, :])
    cd_in = f('cd_w_in')[0]
    cd_out = f('cd_w_out')[0]
    sgw = f('sgu_w')[0]
    sguw = np.ascontiguousarray(sgw.transpose(2, 0, 1))
    jj = np.arange(128)[:, None]
    ii = np.arange(128)[None, :]
    sgum = ((jj // 64) <= (ii // 64)).astype(np.float32)
    lngb = np.ascontiguousarray(np.broadcast_to(np.stack([f('sgu_ln_g')[0], f('sgu_ln_b')[0]])[None], (128, 2, 1024)))
    sgub = np.ascontiguousarray(np.broadcast_to(np.tile(f('sgu_b')[0], (1, 4))[None], (128, 4, 512)))
    cos, sin = _rot_tables(None)
    cosT = np.ascontiguousarray(cos.T)
    sinT = np.ascontiguousarray(sin.T)
    cstm = np.ascontiguousarray(np.concatenate([cos, sin], -1).reshape(32, 128, 256).transpose(1, 0, 2))
    retg = f('ret_norm_g')[0]
    perm1 = list(range(2048))
    w_out1 = np.ascontiguousarray(cd_out[perm1, :])
    maps = []
    for c in range(8):
        b, j = divmod(c, 4)
        m = {}
        m['w_zd'] = np.ascontiguousarray(cd_in[:, 3072:5120])
        m['sguw'] = sguw
        m['sgum'] = sgum
        m['lngb'] = lngb
        m['sgub'] = sgub
        sw = (np.arange(128) + 64) % 128
        qcols = np.arange(128 * j, 128 * j + 128)
        kcols = 512 + qcols
        m['w_inC'] = np.ascontiguousarray(np.concatenate([cd_in[:, qcols], cd_in[:, qcols[sw]], cd_in[:, kcols], cd_in[:, kcols[sw]],
                                                          cd_in[:, 1024 + 256 * j:1024 + 256 * j + 256], cd_in[:, 2048 + 256 * j:2048 + 256 * j + 256]], axis=1))
        log_g = np.log(np.float32(1.0) - np.float32(2.0) ** np.float32(-5.0 - j)).astype(np.float32)
        sidx = np.arange(128, dtype=np.float32)
        rcst = np.zeros((128, 264), np.float32)
        sc = np.float32(128.0 ** -0.5)
        rcst[:, 0:128] = np.exp(log_g * np.abs(sidx[None, :] - sidx[:, None])) * ((jj // 64) <= (ii // 64)) * sc
        rcst[:, 128:256] = (np.exp(log_g * (sidx + 1.0)) * sc)[None, :]
        rcst[:, 256] = np.exp(log_g * (127.0 - sidx))
        rcst[:, 257] = np.exp(log_g * np.float32(128.0))
        rcst[:, 258] = retg[0:128]
        rcst[:, 259] = retg[128:256]
        m['retc'] = rcst
        m['cosT'] = cosT
        m['sinT'] = sinT
        m['cstm'] = cstm
        m['w_out1'] = w_out1
        m['xT_seq'] = np.ascontiguousarray(x[b].T)
        xo = np.zeros((D, TH), np.float32)
        xo[:, 2:] = x[b, TOK * j:TOK * j + TOK].T
        if j > 0:
            xo[:, 0:2] = x[b, TOK * j - 2:TOK * j].T
        m['xT_own'] = xo
        m['gains'] = gains
        m['tri'] = cst['tri']
        m['masks'] = cst['masks']
        h0, h1 = 2 * j, 2 * j + 1
        colsA = []
        for h in (h0, h1):
            colsA += list(range(128 * h, 128 * h + 128)) + list(range(1024 + 128 * h, 1024 + 128 * h + 128))
        for h in (h0, h1):
            colsA += list(range(2048 + 128 * h, 2048 + 128 * h + 128))
        m['w_inA'] = np.ascontiguousarray(ab_in[:, colsA])
        colsB = []
        for base in (3072, 4096):
            for cc in range(2):
                colsB += list(range(base + 256 * j + 128 * cc, base + 256 * j + 128 * cc + 128))
        colsB += list(range(5120 + 256 * j, 5120 + 256 * j + 256))
        m['w_inB'] = np.ascontiguousarray(ab_in[:, colsB])
        m['biasM'] = np.ascontiguousarray(rel_bias[bidx, j]).astype(np.float32)
        sB = np.zeros((128, 8), np.float32)
        sB[:, 0] = rel_bias[15, j]
        sB[:, 1] = subln[0:128]
        sB[:, 2] = subln[128:256]
        sB[:, 3] = 0.0 if j == 0 else 1.0
        m['smallB'] = sB
        m['lamv'] = np.ascontiguousarray(np.broadcast_to(lam.reshape(1, 512), (128, 512)))
        m['w_out0'] = w_out0
        for l in range(2):
            m['w_up%d' % l] = w_up_r[l]
            m['convp%d' % l] = convp_r[l]
            m['w_down%d' % l] = w_down_r[l]
        maps.append(m)
    return maps


_NC_CACHE = {}


def kernel(**inputs):
    maps = prep_inputs(inputs)
    if 'nc' not in _NC_CACHE:
        _NC_CACHE['nc'] = build()
    ui = _NC_CACHE['nc'].used_inputs
    maps = [{k: m[k] for k in ui} for m in maps]
    res = run_bass_kernel_spmd(_NC_CACHE['nc'], maps, core_ids=list(range(8)))
    out = np.zeros((NB, SEQ, D), np.float32)
    for c in range(8):
        b, j = divmod(c, 4)
        out[b, TOK * j:TOK * j + TOK, :] = res.results[c]['yT'].T
    return out
```

```python
import math
from contextlib import ExitStack
import numpy as np
import concourse.bass as bass
import concourse.mybir as mybir
from concourse.bass_utils import run_bass_kernel_spmd

F32 = mybir.dt.float32
BF16 = mybir.dt.bfloat16
AF = mybir.ActivationFunctionType
ALU = mybir.AluOpType

D = 2048
SEQ = 4096
NB = 2
TOK = 1024
TH = TOK + 2
KC = D // 128
DFF = 5632
NCH = DFF // 128
EPS = 1e-6
RG = [[0, 1, 2, 3], [4, 5, 6, 7]]
TT3 = [(0, 342), (342, 684), (684, 1026)]
STOP_ORDER = ['n0', 'projA', 'attA', 'attB', 'ag1', 'out0', 'norm0', 'ffn0', 'ag2', 'sgu', 'ret', 'ag3', 'out1', 'ffn1', 'final']


class Sched:
    ENGS = ('pe', 'act', 'dve', 'pool', 'sp')

    def __init__(self, nc, ndsem=8):
        self.nc = nc
        self.ndsem = ndsem
        self.streams = {e: [] for e in self.ENGS}
        self.cnt = {}
        self.seen = {e: {} for e in self.ENGS}
        self.lastw = {}
        self.readers = {}
        self.dma_idx = {e: 0 for e in self.ENGS}
        self.semkeys = []
        self.sems = {}

    def _semkey(self, k):
        if k not in self.cnt:
            self.cnt[k] = 0
            self.semkeys.append(k)
        return k

    def add(self, eng, fn, reads=(), writes=(), dma=False):
        deps = {}

        def need(d):
            for semk, val in d.items():
                if deps.get(semk, 0) < val:
                    deps[semk] = val
        for b in reads:
            need(self.lastw.get(b, {}))
        for b in writes:
            need(self.lastw.get(b, {}))
            need(self.readers.get(b, {}))
        if dma == 'cc':
            semk = self._semkey('CC')
            val = self.cnt[semk] + 1
            inc = 1
        elif dma:
            k = self.dma_idx[eng]
            self.dma_idx[eng] += 1
            semk = self._semkey('D_%s_%d' % (eng, k % self.ndsem))
            val = 16 * (k // self.ndsem + 1)
            if val > 16:
                need({semk: val - 16})
            inc = 16
        else:
            semk = self._semkey('E_' + eng)
            val = self.cnt[semk] + 1
            inc = 1
        self.cnt[semk] = val
        waits = []
        for sk, v in deps.items():
            if self.seen[eng].get(sk, 0) >= v:
                continue
            if sk == 'E_pe' and eng == 'pe':
                continue
            waits.append((sk, v))
            self.seen[eng][sk] = v
        self.streams[eng].append((waits, fn, semk, inc))
        for b in reads:
            r = self.readers.setdefault(b, {})
            if r.get(semk, 0) < val:
                r[semk] = val
        for b in writes:
            self.lastw[b] = {semk: val}
            self.readers[b] = {}
        return (semk, val)

    def barrier(self):
        for eng in self.ENGS:
            waits = []
            for sk, v in self.cnt.items():
                if v == 0 or self.seen[eng].get(sk, 0) >= v:
                    continue
                if sk == 'E_' + eng and eng in ('pe', 'sp'):
                    continue
                if sk == 'CC':
                    continue
                waits.append((sk, v))
                self.seen[eng][sk] = v
            if waits:
                self.streams[eng].append((waits, None, None, 0))

    def emit(self):
        nc = self.nc
        with ExitStack() as es:
            for k in self.semkeys:
                self.sems[k] = es.enter_context(nc.semaphore(k))
            with nc.Block() as block:
                def runner(name):
                    def run(e):
                        for waits, fn, semk, inc in self.streams[name]:
                            for sk, v in waits:
                                e.wait_ge(self.sems[sk], v)
                            if fn is not None:
                                ins = fn(e)
                                ins.then_inc(self.sems[semk], inc)
                    return run
                block.tensor(runner('pe'))
                block.scalar(runner('act'))
                block.vector(runner('dve'))
                block.gpsimd(runner('pool'))
                block.sync(runner('sp'))


def _rel_bucket_np(rel):
    nb = 16
    max_exact = 8
    ret = np.where(rel > 0, nb, 0)
    n = np.abs(rel)
    n_f = np.maximum(n, 1).astype(np.float32)
    large = max_exact + (np.log(n_f / np.float32(max_exact)) / np.float32(math.log(128 / max_exact))
                         * np.float32(nb - max_exact)).astype(np.int32)
    large = np.minimum(large, nb - 1)
    return ret + np.where(n < max_exact, n, large)


def _consts():
    c = {}
    j = np.arange(128)[:, None]
    s = np.arange(128)[None, :]
    tri = np.zeros((128, 3, 128), np.float32)
    tri[:, 0, :] = 1.0
    tri[:, 1, :] = (j > s)
    tri[:, 2, :] = (j <= s)
    c['tri'] = tri
    srow = np.arange(128)[:, None]
    t = np.arange(512)[None, :]
    mk = np.zeros((128, 8, 512), np.float32)
    for oi, o in enumerate((0, 128, 256, 384)):
        mk[:, oi, :] = ((o + srow) < t)
        mk[:, 4 + oi, :] = (((o + srow) // 64) <= (t // 64))
    c['masks'] = mk
    return c


def _bias_index():
    s = np.arange(128)[:, None]
    u = np.arange(1024)[None, :]
    return _rel_bucket_np((s - u + 384).astype(np.int32))


def _rot_tables(gamma):
    d = 128
    inv_freq = (10000.0 ** (-np.arange(0, d, 2, dtype=np.float32) / d)).astype(np.float32)
    ang = np.arange(SEQ, dtype=np.float32)[:, None] * inv_freq[None, :]
    cos = np.concatenate([np.cos(ang), np.cos(ang)], -1).astype(np.float32)
    sin = np.concatenate([-np.sin(ang), np.sin(ang)], -1).astype(np.float32)
    return cos, sin


def build(stop='final', dbg=()):
    nc = bass.Bass("TRN2", target_bir_lowering=False)
    S = Sched(nc)
    stop_i = STOP_ORDER.index(stop)

    def upto(name):
        return STOP_ORDER.index(name) <= stop_i

    used_inputs = []

    def din(name, shape, dt=F32, need='n0'):
        if not upto(need):
            return None
        used_inputs.append(name)
        return nc.dram_tensor(name, list(shape), dt, kind="ExternalInput").ap()

    def dout(name, shape, dt=F32):
        return nc.dram_tensor(name, list(shape), dt, kind="ExternalOutput").ap()

    def dint(name, shape, dt):
        return nc.dram_tensor(name, list(shape), dt).ap()

    xT_seq = din("xT_seq", [D, SEQ])
    xT_own = din("xT_own", [D, TH], need="out0")
    gains = din("gains", [128, 5, KC])
    tri_d = din("tri", [128, 3, 128])
    masks_d = din("masks", [128, 8, 512])
    w_inA = din("w_inA", [D, 768])
    w_inB = din("w_inB", [D, 768], need="attB")
    biasM_d = din("biasM", [128, 1024], need="attB")
    smallB_d = din("smallB", [128, 8])
    lamv_d = din("lamv", [128, 512], need="attB")
    w_out0 = din("w_out0", [D, D], need="out0")
    w_up = [din("w_up%d" % l, [D, NCH, 256], need="ffn%d" % l) for l in range(2)]
    convp = [din("convp%d" % l, [128, NCH, 8], need="ffn%d" % l) for l in range(2)]
    w_down = [din("w_down%d" % l, [DFF, D], need="ffn%d" % l) for l in range(2)]
    w_zd = din("w_zd", [D, 2048], need="sgu")
    sguw_d = din("sguw", [128, 4, 128], need="sgu")
    sgum_d = din("sgum", [128, 128], need="sgu")
    lngb_d = din("lngb", [128, 2, 1024], need="sgu")
    sgub_d = din("sgub", [128, 4, 512], need="sgu")
    w_inC = din("w_inC", [D, 1024], need="ret")
    retc_d = din("retc", [128, 264], need="ret")
    cosT_d = din("cosT", [128, SEQ], need="ret")
    sinT_d = din("sinT", [128, SEQ], need="ret")
    cstm_d = din("cstm", [128, 32, 256], need="ret")
    w_out1 = din("w_out1", [D, D], need="out1")
    yT = dout("yT", [D, TOK])
    dbg_out = {}
    for name, shape, dt in dbg:
        dbg_out[name] = dout("dbg_" + name, shape, dt)

    hseq = dint("hseq", [D, SEQ], BF16)
    cin1s = [dint("cin1s_%d" % t, [512, TOK], BF16) for t in range(4)]
    cout1a = dint("cout1a", [4 * 2048, TOK], BF16)
    cin2 = [dint("cin2_%d" % g, [1024, 512], BF16) for g in range(4)]
    cout2 = [dint("cout2_%d" % g, [4096, 512], BF16) for g in range(4)]
    cin3s = [dint("cin3s_%d" % t, [256, TOK], BF16) for t in range(4)]
    cout3a = dint("cout3a", [4 * 1024, TOK], BF16)
    cinH = dint("cinH", [128, 48], F32)
    coutH = dint("coutH", [512, 48], F32)
    x1s = dint("x1s", [D, TOK], F32)

    pid = nc.partition_id()
    rank = pid % 4

    def dma(q, out, in_, reads, writes):
        return S.add(q, lambda e: e.dma_start(out=out, in_=in_), reads, writes, dma=True)

    def act(out, in_, func, reads, writes, **kw):
        return S.add('act', lambda e: e.activation(out=out, in_=in_, func=func, **kw), reads, writes)

    def tt(out, in0, in1, op, reads, writes, eng='dve'):
        return S.add(eng, lambda e: e.tensor_tensor(out=out, in0=in0, in1=in1, op=op), reads, writes)

    def ts(out, in0, s1, s2, op0, op1, reads, writes, eng='dve'):
        if op1 is None:
            return S.add(eng, lambda e: e.tensor_scalar(out=out, in0=in0, scalar1=s1, scalar2=None, op0=op0), reads, writes)
        return S.add(eng, lambda e: e.tensor_scalar(out=out, in0=in0, scalar1=s1, scalar2=s2, op0=op0, op1=op1), reads, writes)

    def stt(out, in0, scalar, in1, op0, op1, reads, writes):
        return S.add('dve', lambda e: e.scalar_tensor_tensor(out=out, in0=in0, scalar=scalar, in1=in1, op0=op0, op1=op1), reads, writes)

    def cp(out, in_, reads, writes, eng='dve'):
        if eng == 'act':
            return act(out, in_, AF.Copy, reads, writes)
        return S.add(eng, lambda e: e.tensor_copy(out=out, in_=in_), reads, writes)

    def mms(lst, reads, writes):
        def fn(e):
            r = None
            for (o, l, rh, st, sp) in lst:
                r = e.matmul(o, lhsT=l, rhs=rh, start=st, stop=sp)
            return r
        return S.add('pe', fn, reads, writes)

    with ExitStack() as glob:
        uniq = [0]

        def sb(es, name, shape, dt):
            uniq[0] += 1
            return es.enter_context(nc.sbuf_tensor("s%d_%s" % (uniq[0], name), list(shape), dt))

        ps = glob.enter_context(nc.psum_tensor("ps", [128, 8, 512], F32))
        tri = sb(glob, "tri", [128, 3, 128], BF16)
        masks = sb(glob, "masks", [128, 8, 512], BF16)
        gn = sb(glob, "gn", [128, 5, KC], F32)
        smallB = sb(glob, "smallB", [128, 8], F32)
        with ExitStack() as es:
            trif = sb(es, "trif", [128, 3, 128], F32)
            mkf = sb(es, "mkf", [128, 8, 512], F32)
            dma('sp', trif[:], tri_d, [], ['trif'])
            dma('sp', mkf[:], masks_d, [], ['mkf'])
            dma('sp', gn[:], gains, [], ['gn'])
            dma('sp', smallB[:], smallB_d, [], ['smallB'])
            cp(tri[:], trif[:], ['trif'], ['tri'])
            cp(masks[:], mkf[:], ['mkf'], ['masks'])
            S.barrier()
        ones = tri[:, 0, :]
        Umat = tri[:, 1, :]
        Lmat = tri[:, 2, :]

        def rstd_from_ps(bank_ap, out_ap, n, reads, writes):
            act(out_ap, bank_ap, AF.Ln, reads, writes, scale=1.0 / n, bias=eps_t[:, 0:1])
            act(out_ap, out_ap, AF.Exp, writes, writes, scale=-0.5)

        eps_t = sb(glob, "eps_t", [128, 1], F32)
        S.add('dve', lambda e: e.memset(eps_t[:], EPS), [], ['eps'])
        S.barrier()

        xsv = xT_seq.rearrange("(kc p) t -> p kc t", p=128)
        hsv = hseq.rearrange("(kc p) t -> p kc t", p=128)
        wst = ExitStack()
        wA_t = sb(wst, "wA_t", [128, KC, 768], BF16)
        wB_t = sb(wst, "wB_t", [128, KC, 768], BF16)
        wvA = w_inA.rearrange("(kc p) n -> p kc n", p=128)
        for i in range(3):
            dma('pool', wA_t[:, :, 256 * i:256 * i + 256], wvA[:, :, 256 * i:256 * i + 256], [], ['wA%d' % i])
        mix = ExitStack()
        oT_all = sb(mix, "oT_all", [128, 4, SEQ], BF16)
        if upto('attB'):
            wvB = w_inB.rearrange("(kc p) n -> p kc n", p=128)
            for i in range(3):
                dma('pool', wB_t[:, :, 256 * i:256 * i + 256], wvB[:, :, 256 * i:256 * i + 256], ['hs16_15'], ['wB%d' % i])

        def in_proj(es, wA, QK, V, tagp):
            hbb = [sb(es, "hin%s%d" % (tagp, i), [128, KC, 512], BF16) for i in range(2)]
            wk = ['w%s%d' % (tagp, i) for i in range(3)]
            ev = 0
            for t8 in range(8):
                p = t8 % 2
                sl = slice(512 * t8, 512 * t8 + 512)
                dma('sp', hbb[p][:], hsv[:, :, sl], ['hs16_%d' % (2 * t8), 'hs16_%d' % (2 * t8 + 1)], ['hin%d' % p])
                for g in range(4):
                    bk = ev % 4
                    mms([(ps[:, bk, :], wA[:, kc, 128 * g:128 * g + 128], hbb[p][:, kc, :], kc == 0, kc == KC - 1) for kc in range(KC)],
                        ['hin%d' % p] + wk, ['ps%d' % bk])
                    cp(QK[:, g, sl], ps[:, bk, :], ['ps%d' % bk], ['QK%d_%d' % (g, t8)], eng=('act' if ev % 2 else 'dve'))
                    ev += 1
                for sub in range(4):
                    bk = ev % 4
                    mms([(ps[:, bk, 0:256], hbb[p][:, kc, 128 * sub:128 * sub + 128], wA[:, kc, 512:768], kc == 0, kc == KC - 1) for kc in range(KC)],
                        ['hin%d' % p] + wk, ['ps%d' % bk])
                    cp(V[:, 4 * t8 + sub, :], ps[:, bk, 0:256], ['ps%d' % bk], ['V%d' % (4 * t8 + sub)], eng=('act' if ev % 2 else 'dve'))
                    ev += 1

        scl = 128.0 ** -0.5

        def ag_slab(t, cin, cout_all, src, ng, tagp):
            rows = ng * 128 * 4
            dma('sp', cin[t].rearrange("(g p) t -> p g t", p=128), src[:, :, TOK * t:TOK * t + TOK],
                ['oT%d_%d' % (g, qt) for g in range(ng) for qt in (2 * t, 2 * t + 1)], ['%sin%d' % (tagp, t)])
            S.add('pool', lambda e: e.collective_compute("AllGather", ALU.bypass, replica_groups=RG, ins=[cin[t].opt()],
                                                         outs=[cout_all[rows * t:rows * t + rows, :].opt()]),
                  ['%sin%d' % (tagp, t)], ['%sout%d' % (tagp, t)], dma='cc')

        if upto('projA'):
            with ExitStack() as es:
                QK = sb(es, "QKa", [128, 4, SEQ], BF16)
                V = sb(es, "Va", [128, 32, 256], BF16)
                with ExitStack() as es2:
                    xs = [sb(es2, "xs%d" % i, [128, KC, 256], F32) for i in range(2)]
                    sq = sb(es2, "sq", [128, KC, 256], BF16)
                    hb = [sb(es2, "hb%d" % i, [128, KC, 256], BF16) for i in range(2)]
                    rs = [sb(es2, "rs%d" % i, [128, 256], F32) for i in range(2)]
                    wkA = ['wA%d' % i for i in range(3)]
                    dma('sp', xs[0][:], xsv[:, :, 0:256], [], ['xs0'])
                    dma('sp', xs[1][:], xsv[:, :, 256:512], [], ['xs1'])
                    evc = [0]

                    def stageN(t16):
                        p = t16 % 2
                        sl = slice(256 * t16, 256 * t16 + 256)
                        act(sq[:], xs[p][:], AF.Square, ['xs%d' % p], ['sq'])
                        mms([(ps[:, p, 0:256], ones, sq[:, kc, :], kc == 0, kc == KC - 1) for kc in range(KC)], ['sq', 'tri'], ['ps%d' % p])
                        rstd_from_ps(ps[:, p, 0:256], rs[p][:], D, ['ps%d' % p], ['rs%d' % p])
                        for kc in range(KC):
                            stt(hb[p][:, kc, :], xs[p][:, kc, :], gn[:, 0, kc:kc + 1], rs[p][:], ALU.mult, ALU.mult,
                                ['xs%d' % p, 'rs%d' % p, 'gn'], ['hb%d_%d' % (p, kc)])
                        if t16 + 2 < 16:
                            dma('sp', xs[p][:], xsv[:, :, 256 * (t16 + 2):256 * (t16 + 2) + 256], [], ['xs%d' % p])
                        dma('sp', hsv[:, :, sl], hb[p][:], ['hb%d_%d' % (p, kc) for kc in range(KC)], ['hs16_%d' % t16])

                    def stageP(t16):
                        p = t16 % 2
                        sl = slice(256 * t16, 256 * t16 + 256)
                        hbk = ['hb%d_%d' % (p, kc) for kc in range(KC)]
                        for g in range(4):
                            ev = evc[0]
                            bk = 2 + ev % 6
                            mms([(ps[:, bk, 0:256], wA_t[:, kc, 128 * g:128 * g + 128], hb[p][:, kc, :], kc == 0, kc == KC - 1) for kc in range(KC)],
                                hbk + wkA, ['ps%d' % bk])
                            cp(QK[:, g, sl], ps[:, bk, 0:256], ['ps%d' % bk], ['QK%d_%d' % (g, t16)], eng=('act' if ev % 2 else 'dve'))
                            evc[0] += 1
                        for sub in range(2):
                            ev = evc[0]
                            bk = 2 + ev % 6
                            mms([(ps[:, bk, 0:256], hb[p][:, kc, 128 * sub:128 * sub + 128], wA_t[:, kc, 512:768], kc == 0, kc == KC - 1) for kc in range(KC)],
                                hbk + wkA, ['ps%d' % bk])
                            cp(V[:, 2 * t16 + sub, :], ps[:, bk, 0:256], ['ps%d' % bk], ['V%d' % (2 * t16 + sub)], eng=('act' if ev % 2 else 'dve'))
                            evc[0] += 1

                    stageN(0)
                    for t16 in range(16):
                        if t16 + 1 < 16:
                            stageN(t16 + 1)
                        stageP(t16)
                    S.barrier()
                if 'QKa' in dbg_out:
                    dma('sp', dbg_out['QKa'], QK[:], [], ['dbg1'])
                    dma('sp', dbg_out['Va'], V[:], [], ['dbg2'])
                if upto('attA'):
                    with ExitStack() as es2:
                        Eb = [[sb(es2, "Eb%d%d" % (h, q), [128, 512], F32) for q in range(2)] for h in range(2)]
                        SPf = [[sb(es2, "SPf%d%d" % (h, q), [128, 512], F32) for q in range(2)] for h in range(2)]
                        SPb = [[sb(es2, "SPb%d%d" % (h, q), [128, 512], BF16) for q in range(2)] for h in range(2)]
                        T1 = [[sb(es2, "T1%d%d" % (h, q), [128, 512], F32) for q in range(2)] for h in range(2)]
                        Wb = [[sb(es2, "Wb%d%d" % (h, q), [128, 512], BF16) for q in range(2)] for h in range(2)]
                        blocks = [(qt, bi, i) for qt in range(8) for bi, i in enumerate(range(4 * qt + 3, -1, -1))]

                        def A1(n):
                            qt, bi, i = blocks[n]
                            q = n % 2
                            qsl = slice(512 * qt, 512 * qt + 512)
                            ksl = slice(128 * i, 128 * i + 128)
                            for h in range(2):
                                zb = 2 * q + h
                                mms([(ps[:, zb, :], QK[:, 2 * h + 1, ksl], QK[:, 2 * h, qsl], True, True)], [], ['ps%d' % zb])
                            for h in range(2):
                                zb = 2 * q + h
                                kq = '%d%d' % (h, q)
                                act(Eb[h][q][:], ps[:, zb, :], AF.Exp, ['ps%d' % zb], ['Eb' + kq], scale=scl)
                                act(SPf[h][q][:], Eb[h][q][:], AF.Ln, ['Eb' + kq], ['SPf' + kq], bias=1.0)

                        def A2(n):
                            qt, bi, i = blocks[n]
                            q = n % 2
                            o = 128 * i - 512 * qt
                            diag = o >= 0
                            first = bi == 0
                            for h in range(2):
                                kq = '%d%d' % (h, q)
                                if diag:
                                    tt(SPb[h][q][:], SPf[h][q][:], masks[:, o // 128, :], ALU.mult, ['SPf' + kq, 'masks'], ['SPb' + kq])
                                else:
                                    cp(SPb[h][q][:], SPf[h][q][:], ['SPf' + kq], ['SPb' + kq])
                            for h in range(2):
                                kq = '%d%d' % (h, q)
                                mms([(ps[:, 4 + h, :], Umat, SPb[h][q][:], first, True)], ['SPb' + kq, 'tri'], ['ps%d' % (4 + h)])
                            for h in range(2):
                                zb = 2 * q + h
                                kq = '%d%d' % (h, q)
                                stt(T1[h][q][:], ps[:, zb, :], scl, SPf[h][q][:], ALU.mult, ALU.subtract, ['ps%d' % zb, 'SPf' + kq], ['T1' + kq])
                            for h in range(2):
                                kq = '%d%d' % (h, q)
                                tt(T1[h][q][:], T1[h][q][:], ps[:, 4 + h, :], ALU.subtract, ['T1' + kq, 'ps%d' % (4 + h)], ['T1' + kq])
                                mms([(ps[:, 4 + h, :], Lmat, SPb[h][q][:], False, True)], ['SPb' + kq, 'tri'], ['ps%d' % (4 + h)])

                        def A3(n):
                            qt, bi, i = blocks[n]
                            q = n % 2
                            qsl = slice(512 * qt, 512 * qt + 512)
                            o = 128 * i - 512 * qt
                            diag = o >= 0
                            first = bi == 0
                            last = i == 0
                            for h in range(2):
                                kq = '%d%d' % (h, q)
                                act(Wb[h][q][:], T1[h][q][:], AF.Exp, ['T1' + kq], ['Wb' + kq])
                                if diag:
                                    tt(Wb[h][q][:], Wb[h][q][:], masks[:, o // 128, :], ALU.mult, ['Wb' + kq, 'masks'], ['Wb' + kq])
                            for h in range(2):
                                kq = '%d%d' % (h, q)
                                mms([(ps[:, 6 + h, :], V[:, i, 128 * h:128 * h + 128], Wb[h][q][:], first, last)], ['Wb' + kq], ['ps%d' % (6 + h)])
                            if last:
                                for h in range(2):
                                    cp(oT_all[:, h, qsl], ps[:, 6 + h, :], ['ps%d' % (6 + h)], ['oT%d_%d' % (h, qt)], eng='act')

                        NBk = len(blocks)
                        for n in range(NBk + 2):
                            if n < NBk:
                                A1(n)
                            if 0 <= n - 1 < NBk:
                                A2(n - 1)
                            if 0 <= n - 2 < NBk:
                                A3(n - 2)
                        S.barrier()
        if 'oT' in dbg_out and not upto('attB'):
            dma('sp', dbg_out['oT'], oT_all[:], ['oT%d_%d' % (h, qt) for h in range(2) for qt in range(8)], ['dbg3'])

        if upto('attB'):
            with ExitStack() as es:
                QK = sb(es, "QKb", [128, 4, SEQ], BF16)
                V = sb(es, "Vb", [128, 32, 256], BF16)
                with ExitStack() as es2:
                    in_proj(es2, wB_t, QK, V, "B")
                    S.barrier()
                with ExitStack() as es2:
                    biasM = sb(es2, "biasM", [128, 1024], F32)
                    lamv = sb(es2, "lamv", [128, 512], F32)
                    lt = sb(es2, "lt", [128, 8], F32)
                    dma('sp', biasM[:], biasM_d, [], ['biasM'])
                    dma('sp', lamv[:], lamv_d, [], ['lamv'])
                    lam_init = 0.8 - 0.6 * math.exp(-0.3 * 0)
                    lp = sb(es2, "lp", [128, 256], F32)
                    tt(lp[:, 0:128], lamv[:, 0:128], lamv[:, 128:256], ALU.mult, ['lamv'], ['lp0'])
                    tt(lp[:, 128:256], lamv[:, 256:384], lamv[:, 384:512], ALU.mult, ['lamv'], ['lp1'])
                    S.add('dve', lambda e: e.reduce_sum(out=lt[:, 0:1], in_=lp[:, 0:128], axis=mybir.AxisListType.X), ['lp0'], ['lt0'])
                    S.add('dve', lambda e: e.reduce_sum(out=lt[:, 1:2], in_=lp[:, 128:256], axis=mybir.AxisListType.X), ['lp1'], ['lt1'])
                    act(lt[:, 2:4], lt[:, 0:2], AF.Exp, ['lt0', 'lt1'], ['lt23'])
                    tt(lt[:, 4:5], lt[:, 3:4], lt[:, 2:3], ALU.subtract, ['lt23'], ['lt4'])
                    ts(lt[:, 4:5], lt[:, 4:5], -lam_init, None, ALU.add, None, ['lt4'], ['lt4'])
                    neglam = lt[:, 4:5]
                    ts(lt[:, 5:7], smallB[:, 1:3], 1.0 - lam_init, None, ALU.mult, None, ['smallB'], ['lt56'])
                    gsub = lt[:, 5:7]
                    bconst = smallB[:, 0:1]
                    Tn = [sb(es2, "Tn%d%d" % (c, q), [128, 512], F32) for c in range(2) for q in range(2)]
                    Ebf = [sb(es2, "Ebf%d%d" % (c, q), [128, 512], BF16) for c in range(2) for q in range(2)]
                    Esum = [sb(es2, "Esum%d" % c, [128, 512], F32) for c in range(2)]
                    Esb = [sb(es2, "Esb%d" % c, [128, 512], BF16) for c in range(2)]
                    rD = [sb(es2, "rD%d" % c, [128, 512], F32) for c in range(2)]
                    oh = [sb(es2, "oh%d" % c, [128, 512], F32) for c in range(2)]
                    tA = sb(es2, "tA", [128, 512], F32)
                    sqh = [sb(es2, "sqh%d" % c, [128, 512], BF16) for c in range(2)]
                    rsb = sb(es2, "rsb", [128, 512], F32)
                    blocksB = []
                    for qt in range(8):
                        allb = list(range(4 * qt + 3, -1, -1))
                        nearb = [i for i in allb if 128 * i - 512 * qt >= -128]
                        farb = [i for i in allb if 128 * i - 512 * qt < -128]
                        order = []
                        while nearb or farb:
                            if farb:
                                order.append(farb.pop(0))
                            if nearb:
                                order.append(nearb.pop(0))
                        for pos, i in enumerate(order):
                            blocksB.append((qt, pos, i, len(order)))

                    def B1(n):
                        qt, bi, i, nbq = blocksB[n]
                        q = n % 2
                        qsl = slice(512 * qt, 512 * qt + 512)
                        o = 128 * i - 512 * qt
                        near = o >= -128
                        diag = o >= 0
                        ksl = slice(128 * i, 128 * i + 128)
                        for c in range(2):
                            zb = 2 * q + c
                            mms([(ps[:, zb, :], QK[:, 2 + c, ksl], QK[:, c, qsl], True, True)], [], ['ps%d' % zb])
                        for c in range(2):
                            zb = 2 * q + c
                            kq = '%d%d' % (c, q)
                            E = Ebf[2 * c + q]
                            if near:
                                T = Tn[2 * c + q]
                                u0 = 384 - o
                                stt(T[:], ps[:, zb, :], scl, biasM[:, u0:u0 + 512], ALU.mult, ALU.add, ['ps%d' % zb, 'biasM'], ['Tn' + kq])
                                act(E[:], T[:], AF.Exp, ['Tn' + kq], ['Ebf' + kq])
                                if diag:
                                    tt(E[:], E[:], masks[:, 4 + o // 128, :], ALU.mult, ['Ebf' + kq, 'masks'], ['Ebf' + kq])
                            else:
                                act(E[:], ps[:, zb, :], AF.Exp, ['ps%d' % zb, 'smallB'], ['Ebf' + kq], scale=scl, bias=bconst)

                    def B2(n):
                        qt, bi, i, nbq = blocksB[n]
                        q = n % 2
                        qsl = slice(512 * qt, 512 * qt + 512)
                        first = bi == 0
                        last = bi == nbq - 1
                        for c in range(2):
                            kq = '%d%d' % (c, q)
                            E = Ebf[2 * c + q]
                            mms([(ps[:, 4 + 2 * c + hf, :], V[:, i, 128 * hf:128 * hf + 128], E[:], first, last) for hf in range(2)],
                                ['Ebf' + kq], ['ps%d' % (4 + 2 * c), 'ps%d' % (5 + 2 * c)])
                            eg = 'dve'
                            if first:
                                cp(Esum[c][:], E[:], ['Ebf' + kq], ['Esum%d' % c], eng=eg)
                            else:
                                tt(Esum[c][:], Esum[c][:], E[:], ALU.add, ['Ebf' + kq, 'Esum%d' % c], ['Esum%d' % c], eng=eg)
                        if last:
                            for c in range(2):
                                cp(Esb[c][:], Esum[c][:], ['Esum%d' % c], ['Esb%d' % c])
                                mms([(ps[:, c, :], ones, Esb[c][:], True, True)], ['Esb%d' % c, 'tri'], ['ps%d' % c])
                                S.add('dve', lambda e, c=c: e.reciprocal(out=rD[c][:], in_=ps[:, c, :]), ['ps%d' % c], ['rD%d' % c])
                            for hf in range(2):
                                tt(tA[:], ps[:, 4 + hf, :], rD[0][:], ALU.mult, ['ps%d' % (4 + hf), 'rD0'], ['tA'])
                                tt(oh[hf][:], ps[:, 6 + hf, :], rD[1][:], ALU.mult, ['ps%d' % (6 + hf), 'rD1'], ['oh%d' % hf])
                                stt(oh[hf][:], oh[hf][:], neglam, tA[:], ALU.mult, ALU.add, ['oh%d' % hf, 'tA', 'lt4'], ['oh%d' % hf])
                                act(sqh[hf][:], oh[hf][:], AF.Square, ['oh%d' % hf], ['sqh%d' % hf])
                            mms([(ps[:, 2, :], ones, sqh[hf][:], hf == 0, hf == 1) for hf in range(2)], ['sqh0', 'sqh1', 'tri'], ['ps2'])
                            rstd_from_ps(ps[:, 2, :], rsb[:], 256, ['ps2'], ['rsb'])
                            for hf in range(2):
                                stt(oT_all[:, 2 + hf, qsl], oh[hf][:], gsub[:, hf:hf + 1], rsb[:], ALU.mult, ALU.mult,
                                    ['oh%d' % hf, 'rsb', 'lt56'], ['oT%d_%d' % (2 + hf, qt)])
                            if qt % 2 == 1 and upto('ag1'):
                                ag_slab(qt // 2, cin1s, cout1a, oT_all, 4, 'c1')

                    for n in range(len(blocksB) + 1):
                        if n < len(blocksB):
                            B1(n)
                        if n > 0:
                            B2(n - 1)
                    S.barrier()
        if 'oT' in dbg_out and upto('attB'):
            dma('sp', dbg_out['oT'], oT_all[:], ['oT%d_%d' % (h, qt) for h in range(4) for qt in range(8)], ['dbg3'])

        mix.close()
        wst.close()
        S.barrier()

        hpk = sb(glob, "hpk", [128, 48], F32)
        hh = sb(glob, "hh", [128, 48], F32)
        odT = sb(glob, "odT", [128, 8, TOK], BF16)
        tok = ExitStack()
        xT = sb(tok, "xT", [128, KC, TH], F32)
        aT = sb(tok, "aT", [128, KC, TH], BF16)
        rs1 = sb(tok, "rs1", [128, TH], F32)

        def out_proj(w_d, lname):
            with ExitStack() as es:
                wo = [sb(es, "wo%d" % i, [128, KC, 256], BF16) for i in range(3)]
                wv = w_d.rearrange("(kc p) n -> p kc n", p=128)
                n = 0
                for c2 in range(8):
                    wb = c2 % 3
                    dma('pool', wo[wb][:], wv[:, :, 256 * c2:256 * c2 + 256], [], ['wo%d' % wb])
                    for half in range(2):
                        cc = 2 * c2 + half
                        for (a, b) in TT3:
                            bk = n % 8
                            n += 1
                            mms([(ps[:, bk, 0:b - a], wo[wb][:, kc, 128 * half:128 * half + 128], aT[:, kc, a:b], kc == 0, kc == KC - 1) for kc in range(KC)],
                                ['wo%d' % wb] + ['aT%d' % kc for kc in range(KC)], ['ps%d' % bk])
                            tt(xT[:, cc, a:b], xT[:, cc, a:b], ps[:, bk, 0:b - a], ALU.add, ['ps%d' % bk, 'xT%d' % cc], ['xT%d' % cc])
                S.barrier()

        def norm_to_aT(gi, t0, t1):
            with ExitStack() as es:
                sq = sb(es, "sqn", [128, KC, 342], BF16)
                tiles = [(a, min(a + 342, t1)) for a in range(t0, t1, 342)]
                for ti, (a, b) in enumerate(tiles):
                    bk = ti % 8
                    act(sq[:, :, 0:b - a], xT[:, :, a:b], AF.Square, ['xT%d' % kc for kc in range(KC)], ['sqn'])
                    mms([(ps[:, bk, 0:b - a], ones, sq[:, kc, 0:b - a], kc == 0, kc == KC - 1) for kc in range(KC)], ['sqn', 'tri'], ['ps%d' % bk])
                    rstd_from_ps(ps[:, bk, 0:b - a], rs1[:, a:b], D, ['ps%d' % bk], ['rs1_%d' % ti])
                    for kc in range(KC):
                        stt(aT[:, kc, a:b], xT[:, kc, a:b], gn[:, gi, kc:kc + 1], rs1[:, a:b], ALU.mult, ALU.mult,
                            ['xT%d' % kc, 'rs1_%d' % ti, 'gn'], ['aT%d' % kc])
                S.barrier()

        def conv_ffn(l):
            GK = 4
            with ExitStack() as es:
                wu = [sb(es, "wu%d" % i, [128, KC, 256], BF16) for i in range(3)]
                cpar = sb(es, "cpar", [128, NCH, 8], F32)
                Ur = [sb(es, "Ur%d" % i, [128, TH], F32) for i in range(2)]
                Cc = [sb(es, "Cc%d" % i, [128, TOK], F32) for i in range(2)]
                gat = [sb(es, "gat%d" % i, [128, GK, TOK], BF16) for i in range(2)]
                wd = [sb(es, "wd%d" % i, [128, GK, 256], BF16) for i in range(3)]
                dma('sp', cpar[:], convp[l], [], ['cpar'])
                wdv = w_down[l].rearrange("(c p) n -> p c n", p=128)
                aTk = ['aT%d' % kc for kc in range(KC)]
                nw = 0
                nd = 0
                for kg in range(NCH // GK):
                    gp = kg % 2
                    for ci in range(GK):
                        c = kg * GK + ci
                        wb = nw % 3
                        nw += 1
                        dma('pool', wu[wb][:], w_up[l][:, c, :].rearrange("(kc p) n -> p kc n", p=128), [], ['wu%d' % wb])
                        for ag in range(2):
                            b0 = 3 * ag
                            for ti, (a, b) in enumerate(TT3):
                                mms([(ps[:, b0 + ti, 0:b - a], wu[wb][:, kc, 128 * ag:128 * ag + 128], aT[:, kc, a:b], kc == 0, kc == KC - 1) for kc in range(KC)],
                                    ['wu%d' % wb] + aTk, ['ps%d' % (b0 + ti)])
                            pk = ['ps%d' % (b0 + ti) for ti in range(3)]
                            act(Ur[ag][:].rearrange("p (a b) -> p a b", a=3), ps[:, b0:b0 + 3, 0:342], AF.Copy, pk, ['Ur%d' % ag])
                            pb = 4 * ag
                            ts(Cc[ag][:], Ur[ag][:, 2:TH], cpar[:, c, pb + 2:pb + 3], cpar[:, c, pb + 3:pb + 4], ALU.mult, ALU.add,
                               ['Ur%d' % ag, 'cpar'], ['Cc%d' % ag])
                            stt(Cc[ag][:], Ur[ag][:, 1:TH - 1], cpar[:, c, pb + 1:pb + 2], Cc[ag][:], ALU.mult, ALU.add, ['Ur%d' % ag, 'Cc%d' % ag, 'cpar'], ['Cc%d' % ag])
                            stt(Cc[ag][:], Ur[ag][:, 0:TH - 2], cpar[:, c, pb + 0:pb + 1], Cc[ag][:], ALU.mult, ALU.add, ['Ur%d' % ag, 'Cc%d' % ag, 'cpar'], ['Cc%d' % ag])
                        act(Cc[1][:], Cc[1][:], AF.Silu, ['Cc1'], ['Cc1'])
                        tt(gat[gp][:, ci, :], Cc[0][:], Cc[1][:], ALU.mult, ['Cc0', 'Cc1'], ['gat%d_%d' % (gp, ci)])
                    gk = ['gat%d_%d' % (gp, ci) for ci in range(GK)]
                    for c2 in range(8):
                        wb = nd % 3
                        nd += 1
                        dma('pool', wd[wb][:], wdv[:, kg * GK:kg * GK + GK, 256 * c2:256 * c2 + 256], [], ['wd%d' % wb])
                        for half in range(2):
                            cc = 2 * c2 + half
                            for th in range(2):
                                bk = (6, 7, 4, 5)[(2 * half + th) % 4]
                                mms([(ps[:, bk, :], wd[wb][:, ci, 128 * half:128 * half + 128], gat[gp][:, ci, 512 * th:512 * th + 512], ci == 0, ci == GK - 1) for ci in range(GK)],
                                    ['wd%d' % wb] + gk, ['ps%d' % bk])
                                sl = slice(2 + 512 * th, 2 + 512 * th + 512)
                                tt(xT[:, cc, sl], xT[:, cc, sl], ps[:, bk, :], ALU.add, ['ps%d' % bk, 'xT%d' % cc], ['xT%d' % cc])
                S.barrier()

        xTk = ['xT%d' % kc for kc in range(KC)]
        aTk = ['aT%d' % kc for kc in range(KC)]
        if upto('out0'):
            dma('sp', xT[:], xT_own.rearrange("(kc p) t -> p kc t", p=128), [], xTk)
            cv = cout1a.rearrange("(sk p) t -> p sk t", p=128)
            c1k = ['c1out%d' % t for t in range(4)]
            dma('sp', aT[:, :, 2:TH], cv[:, bass.ds(rank * 16, 16), :], c1k, aTk)
            dma('sp', aT[:, :, 0:2], cv[:, bass.ds(((rank + 3) % 4) * 16, 16), TOK - 2:TOK], c1k, ['aTh'])
            ts(aT[:, :, 0:2], aT[:, :, 0:2], smallB[:, 3:4], None, ALU.mult, None, ['aTh', 'smallB'] + aTk, aTk)
            out_proj(w_out0, 'o0')
            if 'xmid0' in dbg_out:
                dma('sp', dbg_out['xmid0'].rearrange("(kc p) t -> p kc t", p=128), xT[:], xTk, ['dbg4'])
        if upto('norm0'):
            norm_to_aT(2, 0, TH)
        if upto('ffn0'):
            conv_ffn(0)
            if 'x1' in dbg_out:
                dma('sp', dbg_out['x1'].rearrange("(kc p) t -> p kc t", p=128), xT[:], xTk, ['dbg5'])


        def final_out(normed):
            with ExitStack() as es:
                yo = sb(es, "yo", [128, KC, TOK], F32)
                if normed:
                    sq = sb(es, "sqf", [128, KC, 512], BF16)
                    for th in range(2):
                        sl = slice(2 + 512 * th, 2 + 512 * th + 512)
                        act(sq[:], xT[:, :, sl], AF.Square, xTk, ['sqf'])
                        mms([(ps[:, th, :], ones, sq[:, kc, :], kc == 0, kc == KC - 1) for kc in range(KC)], ['sqf', 'tri'], ['ps%d' % th])
                        rstd_from_ps(ps[:, th, :], rs1[:, sl], D, ['ps%d' % th], ['rsf%d' % th])
                        for kc in range(KC):
                            stt(yo[:, kc, 512 * th:512 * th + 512], xT[:, kc, sl], gn[:, 4, kc:kc + 1], rs1[:, sl], ALU.mult, ALU.mult,
                                ['xT%d' % kc, 'rsf%d' % th, 'gn'], ['yo%d_%d' % (kc, th)])
                    yk = ['yo%d_%d' % (kc, th) for kc in range(KC) for th in range(2)]
                else:
                    for kc in range(KC):
                        cp(yo[:, kc, :], xT[:, kc, 2:TH], ['xT%d' % kc], ['yo%d' % kc], eng=('act' if kc % 2 else 'dve'))
                    yk = ['yo%d' % kc for kc in range(KC)]
                dma('sp', yT.rearrange("(kc p) t -> p kc t", p=128), yo[:], yk, ['yT'])
                S.barrier()

        if not upto('ag2'):
            final_out(False)
            tok.close()
            S.barrier()
        else:
            norm_to_aT(1, 2, TH)
            dma('sp', x1s.rearrange("(kc p) t -> p kc t", p=128), xT[:, :, 2:TH], xTk, ['x1s'])
            cp(hpk[:, 0:32].rearrange("p (k t) -> p k t", t=2), xT[:, :, TH - 2:TH], xTk, ['hpk0'])
            for q in range(4):
                th_, h_ = divmod(q, 2)
                dma('sp', cin2[q].rearrange("(k p) t -> p k t", p=128), aT[:, 8 * h_:8 * h_ + 8, 2 + 512 * th_:2 + 512 * th_ + 512], aTk, ['c2in%d' % q])
                S.add('pool', lambda e, q=q: e.collective_compute("AllGather", ALU.bypass, replica_groups=RG, ins=[cin2[q].opt()], outs=[cout2[q].opt()]),
                      ['c2in%d' % q], ['c2out%d' % q], dma='cc')
            S.barrier()
            tok.close()
            S.barrier()

            if upto('sgu'):
                with ExitStack() as es:
                    hown = sb(es, "hown", [128, KC, TOK], BF16)
                    for q in range(4):
                        th_, h_ = divmod(q, 2)
                        dma('sp', hown[:, 8 * h_:8 * h_ + 8, 512 * th_:512 * th_ + 512], cin2[q].rearrange("(k p) t -> p k t", p=128), ['c2in%d' % q], ['hown%d' % q])
                    hk = ['hown%d' % q for q in range(4)]
                    lng = sb(es, "lng", [128, 2, 1024], F32)
                    WT = sb(es, "WTs", [128, 4, 128], BF16)
                    bsb = sb(es, "bsb", [128, 4, 512], F32)
                    sgm = sb(es, "sgm", [128, 128], F32)
                    with ExitStack() as es2:
                        WTf = sb(es2, "WTf", [128, 4, 128], F32)
                        dma('sp', WTf[:], sguw_d, [], ['WTf'])
                        dma('sp', sgm[:], sgum_d, [], ['sgm'])
                        dma('sp', lng[:], lngb_d, [], ['lng'])
                        dma('sp', bsb[:], sgub_d, [], ['bsb'])
                        for g in range(4):
                            tt(WT[:, g, :], WTf[:, g, :], sgm[:], ALU.mult, ['WTf', 'sgm'], ['WT%d' % g])
                        S.barrier()
                    wzv = w_zd.rearrange("(kc p) n -> p kc n", p=128)
                    uT = sb(es, "uT", [128, 8, TOK], BF16)
                    vn = sb(es, "vn", [128, 8, 1024], BF16)
                    g1 = [sb(es, "g1_%d" % i, [128, 512], F32) for i in range(2)]
                    g2 = [sb(es, "g2_%d" % i, [128, 512], F32) for i in range(2)]
                    GC = 2.0 * math.sqrt(2.0 / math.pi)
                    gcount = [0]

                    def gelu_from_ps(bank, out_ap, pk, wk):
                        i = gcount[0] % 2
                        gcount[0] += 1
                        act(g1[i][:], bank, AF.Square, [pk], ['g1_%d' % i])
                        ts(g1[i][:], g1[i][:], 0.044715, 1.0, ALU.mult, ALU.add, ['g1_%d' % i], ['g1_%d' % i])
                        tt(g1[i][:], g1[i][:], bank, ALU.mult, ['g1_%d' % i, pk], ['g1_%d' % i])
                        act(g2[i][:], g1[i][:], AF.Sigmoid, ['g1_%d' % i], ['g2_%d' % i], scale=GC)
                        tt(out_ap, g2[i][:], bank, ALU.mult, ['g2_%d' % i, pk], [wk])
                    with ExitStack() as es2:
                        wz = [sb(es2, "wz%d" % i, [128, KC, 256], BF16) for i in range(3)]
                        n = 0
                        for c2 in range(4):
                            wb = c2 % 3
                            dma('pool', wz[wb][:], wzv[:, :, 256 * c2:256 * c2 + 256], [], ['wz%d' % wb])
                            for half in range(2):
                                ch = 2 * c2 + half
                                for th in range(2):
                                    bk = n % 4
                                    n += 1
                                    mms([(ps[:, bk, :], wz[wb][:, kc, 128 * half:128 * half + 128], hown[:, kc, 512 * th:512 * th + 512], kc == 0, kc == KC - 1) for kc in range(KC)],
                                        ['wz%d' % wb] + hk, ['ps%d' % bk])
                                    gelu_from_ps(ps[:, bk, :], uT[:, ch, 512 * th:512 * th + 512], 'ps%d' % bk, 'uT%d_%d' % (ch, th))
                        S.barrier()
                    with ExitStack() as es2:
                        wv2 = [sb(es2, "wv2_%d" % i, [128, KC, 512], BF16) for i in range(2)]
                        for i in range(2):
                            for hhf in range(2):
                                dma('pool', wv2[i][:, :, 256 * hhf:256 * hhf + 256], wzv[:, :, 1024 + 512 * i + 256 * hhf:1024 + 512 * i + 256 * hhf + 256], [], ['wv2_%d_%d' % (i, hhf)])
                        wvk = ['wv2_%d_%d' % (i, hhf) for i in range(2) for hhf in range(2)]
                        vg = [sb(es2, "vg%d" % i, [128, 1024], F32) for i in range(2)]
                        st = sb(es2, "lnst", [128, 16], F32)
                        junk = sb(es2, "lnjunk", [128, 1024], BF16)
                        n = 0
                        for t8 in range(8):
                            p = t8 % 2
                            for i in range(2):
                                bk = 4 + n % 4
                                n += 1
                                mms([(ps[:, bk, :], hown[:, kc, 128 * t8:128 * t8 + 128], wv2[i][:, kc, :], kc == 0, kc == KC - 1) for kc in range(KC)],
                                    wvk + hk, ['ps%d' % bk])
                                gelu_from_ps(ps[:, bk, :], vg[p][:, 512 * i:512 * i + 512], 'ps%d' % bk, 'vg%d_%d' % (p, i))
                            vk = ['vg%d_0' % p, 'vg%d_1' % p]
                            c0 = 8 * p
                            S.add('dve', lambda e, p=p, c0=c0: e.reduce_sum(out=st[:, c0:c0 + 1], in_=vg[p][:], axis=mybir.AxisListType.X), vk, ['st%d_0' % p])
                            act(junk[:], vg[p][:], AF.Square, vk, ['junk', 'st%d_1' % p], accum_out=st[:, c0 + 1:c0 + 2])
                            ts(st[:, c0 + 2:c0 + 3], st[:, c0:c0 + 1], 1.0 / 1024, None, ALU.mult, None, ['st%d_0' % p], ['st%d_2' % p])
                            tt(st[:, c0 + 3:c0 + 4], st[:, c0 + 2:c0 + 3], st[:, c0 + 2:c0 + 3], ALU.mult, ['st%d_2' % p], ['st%d_3' % p])
                            stt(st[:, c0 + 4:c0 + 5], st[:, c0 + 1:c0 + 2], 1.0 / 1024, st[:, c0 + 3:c0 + 4], ALU.mult, ALU.subtract, ['st%d_1' % p, 'st%d_3' % p], ['st%d_4' % p])
                            act(st[:, c0 + 5:c0 + 6], st[:, c0 + 4:c0 + 5], AF.Ln, ['st%d_4' % p], ['st%d_5' % p], bias=eps_t[:, 0:1])
                            act(st[:, c0 + 5:c0 + 6], st[:, c0 + 5:c0 + 6], AF.Exp, ['st%d_5' % p], ['st%d_5' % p], scale=-0.5)
                            ts(vg[p][:], vg[p][:], st[:, c0 + 2:c0 + 3], st[:, c0 + 5:c0 + 6], ALU.subtract, ALU.mult, vk + ['st%d_2' % p, 'st%d_5' % p], ['vgn%d' % p])
                            tt(vg[p][:], vg[p][:], lng[:, 0, :], ALU.mult, ['vgn%d' % p, 'lng'], ['vgn%d' % p])
                            tt(vn[:, t8, :], vg[p][:], lng[:, 1, :], ALU.add, ['vgn%d' % p, 'lng'], ['vn%d' % t8] + vk)
                        S.barrier()
                    n = 0
                    for ch in range(8):
                        g = ch // 2
                        for th in range(2):
                            bk = n % 4
                            n += 1
                            for t4 in range(4):
                                t8 = 4 * th + t4
                                mms([(ps[:, bk, 128 * t4:128 * t4 + 128], vn[:, t8, 128 * ch:128 * ch + 128], WT[:, g, :], True, True)],
                                    ['vn%d' % t8, 'WT%d' % g], ['ps%d_%d' % (bk, t4)])
                            i = n % 2
                            pk4 = ['ps%d_%d' % (bk, t4) for t4 in range(4)]
                            tt(g1[i][:], ps[:, bk, :], bsb[:, g, :], ALU.add, pk4 + ['bsb'], ['g1_%d' % i])
                            tt(odT[:, ch, 512 * th:512 * th + 512], g1[i][:], uT[:, ch, 512 * th:512 * th + 512], ALU.mult, ['g1_%d' % i, 'uT%d_%d' % (ch, th)], ['odT%d_%d' % (ch, th)] + pk4)
                    cp(hpk[:, 32:48].rearrange("p (k t) -> p k t", t=2), odT[:, :, TOK - 2:TOK], ['odT%d_1' % ch for ch in range(8)], ['hpk1'])
                    S.barrier()
                if 'odT' in dbg_out:
                    dma('sp', dbg_out['odT'], odT[:], [], ['dbg6'])

            if upto('ret'):
                with ExitStack() as es:
                    ocT = sb(es, "ocT", [128, 2, SEQ], BF16)
                    QKr = sb(es, "QKr", [128, 2, SEQ], BF16)
                    gate = sb(es, "gate", [128, 2, SEQ], BF16)
                    Vr = sb(es, "Vr", [128, 32, 256], BF16)
                    Ktm = sb(es, "Ktm", [128, 32, 128], BF16)
                    wC = sb(es, "wC", [128, KC, 1024], BF16)
                    hin = [sb(es, "hinC%d" % i, [128, KC, 512], BF16) for i in range(2)]
                    rc = sb(es, "retc", [128, 264], F32)
                    cs = [sb(es, "cs%d" % i, [128, 2, 512], F32) for i in range(2)]
                    cstm = [sb(es, "cstm%d" % i, [128, 4, 256], F32) for i in range(2)]
                    ra = sb(es, "ra", [128, 512], F32)
                    rb = sb(es, "rb", [128, 512], F32)
                    SD = [sb(es, "SD%d" % i, [128, 128], BF16) for i in range(2)]
                    Qs = [sb(es, "Qs%d" % i, [128, 128], BF16) for i in range(2)]
                    Sf = sb(es, "Sf", [128, 256], F32)
                    Sb = [sb(es, "Sb%d" % i, [128, 256], BF16) for i in range(2)]
                    yb = sb(es, "yb", [128, 2, 512], F32)
                    sqy = sb(es, "sqy", [128, 2, 512], BF16)
                    rsy = sb(es, "rsy", [128, 512], F32)
                    wcv = w_inC.rearrange("(kc p) n -> p kc n", p=128)
                    for i in range(4):
                        dma('pool', wC[:, :, 256 * i:256 * i + 256], wcv[:, :, 256 * i:256 * i + 256], [], ['wC%d' % i])
                    wck = ['wC%d' % i for i in range(4)]
                    dma('sp', rc[:], retc_d, [], ['retc'])
                    Dblk = rc[:, 0:128]
                    qdec = rc[:, 128:256]
                    kdec = rc[:, 256:257]
                    g128 = rc[:, 257:258]
                    retg = rc[:, 258:260]
                    S.add('dve', lambda e: e.memset(Sf[:], 0.0), [], ['Sf'])
                    ra2 = sb(es, "ra2", [128, 512], F32)
                    rk = [sb(es, "rk%d" % i, [128, 128], F32) for i in range(2)]

                    def loads(t8):
                        p = t8 % 2
                        r = t8 // 2
                        csl = slice(512 * (t8 % 2), 512 * (t8 % 2) + 512)
                        sl = slice(512 * t8, 512 * t8 + 512)
                        for q in range(2):
                            qq = 2 * (t8 % 2) + q
                            dma('sp', hin[p][:, 8 * q:8 * q + 8, :], cout2[qq][1024 * r:1024 * r + 1024, :].rearrange("(k p) t -> p k t", p=128), ['c2out%d' % qq], ['hinC%d_%d' % (p, q)])
                        dma('sp', cs[p][:, 0, :], cosT_d[:, sl], [], ['cs%d_0' % p])
                        dma('sp', cs[p][:, 1, :], sinT_d[:, sl], [], ['cs%d_1' % p])
                        dma('sp', cstm[p][:], cstm_d[:, 4 * t8:4 * t8 + 4, :], [], ['cstm%d' % p])

                    def inproj_groups(t8):
                        p = t8 % 2
                        sl = slice(512 * t8, 512 * t8 + 512)
                        hk = ['hinC%d_%d' % (p, q) for q in range(2)]
                        G = []

                        def g_qk(qk):
                            b0 = 2 * qk
                            for w in range(2):
                                g = 2 * qk + w
                                mms([(ps[:, b0 + w, :], wC[:, kc, 128 * g:128 * g + 128], hin[p][:, kc, :], kc == 0, kc == KC - 1) for kc in range(KC)], wck + hk, ['ps%d' % (b0 + w)])
                            tt(ra[:], ps[:, b0, :], cs[p][:, 0, :], ALU.mult, ['ps%d' % b0, 'cs%d_0' % p], ['ra'])
                            tt(rb[:], ps[:, b0 + 1, :], cs[p][:, 1, :], ALU.mult, ['ps%d' % (b0 + 1), 'cs%d_1' % p], ['rb'])
                            tt(QKr[:, qk, sl], ra[:], rb[:], ALU.add, ['ra', 'rb'], ['QKr%d_%d' % (qk, t8)])

                        def g_gate(hf):
                            g = 6 + hf
                            mms([(ps[:, hf, :], wC[:, kc, 128 * g:128 * g + 128], hin[p][:, kc, :], kc == 0, kc == KC - 1) for kc in range(KC)], wck + hk, ['ps%d' % hf])
                            act(gate[:, hf, sl], ps[:, hf, :], AF.Silu, ['ps%d' % hf], ['gate%d_%d' % (hf, t8)])

                        def g_v(sub):
                            blk = 4 * t8 + sub
                            tsl = slice(128 * sub, 128 * sub + 128)
                            mms([(ps[:, 2, 0:256], hin[p][:, kc, tsl], wC[:, kc, 512:768], kc == 0, kc == KC - 1) for kc in range(KC)], wck + hk, ['ps2'])
                            cp(Vr[:, blk, :], ps[:, 2, 0:256], ['ps2'], ['Vr%d' % blk], eng='act')

                        def g_k(sub):
                            blk = 4 * t8 + sub
                            tsl = slice(128 * sub, 128 * sub + 128)
                            mms([(ps[:, 3, 0:256], hin[p][:, kc, tsl], wC[:, kc, 256:512], kc == 0, kc == KC - 1) for kc in range(KC)], wck + hk, ['ps3'])
                            tt(rk[0][:], ps[:, 3, 0:128], cstm[p][:, sub, 0:128], ALU.mult, ['ps3', 'cstm%d' % p], ['rk0'])
                            tt(rk[1][:], ps[:, 3, 128:256], cstm[p][:, sub, 128:256], ALU.mult, ['ps3', 'cstm%d' % p], ['rk1'])
                            tt(rk[0][:], rk[0][:], rk[1][:], ALU.add, ['rk0', 'rk1'], ['rk0'])
                            ts(Ktm[:, blk, :], rk[0][:], kdec, None, ALU.mult, None, ['rk0', 'retc'], ['Ktm%d' % blk])
                        G.append(lambda: g_qk(0))
                        G.append(lambda: g_qk(1))
                        G.append(lambda: g_gate(0))
                        G.append(lambda: g_gate(1))
                        for sub in range(4):
                            G.append(lambda sub=sub: g_v(sub))
                            G.append(lambda sub=sub: g_k(sub))
                        return G

                    def rec_steps(t8):
                        R = []
                        for sub in range(4):
                            blk = 4 * t8 + sub
                            bsl = slice(128 * blk, 128 * blk + 128)
                            pb = blk % 2

                            def r_a(blk=blk, bsl=bsl, pb=pb):
                                mms([(ps[:, 4, 0:128], QKr[:, 1, bsl], QKr[:, 0, bsl], True, True)], ['QKr0_%d' % t8, 'QKr1_%d' % t8], ['ps4'])
                                tt(SD[pb][:], ps[:, 4, 0:128], Dblk, ALU.mult, ['ps4', 'retc'], ['SD%d' % pb])
                                if blk > 0:
                                    tt(Qs[pb][:], QKr[:, 0, bsl], qdec, ALU.mult, ['QKr0_%d' % t8, 'retc'], ['Qs%d' % pb])

                            def r_b(blk=blk, pb=pb, sub=sub):
                                lst = []
                                for hf in range(2):
                                    lst.append((ps[:, 5, 128 * hf:128 * hf + 128], Vr[:, blk, 128 * hf:128 * hf + 128], SD[pb][:], True, blk == 0))
                                rd = ['SD%d' % pb, 'Vr%d' % blk]
                                if blk > 0:
                                    for hf in range(2):
                                        lst.append((ps[:, 5, 128 * hf:128 * hf + 128], Sb[(blk - 1) % 2][:, 128 * hf:128 * hf + 128], Qs[pb][:], False, True))
                                    rd += ['Qs%d' % pb, 'Sb%d' % ((blk - 1) % 2)]
                                    lst = [lst[0], lst[2], lst[1], lst[3]]
                                mms(lst, rd, ['ps5'])
                                cp(yb[:, :, 128 * sub:128 * sub + 128], ps[:, 5, 0:256].rearrange("p (h t) -> p h t", h=2), ['ps5'], ['yb%d' % sub], eng='act')

                            def r_c(blk=blk):
                                mms([(ps[:, 6, 0:256], Ktm[:, blk, :], Vr[:, blk, :], True, True)], ['Ktm%d' % blk, 'Vr%d' % blk], ['ps6'])
                                stt(Sf[:], Sf[:], g128, ps[:, 6, 0:256], ALU.mult, ALU.add, ['Sf', 'ps6', 'retc'], ['Sf'])
                                cp(Sb[blk % 2][:], Sf[:], ['Sf'], ['Sb%d' % (blk % 2)])
                            R += [r_a, r_b, r_c]
                        return R

                    def norm_gate(t8):
                        sl = slice(512 * t8, 512 * t8 + 512)
                        ybk = ['yb%d' % sub for sub in range(4)]
                        act(sqy[:], yb[:], AF.Square, ybk, ['sqy'])
                        mms([(ps[:, 7, :], ones, sqy[:, hf, :], hf == 0, hf == 1) for hf in range(2)], ['sqy', 'tri'], ['ps7'])
                        rstd_from_ps(ps[:, 7, :], rsy[:], 256, ['ps7'], ['rsy'])
                        for hf in range(2):
                            stt(ra2[:], yb[:, hf, :], retg[:, hf:hf + 1], rsy[:], ALU.mult, ALU.mult, ybk + ['rsy', 'retc'], ['ra2'])
                            tt(ocT[:, hf, sl], ra2[:], gate[:, hf, sl], ALU.mult, ['ra2', 'gate%d_%d' % (hf, t8)], ['oT%d_%d' % (hf, t8)])
                        if t8 % 2 == 1 and upto('ag3'):
                            ag_slab(t8 // 2, cin3s, cout3a, ocT, 2, 'c3')

                    loads(0)
                    for t8 in range(9):
                        if t8 + 1 < 8:
                            loads(t8 + 1)
                        G = inproj_groups(t8) if t8 < 8 else []
                        R = rec_steps(t8 - 1) if t8 >= 1 else []
                        for k in range(max(len(G), len(R))):
                            if k < len(G):
                                G[k]()
                            if k < len(R):
                                R[k]()
                        if t8 >= 1:
                            norm_gate(t8 - 1)
                    if 'ocT' in dbg_out:
                        dma('sp', dbg_out['ocT'], ocT[:], ['oT%d_%d' % (hf, t8) for hf in range(2) for t8 in range(8)], ['dbg7'])
                    S.barrier()

            if upto('ag3'):
                dma('sp', cinH, hpk[:], ['hpk0', 'hpk1'], ['cinH'])
                S.add('pool', lambda e: e.collective_compute("AllGather", ALU.bypass, replica_groups=RG, ins=[cinH.opt()], outs=[coutH.opt()]),
                      ['cinH'], ['coutH'], dma='cc')
                dma('sp', hh[:], coutH[bass.ds(((rank + 3) % 4) * 128, 128), :], ['coutH'], ['hh'])
                ts(hh[:], hh[:], smallB[:, 3:4], None, ALU.mult, None, ['hh', 'smallB'], ['hh'])

            if upto('out1'):
                tok = ExitStack()
                xT = sb(tok, "xTb", [128, KC, TH], F32)
                aT = sb(tok, "aTb", [128, KC, TH], BF16)
                rs1 = sb(tok, "rs1b", [128, TH], F32)
                dma('sp', xT[:, :, 2:TH], x1s.rearrange("(kc p) t -> p kc t", p=128), ['x1s'], ['xTown'])
                cp(xT[:, :, 0:2], hh[:, 0:32].rearrange("p (k t) -> p k t", t=2), ['hh', 'xTown'], xTk)
                cv3 = cout3a.rearrange("(sk p) t -> p sk t", p=128)
                c3k = ['c3out%d' % t for t in range(4)]
                dma('sp', aT[:, 0:8, 2:TH], cv3[:, bass.ds(rank * 8, 8), :], c3k, ['aT%d' % kc for kc in range(8)])
                dma('sp', aT[:, 0:8, 0:2], cv3[:, bass.ds(((rank + 3) % 4) * 8, 8), TOK - 2:TOK], c3k, ['aTh3'])
                ts(aT[:, 0:8, 0:2], aT[:, 0:8, 0:2], smallB[:, 3:4], None, ALU.mult, None, ['aTh3', 'smallB'] + ['aT%d' % kc for kc in range(8)], ['aT%d' % kc for kc in range(8)])
                for ch in range(8):
                    cp(aT[:, 8 + ch, 2:TH], odT[:, ch, :], [], ['aT%d' % (8 + ch)], eng=('act' if ch % 2 else 'dve'))
                cp(aT[:, 8:16, 0:2], hh[:, 32:48].rearrange("p (k t) -> p k t", t=2), ['hh'] + ['aT%d' % (8 + ch) for ch in range(8)], ['aT%d' % (8 + ch) for ch in range(8)])
                S.barrier()
                out_proj(w_out1, 'o1')
                if 'xmid1' in dbg_out:
                    dma('sp', dbg_out['xmid1'].rearrange("(kc p) t -> p kc t", p=128), xT[:], xTk, ['dbg8'])
                if upto('ffn1'):
                    norm_to_aT(3, 0, TH)
                    conv_ffn(1)
                final_out(upto('final'))
                tok.close()
                S.barrier()
        S.emit()
    nc.used_inputs = used_inputs
    return nc


def _pp(v, n=KC):
    return np.ascontiguousarray(np.asarray(v, np.float32).reshape(n, 128).T)


def prep_inputs(inp):
    f = lambda k: np.asarray(inp[k], np.float32)
    x = f('x')
    cst = _consts()
    bidx = _bias_index()
    ab_in = f('ab_w_in')[0]
    ab_out = f('ab_w_out')[0]
    rel_bias = f('rel_bias')
    lam = f('diff_lambda')[0]
    subln = f('diff_subln_g')[0]
    gains = np.stack([_pp(f('norm_mix_g')[0]), _pp(f('norm_mix_g')[1]), _pp(f('norm_ffn_g')[0]), _pp(f('norm_ffn_g')[1]),
                      _pp(f('final_norm_g'))], axis=1)
    w_up_r, convp_r, w_down_r = [], [], []
    for l in range(2):
        wu = f('ffn_w_up')[l]
        w_up_r.append(np.ascontiguousarray(np.concatenate([wu[:, :DFF].reshape(D, NCH, 128), wu[:, DFF:].reshape(D, NCH, 128)], axis=2)))
        cw = f('ffn_conv_w')[l]
        cb = f('ffn_conv_b')[l]
        cp_ = np.zeros((128, NCH, 8), np.float32)
        for ag in range(2):
            for jj in range(3):
                cp_[:, :, 4 * ag + jj] = cw[jj, ag * DFF:(ag + 1) * DFF].reshape(NCH, 128).T
            cp_[:, :, 4 * ag + 3] = cb[ag * DFF:(ag + 1) * DFF].reshape(NCH, 128).T
        convp_r.append(cp_)
        w_down_r.append(np.ascontiguousarray(f('ffn_w_down')[l]))
    perm = []
    for r in range(4):
        for g in range(4):
            base = [128 * (2 * r), 128 * (2 * r + 1), 1024 + 256 * r, 1024 + 256 * r + 128][g]
            perm += list(range(base, base + 128))
    w_out0 = np.ascontiguousarray(ab_out[perm, :])
    cd_in = f('cd_w_in')[0]
    cd_out = f('cd_w_out')[0]
    sgw = f('sgu_w')[0]
    sguw = np.ascontiguousarray(sgw.transpose(2, 0, 1))
    jj = np.arange(128)[:, None]
    ii = np.arange(128)[None, :]
    sgum = ((jj // 64) <= (ii // 64)).astype(np.float32)
    lngb = np.ascontiguousarray(np.broadcast_to(np.stack([f('sgu_ln_g')[0], f('sgu_ln_b')[0]])[None], (128, 2, 1024)))
    sgub = np.ascontiguousarray(np.broadcast_to(np.tile(f('sgu_b')[0], (1, 4))[None], (128, 4, 512)))
    cos, sin = _rot_tables(None)
    cosT = np.ascontiguousarray(cos.T)
    sinT = np.ascontiguousarray(sin.T)
    cstm = np.ascontiguousarray(np.concatenate([cos, sin], -1).reshape(32, 128, 256).transpose(1, 0, 2))
    retg = f('ret_norm_g')[0]
    perm1 = list(range(2048))
    w_out1 = np.ascontiguousarray(cd_out[perm1, :])
    maps = []
    for c in range(8):
        b, j = divmod(c, 4)
        m = {}
        m['w_zd'] = np.ascontiguousarray(cd_in[:, 3072:5120])
        m['sguw'] = sguw
        m['sgum'] = sgum
        m['lngb'] = lngb
        m['sgub'] = sgub
        sw = (np.arange(128) + 64) % 128
        qcols = np.arange(128 * j, 128 * j + 128)
        kcols = 512 + qcols
        m['w_inC'] = np.ascontiguousarray(np.concatenate([cd_in[:, qcols], cd_in[:, qcols[sw]], cd_in[:, kcols], cd_in[:, kcols[sw]],
                                                          cd_in[:, 1024 + 256 * j:1024 + 256 * j + 256], cd_in[:, 2048 + 256 * j:2048 + 256 * j + 256]], axis=1))
        log_g = np.log(np.float32(1.0) - np.float32(2.0) ** np.float32(-5.0 - j)).astype(np.float32)
        sidx = np.arange(128, dtype=np.float32)
        rcst = np.zeros((128, 264), np.float32)
        sc = np.float32(128.0 ** -0.5)
        rcst[:, 0:128] = np.exp(log_g * np.abs(sidx[None, :] - sidx[:, None])) * ((jj // 64) <= (ii // 64)) * sc
        rcst[:, 128:256] = (np.exp(log_g * (sidx + 1.0)) * sc)[None, :]
        rcst[:, 256] = np.exp(log_g * (127.0 - sidx))
        rcst[:, 257] = np.exp(log_g * np.float32(128.0))
        rcst[:, 258] = retg[0:128]
        rcst[:, 259] = retg[128:256]
        m['retc'] = rcst
        m['cosT'] = cosT
        m['sinT'] = sinT
        m['cstm'] = cstm
        m['w_out1'] = w_out1
        m['xT_seq'] = np.ascontiguousarray(x[b].T)
        xo = np.zeros((D, TH), np.float32)
        xo[:, 2:] = x[b, TOK * j:TOK * j + TOK].T
        if j > 0:
            xo[:, 0:2] = x[b, TOK * j - 2:TOK * j].T
        m['xT_own'] = xo
        m['gains'] = gains
        m['tri'] = cst['tri']
        m['masks'] = cst['masks']
        h0, h1 = 2 * j, 2 * j + 1
        colsA = []
        for h in (h0, h1):
            colsA += list(range(128 * h, 128 * h + 128)) + list(range(1024 + 128 * h, 1024 + 128 * h + 128))
        for h in (h0, h1):
            colsA += list(range(2048 + 128 * h, 2048 + 128 * h + 128))
        m['w_inA'] = np.ascontiguousarray(ab_in[:, colsA])
        colsB = []
        for base in (3072, 4096):
            for cc in range(2):
                colsB += list(range(base + 256 * j + 128 * cc, base + 256 * j + 128 * cc + 128))
        colsB += list(range(5120 + 256 * j, 5120 + 256 * j + 256))
        m['w_inB'] = np.ascontiguousarray(ab_in[:, colsB])
        m['biasM'] = np.ascontiguousarray(rel_bias[bidx, j]).astype(np.float32)
        sB = np.zeros((128, 8), np.float32)
        sB[:, 0] = rel_bias[15, j]
        sB[:, 1] = subln[0:128]
        sB[:, 2] = subln[128:256]
        sB[:, 3] = 0.0 if j == 0 else 1.0
        m['smallB'] = sB
        m['lamv'] = np.ascontiguousarray(np.broadcast_to(lam.reshape(1, 512), (128, 512)))
        m['w_out0'] = w_out0
        for l in range(2):
            m['w_up%d' % l] = w_up_r[l]
            m['convp%d' % l] = convp_r[l]
            m['w_down%d' % l] = w_down_r[l]
        maps.append(m)
    return maps


_NC_CACHE = {}


def kernel(**inputs):
    maps = prep_inputs(inputs)
    if 'nc' not in _NC_CACHE:
        _NC_CACHE['nc'] = build()
    ui = _NC_CACHE['nc'].used_inputs
    maps = [{k: m[k] for k in ui} for m in maps]
    res = run_bass_kernel_spmd(_NC_CACHE['nc'], maps, core_ids=list(range(8)))
    out = np.zeros((NB, SEQ, D), np.float32)
    for c in range(8):
        b, j = divmod(c, 4)
        out[b, TOK * j:TOK * j + TOK, :] = res.results[c]['yT'].T
    return out
```

```python
import math
from contextlib import ExitStack
import numpy as np
import concourse.bass as bass
import concourse.mybir as mybir
from concourse.bass_utils import run_bass_kernel_spmd

F32 = mybir.dt.float32
BF16 = mybir.dt.bfloat16
AF = mybir.ActivationFunctionType
ALU = mybir.AluOpType

D = 2048
SEQ = 4096
NB = 2
TOK = 1024
TH = TOK + 2
KC = D // 128
DFF = 5632
NCH = DFF // 128
EPS = 1e-6
RG = [[0, 1, 2, 3], [4, 5, 6, 7]]
TT3 = [(0, 342), (342, 684), (684, 1026)]
STOP_ORDER = ['n0', 'projA', 'attA', 'attB', 'ag1', 'out0', 'norm0', 'ffn0', 'ag2', 'sgu', 'ret', 'ag3', 'out1', 'ffn1', 'final']


class Sched:
    ENGS = ('pe', 'act', 'dve', 'pool', 'sp')

    def __init__(self, nc, ndsem=8):
        self.nc = nc
        self.ndsem = ndsem
        self.streams = {e: [] for e in self.ENGS}
        self.cnt = {}
        self.seen = {e: {} for e in self.ENGS}
        self.lastw = {}
        self.readers = {}
        self.dma_idx = {e: 0 for e in self.ENGS}
        self.semkeys = []
        self.sems = {}

    def _semkey(self, k):
        if k not in self.cnt:
            self.cnt[k] = 0
            self.semkeys.append(k)
        return k

    def add(self, eng, fn, reads=(), writes=(), dma=False):
        deps = {}

        def need(d):
            for semk, val in d.items():
                if deps.get(semk, 0) < val:
                    deps[semk] = val
        for b in reads:
            need(self.lastw.get(b, {}))
        for b in writes:
            need(self.lastw.get(b, {}))
            need(self.readers.get(b, {}))
        if dma == 'cc':
            semk = self._semkey('CC')
            val = self.cnt[semk] + 1
            inc = 1
        elif dma:
            k = self.dma_idx[eng]
            self.dma_idx[eng] += 1
            semk = self._semkey('D_%s_%d' % (eng, k % self.ndsem))
            val = 16 * (k // self.ndsem + 1)
            if val > 16:
                need({semk: val - 16})
            inc = 16
        else:
            semk = self._semkey('E_' + eng)
            val = self.cnt[semk] + 1
            inc = 1
        self.cnt[semk] = val
        waits = []
        for sk, v in deps.items():
            if self.seen[eng].get(sk, 0) >= v:
                continue
            if sk == 'E_pe' and eng == 'pe':
                continue
            waits.append((sk, v))
            self.seen[eng][sk] = v
        self.streams[eng].append((waits, fn, semk, inc))
        for b in reads:
            r = self.readers.setdefault(b, {})
            if r.get(semk, 0) < val:
                r[semk] = val
        for b in writes:
            self.lastw[b] = {semk: val}
            self.readers[b] = {}
        return (semk, val)

    def barrier(self):
        for eng in self.ENGS:
            waits = []
            for sk, v in self.cnt.items():
                if v == 0 or self.seen[eng].get(sk, 0) >= v:
                    continue
                if sk == 'E_' + eng and eng in ('pe', 'sp'):
                    continue
                if sk == 'CC':
                    continue
                waits.append((sk, v))
                self.seen[eng][sk] = v
            if waits:
                self.streams[eng].append((waits, None, None, 0))

    def emit(self):
        nc = self.nc
        with ExitStack() as es:
            for k in self.semkeys:
                self.sems[k] = es.enter_context(nc.semaphore(k))
            with nc.Block() as block:
                def runner(name):
                    def run(e):
                        for waits, fn, semk, inc in self.streams[name]:
                            for sk, v in waits:
                                e.wait_ge(self.sems[sk], v)
                            if fn is not None:
                                ins = fn(e)
                                ins.then_inc(self.sems[semk], inc)
                    return run
                block.tensor(runner('pe'))
                block.scalar(runner('act'))
                block.vector(runner('dve'))
                block.gpsimd(runner('pool'))
                block.sync(runner('sp'))


def _rel_bucket_np(rel):
    nb = 16
    max_exact = 8
    ret = np.where(rel > 0, nb, 0)
    n = np.abs(rel)
    n_f = np.maximum(n, 1).astype(np.float32)
    large = max_exact + (np.log(n_f / np.float32(max_exact)) / np.float32(math.log(128 / max_exact))
                         * np.float32(nb - max_exact)).astype(np.int32)
    large = np.minimum(large, nb - 1)
    return ret + np.where(n < max_exact, n, large)


def _consts():
    c = {}
    j = np.arange(128)[:, None]
    s = np.arange(128)[None, :]
    tri = np.zeros((128, 3, 128), np.float32)
    tri[:, 0, :] = 1.0
    tri[:, 1, :] = (j > s)
    tri[:, 2, :] = (j <= s)
    c['tri'] = tri
    srow = np.arange(128)[:, None]
    t = np.arange(512)[None, :]
    mk = np.zeros((128, 8, 512), np.float32)
    for oi, o in enumerate((0, 128, 256, 384)):
        mk[:, oi, :] = ((o + srow) < t)
        mk[:, 4 + oi, :] = (((o + srow) // 64) <= (t // 64))
    c['masks'] = mk
    return c


def _bias_index():
    s = np.arange(128)[:, None]
    u = np.arange(1024)[None, :]
    return _rel_bucket_np((s - u + 384).astype(np.int32))


def _rot_tables(gamma):
    d = 128
    inv_freq = (10000.0 ** (-np.arange(0, d, 2, dtype=np.float32) / d)).astype(np.float32)
    ang = np.arange(SEQ, dtype=np.float32)[:, None] * inv_freq[None, :]
    cos = np.concatenate([np.cos(ang), np.cos(ang)], -1).astype(np.float32)
    sin = np.concatenate([-np.sin(ang), np.sin(ang)], -1).astype(np.float32)
    return cos, sin


def build(stop='final', dbg=()):
    nc = bass.Bass("TRN2", target_bir_lowering=False)
    S = Sched(nc)
    stop_i = STOP_ORDER.index(stop)

    def upto(name):
        return STOP_ORDER.index(name) <= stop_i

    used_inputs = []

    def din(name, shape, dt=F32, need='n0'):
        if not upto(need):
            return None
        used_inputs.append(name)
        return nc.dram_tensor(name, list(shape), dt, kind="ExternalInput").ap()

    def dout(name, shape, dt=F32):
        return nc.dram_tensor(name, list(shape), dt, kind="ExternalOutput").ap()

    def dint(name, shape, dt):
        return nc.dram_tensor(name, list(shape), dt).ap()

    xT_seq = din("xT_seq", [D, SEQ])
    xT_own = din("xT_own", [D, TH], need="out0")
    gains = din("gains", [128, 5, KC])
    tri_d = din("tri", [128, 3, 128])
    masks_d = din("masks", [128, 8, 512])
    w_inA = din("w_inA", [D, 768])
    w_inB = din("w_inB", [D, 768], need="attB")
    biasM_d = din("biasM", [128, 1024], need="attB")
    smallB_d = din("smallB", [128, 8])
    lamv_d = din("lamv", [128, 512], need="attB")
    w_out0 = din("w_out0", [D, D], need="out0")
    w_up = [din("w_up%d" % l, [D, NCH, 256], need="ffn%d" % l) for l in range(2)]
    convp = [din("convp%d" % l, [128, NCH, 8], need="ffn%d" % l) for l in range(2)]
    w_down = [din("w_down%d" % l, [DFF, D], need="ffn%d" % l) for l in range(2)]
    w_zd = din("w_zd", [D, 2048], need="sgu")
    sguw_d = din("sguw", [128, 4, 128], need="sgu")
    sgum_d = din("sgum", [128, 128], need="sgu")
    lngb_d = din("lngb", [128, 2, 1024], need="sgu")
    sgub_d = din("sgub", [128, 4, 512], need="sgu")
    w_inC = din("w_inC", [D, 1024], need="ret")
    retc_d = din("retc", [128, 264], need="ret")
    cosT_d = din("cosT", [128, SEQ], need="ret")
    sinT_d = din("sinT", [128, SEQ], need="ret")
    cstm_d = din("cstm", [128, 32, 256], need="ret")
    w_out1 = din("w_out1", [D, D], need="out1")
    yT = dout("yT", [D, TOK])
    dbg_out = {}
    for name, shape, dt in dbg:
        dbg_out[name] = dout("dbg_" + name, shape, dt)

    hseq = dint("hseq", [D, SEQ], BF16)
    cin1s = [dint("cin1s_%d" % t, [512, TOK], BF16) for t in range(4)]
    cout1a = dint("cout1a", [4 * 2048, TOK], BF16)
    cin2 = [dint("cin2_%d" % g, [1024, 512], BF16) for g in range(4)]
    cout2 = [dint("cout2_%d" % g, [4096, 512], BF16) for g in range(4)]
    cin3s = [dint("cin3s_%d" % t, [256, TOK], BF16) for t in range(4)]
    cout3a = dint("cout3a", [4 * 1024, TOK], BF16)
    cinH = dint("cinH", [128, 48], F32)
    coutH = dint("coutH", [512, 48], F32)
    x1s = dint("x1s", [D, TOK], F32)

    pid = nc.partition_id()
    rank = pid % 4

    def dma(q, out, in_, reads, writes):
        return S.add(q, lambda e: e.dma_start(out=out, in_=in_), reads, writes, dma=True)

    def act(out, in_, func, reads, writes, **kw):
        return S.add('act', lambda e: e.activation(out=out, in_=in_, func=func, **kw), reads, writes)

    def tt(out, in0, in1, op, reads, writes, eng='dve'):
        return S.add(eng, lambda e: e.tensor_tensor(out=out, in0=in0, in1=in1, op=op), reads, writes)

    def ts(out, in0, s1, s2, op0, op1, reads, writes, eng='dve'):
        if op1 is None:
            return S.add(eng, lambda e: e.tensor_scalar(out=out, in0=in0, scalar1=s1, scalar2=None, op0=op0), reads, writes)
        return S.add(eng, lambda e: e.tensor_scalar(out=out, in0=in0, scalar1=s1, scalar2=s2, op0=op0, op1=op1), reads, writes)

    def stt(out, in0, scalar, in1, op0, op1, reads, writes):
        return S.add('dve', lambda e: e.scalar_tensor_tensor(out=out, in0=in0, scalar=scalar, in1=in1, op0=op0, op1=op1), reads, writes)

    def cp(out, in_, reads, writes, eng='dve'):
        if eng == 'act':
            return act(out, in_, AF.Copy, reads, writes)
        return S.add(eng, lambda e: e.tensor_copy(out=out, in_=in_), reads, writes)

    def mms(lst, reads, writes):
        def fn(e):
            r = None
            for (o, l, rh, st, sp) in lst:
                r = e.matmul(o, lhsT=l, rhs=rh, start=st, stop=sp)
            return r
        return S.add('pe', fn, reads, writes)

    with ExitStack() as glob:
        uniq = [0]

        def sb(es, name, shape, dt):
            uniq[0] += 1
            return es.enter_context(nc.sbuf_tensor("s%d_%s" % (uniq[0], name), list(shape), dt))

        ps = glob.enter_context(nc.psum_tensor("ps", [128, 8, 512], F32))
        tri = sb(glob, "tri", [128, 3, 128], BF16)
        masks = sb(glob, "masks", [128, 8, 512], BF16)
        gn = sb(glob, "gn", [128, 5, KC], F32)
        smallB = sb(glob, "smallB", [128, 8], F32)
        with ExitStack() as es:
            trif = sb(es, "trif", [128, 3, 128], F32)
            mkf = sb(es, "mkf", [128, 8, 512], F32)
            dma('sp', trif[:], tri_d, [], ['trif'])
            dma('sp', mkf[:], masks_d, [], ['mkf'])
            dma('sp', gn[:], gains, [], ['gn'])
            dma('sp', smallB[:], smallB_d, [], ['smallB'])
            cp(tri[:], trif[:], ['trif'], ['tri'])
            cp(masks[:], mkf[:], ['mkf'], ['masks'])
            S.barrier()
        ones = tri[:, 0, :]
        Umat = tri[:, 1, :]
        Lmat = tri[:, 2, :]

        def rstd_from_ps(bank_ap, out_ap, n, reads, writes):
            act(out_ap, bank_ap, AF.Ln, reads, writes, scale=1.0 / n, bias=eps_t[:, 0:1])
            act(out_ap, out_ap, AF.Exp, writes, writes, scale=-0.5)

        eps_t = sb(glob, "eps_t", [128, 1], F32)
        S.add('dve', lambda e: e.memset(eps_t[:], EPS), [], ['eps'])
        S.barrier()

        xsv = xT_seq.rearrange("(kc p) t -> p kc t", p=128)
        hsv = hseq.rearrange("(kc p) t -> p kc t", p=128)
        wst = ExitStack()
        wA_t = sb(wst, "wA_t", [128, KC, 768], BF16)
        wB_t = sb(wst, "wB_t", [128, KC, 768], BF16)
        wvA = w_inA.rearrange("(kc p) n -> p kc n", p=128)
        for i in range(3):
            dma('pool', wA_t[:, :, 256 * i:256 * i + 256], wvA[:, :, 256 * i:256 * i + 256], [], ['wA%d' % i])
        mix = ExitStack()
        oT_all = sb(mix, "oT_all", [128, 4, SEQ], BF16)
        if upto('attB'):
            wvB = w_inB.rearrange("(kc p) n -> p kc n", p=128)
            for i in range(3):
                dma('pool', wB_t[:, :, 256 * i:256 * i + 256], wvB[:, :, 256 * i:256 * i + 256], ['hs16_15'], ['wB%d' % i])

        def in_proj(es, wA, QK, V, tagp):
            hbb = [sb(es, "hin%s%d" % (tagp, i), [128, KC, 512], BF16) for i in range(2)]
            wk = ['w%s%d' % (tagp, i) for i in range(3)]
            ev = 0
            for t8 in range(8):
                p = t8 % 2
                sl = slice(512 * t8, 512 * t8 + 512)
                dma('sp', hbb[p][:], hsv[:, :, sl], ['hs16_%d' % (2 * t8), 'hs16_%d' % (2 * t8 + 1)], ['hin%d' % p])
                for g in range(4):
                    bk = ev % 4
                    mms([(ps[:, bk, :], wA[:, kc, 128 * g:128 * g + 128], hbb[p][:, kc, :], kc == 0, kc == KC - 1) for kc in range(KC)],
                        ['hin%d' % p] + wk, ['ps%d' % bk])
                    cp(QK[:, g, sl], ps[:, bk, :], ['ps%d' % bk], ['QK%d_%d' % (g, t8)], eng=('act' if ev % 2 else 'dve'))
                    ev += 1
                for sub in range(4):
                    bk = ev % 4
                    mms([(ps[:, bk, 0:256], hbb[p][:, kc, 128 * sub:128 * sub + 128], wA[:, kc, 512:768], kc == 0, kc == KC - 1) for kc in range(KC)],
                        ['hin%d' % p] + wk, ['ps%d' % bk])
                    cp(V[:, 4 * t8 + sub, :], ps[:, bk, 0:256], ['ps%d' % bk], ['V%d' % (4 * t8 + sub)], eng=('act' if ev % 2 else 'dve'))
                    ev += 1

        scl = 128.0 ** -0.5

        def ag_slab(t, cin, cout_all, src, ng, tagp):
            rows = ng * 128 * 4
            dma('sp', cin[t].rearrange("(g p) t -> p g t", p=128), src[:, :, TOK * t:TOK * t + TOK],
                ['oT%d_%d' % (g, qt) for g in range(ng) for qt in (2 * t, 2 * t + 1)], ['%sin%d' % (tagp, t)])
            S.add('pool', lambda e: e.collective_compute("AllGather", ALU.bypass, replica_groups=RG, ins=[cin[t].opt()],
                                                         outs=[cout_all[rows * t:rows * t + rows, :].opt()]),
                  ['%sin%d' % (tagp, t)], ['%sout%d' % (tagp, t)], dma='cc')

        if upto('projA'):
            with ExitStack() as es:
                QK = sb(es, "QKa", [128, 4, SEQ], BF16)
                V = sb(es, "Va", [128, 32, 256], BF16)
                with ExitStack() as es2:
                    xs = [sb(es2, "xs%d" % i, [128, KC, 256], F32) for i in range(2)]
                    sq = sb(es2, "sq", [128, KC, 256], BF16)
                    hb = [sb(es2, "hb%d" % i, [128, KC, 256], BF16) for i in range(2)]
                    rs = [sb(es2, "rs%d" % i, [128, 256], F32) for i in range(2)]
                    wkA = ['wA%d' % i for i in range(3)]
                    dma('sp', xs[0][:], xsv[:, :, 0:256], [], ['xs0'])
                    dma('sp', xs[1][:], xsv[:, :, 256:512], [], ['xs1'])
                    evc = [0]

                    def stageN(t16):
                        p = t16 % 2
                        sl = slice(256 * t16, 256 * t16 + 256)
                        act(sq[:], xs[p][:], AF.Square, ['xs%d' % p], ['sq'])
                        mms([(ps[:, p, 0:256], ones, sq[:, kc, :], kc == 0, kc == KC - 1) for kc in range(KC)], ['sq', 'tri'], ['ps%d' % p])
                        rstd_from_ps(ps[:, p, 0:256], rs[p][:], D, ['ps%d' % p], ['rs%d' % p])
                        for kc in range(KC):
                            stt(hb[p][:, kc, :], xs[p][:, kc, :], gn[:, 0, kc:kc + 1], rs[p][:], ALU.mult, ALU.mult,
                                ['xs%d' % p, 'rs%d' % p, 'gn'], ['hb%d_%d' % (p, kc)])
                        if t16 + 2 < 16:
                            dma('sp', xs[p][:], xsv[:, :, 256 * (t16 + 2):256 * (t16 + 2) + 256], [], ['xs%d' % p])
                        dma('sp', hsv[:, :, sl], hb[p][:], ['hb%d_%d' % (p, kc) for kc in range(KC)], ['hs16_%d' % t16])

                    def stageP(t16):
                        p = t16 % 2
                        sl = slice(256 * t16, 256 * t16 + 256)
                        hbk = ['hb%d_%d' % (p, kc) for kc in range(KC)]
                        for g in range(4):
                            ev = evc[0]
                            bk = 2 + ev % 6
                            mms([(ps[:, bk, 0:256], wA_t[:, kc, 128 * g:128 * g + 128], hb[p][:, kc, :], kc == 0, kc == KC - 1) for kc in range(KC)],
                                hbk + wkA, ['ps%d' % bk])
                            cp(QK[:, g, sl], ps[:, bk, 0:256], ['ps%d' % bk], ['QK%d_%d' % (g, t16)], eng=('act' if ev % 2 else 'dve'))
                            evc[0] += 1
                        for sub in range(2):
                            ev = evc[0]
                            bk = 2 + ev % 6
                            mms([(ps[:, bk, 0:256], hb[p][:, kc, 128 * sub:128 * sub + 128], wA_t[:, kc, 512:768], kc == 0, kc == KC - 1) for kc in range(KC)],
                                hbk + wkA, ['ps%d' % bk])
                            cp(V[:, 2 * t16 + sub, :], ps[:, bk, 0:256], ['ps%d' % bk], ['V%d' % (2 * t16 + sub)], eng=('act' if ev % 2 else 'dve'))
                            evc[0] += 1

                    stageN(0)
                    for t16 in range(16):
                        if t16 + 1 < 16:
                            stageN(t16 + 1)
                        stageP(t16)
                    S.barrier()
                if 'QKa' in dbg_out:
                    dma('sp', dbg_out['QKa'], QK[:], [], ['dbg1'])
                    dma('sp', dbg_out['Va'], V[:], [], ['dbg2'])
                if upto('attA'):
                    with ExitStack() as es2:
                        Eb = [[sb(es2, "Eb%d%d" % (h, q), [128, 512], F32) for q in range(2)] for h in range(2)]
                        SPf = [[sb(es2, "SPf%d%d" % (h, q), [128, 512], F32) for q in range(2)] for h in range(2)]
                        SPb = [[sb(es2, "SPb%d%d" % (h, q), [128, 512], BF16) for q in range(2)] for h in range(2)]
                        T1 = [[sb(es2, "T1%d%d" % (h, q), [128, 512], F32) for q in range(2)] for h in range(2)]
                        Wb = [[sb(es2, "Wb%d%d" % (h, q), [128, 512], BF16) for q in range(2)] for h in range(2)]
                        blocks = [(qt, bi, i) for qt in range(8) for bi, i in enumerate(range(4 * qt + 3, -1, -1))]

                        def A1(n):
                            qt, bi, i = blocks[n]
                            q = n % 2
                            qsl = slice(512 * qt, 512 * qt + 512)
                            ksl = slice(128 * i, 128 * i + 128)
                            for h in range(2):
                                zb = 2 * q + h
                                mms([(ps[:, zb, :], QK[:, 2 * h + 1, ksl], QK[:, 2 * h, qsl], True, True)], [], ['ps%d' % zb])
                            for h in range(2):
                                zb = 2 * q + h
                                kq = '%d%d' % (h, q)
                                act(Eb[h][q][:], ps[:, zb, :], AF.Exp, ['ps%d' % zb], ['Eb' + kq], scale=scl)
                                act(SPf[h][q][:], Eb[h][q][:], AF.Ln, ['Eb' + kq], ['SPf' + kq], bias=1.0)

                        def A2(n):
                            qt, bi, i = blocks[n]
                            q = n % 2
                            o = 128 * i - 512 * qt
                            diag = o >= 0
                            first = bi == 0
                            for h in range(2):
                                kq = '%d%d' % (h, q)
                                if diag:
                                    tt(SPb[h][q][:], SPf[h][q][:], masks[:, o // 128, :], ALU.mult, ['SPf' + kq, 'masks'], ['SPb' + kq])
                                else:
                                    cp(SPb[h][q][:], SPf[h][q][:], ['SPf' + kq], ['SPb' + kq])
                            for h in range(2):
                                kq = '%d%d' % (h, q)
                                mms([(ps[:, 4 + h, :], Umat, SPb[h][q][:], first, True)], ['SPb' + kq, 'tri'], ['ps%d' % (4 + h)])
                            for h in range(2):
                                zb = 2 * q + h
                                kq = '%d%d' % (h, q)
                                stt(T1[h][q][:], ps[:, zb, :], scl, SPf[h][q][:], ALU.mult, ALU.subtract, ['ps%d' % zb, 'SPf' + kq], ['T1' + kq])
                            for h in range(2):
                                kq = '%d%d' % (h, q)
                                tt(T1[h][q][:], T1[h][q][:], ps[:, 4 + h, :], ALU.subtract, ['T1' + kq, 'ps%d' % (4 + h)], ['T1' + kq])
                                mms([(ps[:, 4 + h, :], Lmat, SPb[h][q][:], False, True)], ['SPb' + kq, 'tri'], ['ps%d' % (4 + h)])

                        def A3(n):
                            qt, bi, i = blocks[n]
                            q = n % 2
                            qsl = slice(512 * qt, 512 * qt + 512)
                            o = 128 * i - 512 * qt
                            diag = o >= 0
                            first = bi == 0
                            last = i == 0
                            for h in range(2):
                                kq = '%d%d' % (h, q)
                                act(Wb[h][q][:], T1[h][q][:], AF.Exp, ['T1' + kq], ['Wb' + kq])
                                if diag:
                                    tt(Wb[h][q][:], Wb[h][q][:], masks[:, o // 128, :], ALU.mult, ['Wb' + kq, 'masks'], ['Wb' + kq])
                            for h in range(2):
                                kq = '%d%d' % (h, q)
                                mms([(ps[:, 6 + h, :], V[:, i, 128 * h:128 * h + 128], Wb[h][q][:], first, last)], ['Wb' + kq], ['ps%d' % (6 + h)])
                            if last:
                                for h in range(2):
                                    cp(oT_all[:, h, qsl], ps[:, 6 + h, :], ['ps%d' % (6 + h)], ['oT%d_%d' % (h, qt)], eng='act')

                        NBk = len(blocks)
                        for n in range(NBk + 2):
                            if n < NBk:
                                A1(n)
                            if 0 <= n - 1 < NBk:
                                A2(n - 1)
                            if 0 <= n - 2 < NBk:
                                A3(n - 2)
                        S.barrier()
        if 'oT' in dbg_out and not upto('attB'):
            dma('sp', dbg_out['oT'], oT_all[:], ['oT%d_%d' % (h, qt) for h in range(2) for qt in range(8)], ['dbg3'])

        if upto('attB'):
            with ExitStack() as es:
                QK = sb(es, "QKb", [128, 4, SEQ], BF16)
                V = sb(es, "Vb", [128, 32, 256], BF16)
                with ExitStack() as es2:
                    in_proj(es2, wB_t, QK, V, "B")
                    S.barrier()
                with ExitStack() as es2:
                    biasM = sb(es2, "biasM", [128, 1024], F32)
                    lamv = sb(es2, "lamv", [128, 512], F32)
                    lt = sb(es2, "lt", [128, 8], F32)
                    dma('sp', biasM[:], biasM_d, [], ['biasM'])
                    dma('sp', lamv[:], lamv_d, [], ['lamv'])
                    lam_init = 0.8 - 0.6 * math.exp(-0.3 * 0)
                    lp = sb(es2, "lp", [128, 256], F32)
                    tt(lp[:, 0:128], lamv[:, 0:128], lamv[:, 128:256], ALU.mult, ['lamv'], ['lp0'])
                    tt(lp[:, 128:256], lamv[:, 256:384], lamv[:, 384:512], ALU.mult, ['lamv'], ['lp1'])
                    S.add('dve', lambda e: e.reduce_sum(out=lt[:, 0:1], in_=lp[:, 0:128], axis=mybir.AxisListType.X), ['lp0'], ['lt0'])
                    S.add('dve', lambda e: e.reduce_sum(out=lt[:, 1:2], in_=lp[:, 128:256], axis=mybir.AxisListType.X), ['lp1'], ['lt1'])
                    act(lt[:, 2:4], lt[:, 0:2], AF.Exp, ['lt0', 'lt1'], ['lt23'])
                    tt(lt[:, 4:5], lt[:, 3:4], lt[:, 2:3], ALU.subtract, ['lt23'], ['lt4'])
                    ts(lt[:, 4:5], lt[:, 4:5], -lam_init, None, ALU.add, None, ['lt4'], ['lt4'])
                    neglam = lt[:, 4:5]
                    ts(lt[:, 5:7], smallB[:, 1:3], 1.0 - lam_init, None, ALU.mult, None, ['smallB'], ['lt56'])
                    gsub = lt[:, 5:7]
                    bconst = smallB[:, 0:1]
                    Tn = [sb(es2, "Tn%d%d" % (c, q), [128, 512], F32) for c in range(2) for q in range(2)]
                    Ebf = [sb(es2, "Ebf%d%d" % (c, q), [128, 512], BF16) for c in range(2) for q in range(2)]
                    Esum = [sb(es2, "Esum%d" % c, [128, 512], F32) for c in range(2)]
                    Esb = [sb(es2, "Esb%d" % c, [128, 512], BF16) for c in range(2)]
                    rD = [sb(es2, "rD%d" % c, [128, 512], F32) for c in range(2)]
                    oh = [sb(es2, "oh%d" % c, [128, 512], F32) for c in range(2)]
                    tA = sb(es2, "tA", [128, 512], F32)
                    sqh = [sb(es2, "sqh%d" % c, [128, 512], BF16) for c in range(2)]
                    rsb = sb(es2, "rsb", [128, 512], F32)
                    blocksB = []
                    for qt in range(8):
                        allb = list(range(4 * qt + 3, -1, -1))
                        nearb = [i for i in allb if 128 * i - 512 * qt >= -128]
                        farb = [i for i in allb if 128 * i - 512 * qt < -128]
                        order = []
                        while nearb or farb:
                            if farb:
                                order.append(farb.pop(0))
                            if nearb:
                                order.append(nearb.pop(0))
                        for pos, i in enumerate(order):
                            blocksB.append((qt, pos, i, len(order)))

                    def B1(n):
                        qt, bi, i, nbq = blocksB[n]
                        q = n % 2
                        qsl = slice(512 * qt, 512 * qt + 512)
                        o = 128 * i - 512 * qt
                        near = o >= -128
                        diag = o >= 0
                        ksl = slice(128 * i, 128 * i + 128)
                        for c in range(2):
                            zb = 2 * q + c
                            mms([(ps[:, zb, :], QK[:, 2 + c, ksl], QK[:, c, qsl], True, True)], [], ['ps%d' % zb])
                        for c in range(2):
                            zb = 2 * q + c
                            kq = '%d%d' % (c, q)
                            E = Ebf[2 * c + q]
                            if near:
                                T = Tn[2 * c + q]
                                u0 = 384 - o
                                stt(T[:], ps[:, zb, :], scl, biasM[:, u0:u0 + 512], ALU.mult, ALU.add, ['ps%d' % zb, 'biasM'], ['Tn' + kq])
                                act(E[:], T[:], AF.Exp, ['Tn' + kq], ['Ebf' + kq])
                                if diag:
                                    tt(E[:], E[:], masks[:, 4 + o // 128, :], ALU.mult, ['Ebf' + kq, 'masks'], ['Ebf' + kq])
                            else:
                                act(E[:], ps[:, zb, :], AF.Exp, ['ps%d' % zb, 'smallB'], ['Ebf' + kq], scale=scl, bias=bconst)

                    def B2(n):
                        qt, bi, i, nbq = blocksB[n]
                        q = n % 2
                        qsl = slice(512 * qt, 512 * qt + 512)
                        first = bi == 0
                        last = bi == nbq - 1
                        for c in range(2):
                            kq = '%d%d' % (c, q)
                            E = Ebf[2 * c + q]
                            mms([(ps[:, 4 + 2 * c + hf, :], V[:, i, 128 * hf:128 * hf + 128], E[:], first, last) for hf in range(2)],
                                ['Ebf' + kq], ['ps%d' % (4 + 2 * c), 'ps%d' % (5 + 2 * c)])
                            eg = 'dve'
                            if first:
                                cp(Esum[c][:], E[:], ['Ebf' + kq], ['Esum%d' % c], eng=eg)
                            else:
                                tt(Esum[c][:], Esum[c][:], E[:], ALU.add, ['Ebf' + kq, 'Esum%d' % c], ['Esum%d' % c], eng=eg)
                        if last:
                            for c in range(2):
                                cp(Esb[c][:], Esum[c][:], ['Esum%d' % c], ['Esb%d' % c])
                                mms([(ps[:, c, :], ones, Esb[c][:], True, True)], ['Esb%d' % c, 'tri'], ['ps%d' % c])
                                S.add('dve', lambda e, c=c: e.reciprocal(out=rD[c][:], in_=ps[:, c, :]), ['ps%d' % c], ['rD%d' % c])
                            for hf in range(2):
                                tt(tA[:], ps[:, 4 + hf, :], rD[0][:], ALU.mult, ['ps%d' % (4 + hf), 'rD0'], ['tA'])
                                tt(oh[hf][:], ps[:, 6 + hf, :], rD[1][:], ALU.mult, ['ps%d' % (6 + hf), 'rD1'], ['oh%d' % hf])
                                stt(oh[hf][:], oh[hf][:], neglam, tA[:], ALU.mult, ALU.add, ['oh%d' % hf, 'tA', 'lt4'], ['oh%d' % hf])
                                act(sqh[hf][:], oh[hf][:], AF.Square, ['oh%d' % hf], ['sqh%d' % hf])
                            mms([(ps[:, 2, :], ones, sqh[hf][:], hf == 0, hf == 1) for hf in range(2)], ['sqh0', 'sqh1', 'tri'], ['ps2'])
                            rstd_from_ps(ps[:, 2, :], rsb[:], 256, ['ps2'], ['rsb'])
                            for hf in range(2):
                                stt(oT_all[:, 2 + hf, qsl], oh[hf][:], gsub[:, hf:hf + 1], rsb[:], ALU.mult, ALU.mult,
                                    ['oh%d' % hf, 'rsb', 'lt56'], ['oT%d_%d' % (2 + hf, qt)])
                            if qt % 2 == 1 and upto('ag1'):
                                ag_slab(qt // 2, cin1s, cout1a, oT_all, 4, 'c1')

                    for n in range(len(blocksB) + 1):
                        if n < len(blocksB):
                            B1(n)
                        if n > 0:
                            B2(n - 1)
                    S.barrier()
        if 'oT' in dbg_out and upto('attB'):
            dma('sp', dbg_out['oT'], oT_all[:], ['oT%d_%d' % (h, qt) for h in range(4) for qt in range(8)], ['dbg3'])

        mix.close()
        wst.close()
        S.barrier()

        hpk = sb(glob, "hpk", [128, 48], F32)
        hh = sb(glob, "hh", [128, 48], F32)
        odT = sb(glob, "odT", [128, 8, TOK], BF16)
        tok = ExitStack()
        xT = sb(tok, "xT", [128, KC, TH], F32)
        aT = sb(tok, "aT", [128, KC, TH], BF16)
        rs1 = sb(tok, "rs1", [128, TH], F32)

        def out_proj(w_d, lname):
            with ExitStack() as es:
                wo = [sb(es, "wo%d" % i, [128, KC, 256], BF16) for i in range(3)]
                wv = w_d.rearrange("(kc p) n -> p kc n", p=128)
                n = 0
                for c2 in range(8):
                    wb = c2 % 3
                    dma('pool', wo[wb][:], wv[:, :, 256 * c2:256 * c2 + 256], [], ['wo%d' % wb])
                    for half in range(2):
                        cc = 2 * c2 + half
                        for (a, b) in TT3:
                            bk = n % 8
                            n += 1
                            mms([(ps[:, bk, 0:b - a], wo[wb][:, kc, 128 * half:128 * half + 128], aT[:, kc, a:b], kc == 0, kc == KC - 1) for kc in range(KC)],
                                ['wo%d' % wb] + ['aT%d' % kc for kc in range(KC)], ['ps%d' % bk])
                            tt(xT[:, cc, a:b], xT[:, cc, a:b], ps[:, bk, 0:b - a], ALU.add, ['ps%d' % bk, 'xT%d' % cc], ['xT%d' % cc])
                S.barrier()

        def norm_to_aT(gi, t0, t1):
            with ExitStack() as es:
                sq = sb(es, "sqn", [128, KC, 342], BF16)
                tiles = [(a, min(a + 342, t1)) for a in range(t0, t1, 342)]
                for ti, (a, b) in enumerate(tiles):
                    bk = ti % 8
                    act(sq[:, :, 0:b - a], xT[:, :, a:b], AF.Square, ['xT%d' % kc for kc in range(KC)], ['sqn'])
                    mms([(ps[:, bk, 0:b - a], ones, sq[:, kc, 0:b - a], kc == 0, kc == KC - 1) for kc in range(KC)], ['sqn', 'tri'], ['ps%d' % bk])
                    rstd_from_ps(ps[:, bk, 0:b - a], rs1[:, a:b], D, ['ps%d' % bk], ['rs1_%d' % ti])
                    for kc in range(KC):
                        stt(aT[:, kc, a:b], xT[:, kc, a:b], gn[:, gi, kc:kc + 1], rs1[:, a:b], ALU.mult, ALU.mult,
                            ['xT%d' % kc, 'rs1_%d' % ti, 'gn'], ['aT%d' % kc])
                S.barrier()

        def conv_ffn(l):
            GK = 4
            with ExitStack() as es:
                wu = [sb(es, "wu%d" % i, [128, KC, 256], BF16) for i in range(4)]
                cpar = sb(es, "cpar", [128, NCH, 8], F32)
                Ur = [sb(es, "Ur%d" % i, [128, TH], F32) for i in range(2)]
                Cc = [sb(es, "Cc%d" % i, [128, TOK], F32) for i in range(2)]
                gat = [sb(es, "gat%d" % i, [128, GK, TOK], BF16) for i in range(2)]
                wd = [sb(es, "wd%d" % i, [128, GK, 256], BF16) for i in range(4)]
                dma('sp', cpar[:], convp[l], [], ['cpar'])
                wdv = w_down[l].rearrange("(c p) n -> p c n", p=128)
                aTk = ['aT%d' % kc for kc in range(KC)]
                nw = 0
                nd = 0
                for kg in range(NCH // GK):
                    gp = kg % 2
                    for ci in range(GK):
                        c = kg * GK + ci
                        wb = nw % 4
                        nw += 1
                        dma('pool', wu[wb][:], w_up[l][:, c, :].rearrange("(kc p) n -> p kc n", p=128), [], ['wu%d' % wb])
                        for ag in range(2):
                            b0 = 3 * ag
                            for ti, (a, b) in enumerate(TT3):
                                mms([(ps[:, b0 + ti, 0:b - a], wu[wb][:, kc, 128 * ag:128 * ag + 128], aT[:, kc, a:b], kc == 0, kc == KC - 1) for kc in range(KC)],
                                    ['wu%d' % wb] + aTk, ['ps%d' % (b0 + ti)])
                            pk = ['ps%d' % (b0 + ti) for ti in range(3)]
                            act(Ur[ag][:].rearrange("p (a b) -> p a b", a=3), ps[:, b0:b0 + 3, 0:342], AF.Copy, pk, ['Ur%d' % ag])
                            pb = 4 * ag
                            ts(Cc[ag][:], Ur[ag][:, 2:TH], cpar[:, c, pb + 2:pb + 3], cpar[:, c, pb + 3:pb + 4], ALU.mult, ALU.add,
                               ['Ur%d' % ag, 'cpar'], ['Cc%d' % ag])
                            stt(Cc[ag][:], Ur[ag][:, 1:TH - 1], cpar[:, c, pb + 1:pb + 2], Cc[ag][:], ALU.mult, ALU.add, ['Ur%d' % ag, 'Cc%d' % ag, 'cpar'], ['Cc%d' % ag])
                            stt(Cc[ag][:], Ur[ag][:, 0:TH - 2], cpar[:, c, pb + 0:pb + 1], Cc[ag][:], ALU.mult, ALU.add, ['Ur%d' % ag, 'Cc%d' % ag, 'cpar'], ['Cc%d' % ag])
                        act(Cc[1][:], Cc[1][:], AF.Silu, ['Cc1'], ['Cc1'])
                        tt(gat[gp][:, ci, :], Cc[0][:], Cc[1][:], ALU.mult, ['Cc0', 'Cc1'], ['gat%d_%d' % (gp, ci)])
                    gk = ['gat%d_%d' % (gp, ci) for ci in range(GK)]
                    for c2 in range(8):
                        wb = nd % 4
                        nd += 1
                        dma('pool', wd[wb][:], wdv[:, kg * GK:kg * GK + GK, 256 * c2:256 * c2 + 256], [], ['wd%d' % wb])
                        for half in range(2):
                            cc = 2 * c2 + half
                            for th in range(2):
                                bk = (6, 7, 4, 5)[(2 * half + th) % 4]
                                mms([(ps[:, bk, :], wd[wb][:, ci, 128 * half:128 * half + 128], gat[gp][:, ci, 512 * th:512 * th + 512], ci == 0, ci == GK - 1) for ci in range(GK)],
                                    ['wd%d' % wb] + gk, ['ps%d' % bk])
                                sl = slice(2 + 512 * th, 2 + 512 * th + 512)
                                tt(xT[:, cc, sl], xT[:, cc, sl], ps[:, bk, :], ALU.add, ['ps%d' % bk, 'xT%d' % cc], ['xT%d' % cc])
                S.barrier()

        xTk = ['xT%d' % kc for kc in range(KC)]
        aTk = ['aT%d' % kc for kc in range(KC)]
        if upto('out0'):
            dma('sp', xT[:], xT_own.rearrange("(kc p) t -> p kc t", p=128), [], xTk)
            cv = cout1a.rearrange("(sk p) t -> p sk t", p=128)
            c1k = ['c1out%d' % t for t in range(4)]
            dma('sp', aT[:, :, 2:TH], cv[:, bass.ds(rank * 16, 16), :], c1k, aTk)
            dma('sp', aT[:, :, 0:2], cv[:, bass.ds(((rank + 3) % 4) * 16, 16), TOK - 2:TOK], c1k, ['aTh'])
            ts(aT[:, :, 0:2], aT[:, :, 0:2], smallB[:, 3:4], None, ALU.mult, None, ['aTh', 'smallB'] + aTk, aTk)
            out_proj(w_out0, 'o0')
            if 'xmid0' in dbg_out:
                dma('sp', dbg_out['xmid0'].rearrange("(kc p) t -> p kc t", p=128), xT[:], xTk, ['dbg4'])
        if upto('norm0'):
            norm_to_aT(2, 0, TH)
        if upto('ffn0'):
            conv_ffn(0)
            if 'x1' in dbg_out:
                dma('sp', dbg_out['x1'].rearrange("(kc p) t -> p kc t", p=128), xT[:], xTk, ['dbg5'])


        def final_out(normed):
            with ExitStack() as es:
                yo = sb(es, "yo", [128, KC, TOK], F32)
                if normed:
                    sq = sb(es, "sqf", [128, KC, 512], BF16)
                    for th in range(2):
                        sl = slice(2 + 512 * th, 2 + 512 * th + 512)
                        act(sq[:], xT[:, :, sl], AF.Square, xTk, ['sqf'])
                        mms([(ps[:, th, :], ones, sq[:, kc, :], kc == 0, kc == KC - 1) for kc in range(KC)], ['sqf', 'tri'], ['ps%d' % th])
                        rstd_from_ps(ps[:, th, :], rs1[:, sl], D, ['ps%d' % th], ['rsf%d' % th])
                        for kc in range(KC):
                            stt(yo[:, kc, 512 * th:512 * th + 512], xT[:, kc, sl], gn[:, 4, kc:kc + 1], rs1[:, sl], ALU.mult, ALU.mult,
                                ['xT%d' % kc, 'rsf%d' % th, 'gn'], ['yo%d_%d' % (kc, th)])
                    yk = ['yo%d_%d' % (kc, th) for kc in range(KC) for th in range(2)]
                else:
                    for kc in range(KC):
                        cp(yo[:, kc, :], xT[:, kc, 2:TH], ['xT%d' % kc], ['yo%d' % kc], eng=('act' if kc % 2 else 'dve'))
                    yk = ['yo%d' % kc for kc in range(KC)]
                dma('sp', yT.rearrange("(kc p) t -> p kc t", p=128), yo[:], yk, ['yT'])
                S.barrier()

        if not upto('ag2'):
            final_out(False)
            tok.close()
            S.barrier()
        else:
            norm_to_aT(1, 2, TH)
            dma('sp', x1s.rearrange("(kc p) t -> p kc t", p=128), xT[:, :, 2:TH], xTk, ['x1s'])
            cp(hpk[:, 0:32].rearrange("p (k t) -> p k t", t=2), xT[:, :, TH - 2:TH], xTk, ['hpk0'])
            for q in range(4):
                th_, h_ = divmod(q, 2)
                dma('sp', cin2[q].rearrange("(k p) t -> p k t", p=128), aT[:, 8 * h_:8 * h_ + 8, 2 + 512 * th_:2 + 512 * th_ + 512], aTk, ['c2in%d' % q])
                S.add('pool', lambda e, q=q: e.collective_compute("AllGather", ALU.bypass, replica_groups=RG, ins=[cin2[q].opt()], outs=[cout2[q].opt()]),
                      ['c2in%d' % q], ['c2out%d' % q], dma='cc')
            S.barrier()
            tok.close()
            S.barrier()

            if upto('sgu'):
                with ExitStack() as es:
                    hown = sb(es, "hown", [128, KC, TOK], BF16)
                    for q in range(4):
                        th_, h_ = divmod(q, 2)
                        dma('sp', hown[:, 8 * h_:8 * h_ + 8, 512 * th_:512 * th_ + 512], cin2[q].rearrange("(k p) t -> p k t", p=128), ['c2in%d' % q], ['hown%d' % q])
                    hk = ['hown%d' % q for q in range(4)]
                    lng = sb(es, "lng", [128, 2, 1024], F32)
                    WT = sb(es, "WTs", [128, 4, 128], BF16)
                    bsb = sb(es, "bsb", [128, 4, 512], F32)
                    sgm = sb(es, "sgm", [128, 128], F32)
                    with ExitStack() as es2:
                        WTf = sb(es2, "WTf", [128, 4, 128], F32)
                        dma('sp', WTf[:], sguw_d, [], ['WTf'])
                        dma('sp', sgm[:], sgum_d, [], ['sgm'])
                        dma('sp', lng[:], lngb_d, [], ['lng'])
                        dma('sp', bsb[:], sgub_d, [], ['bsb'])
                        for g in range(4):
                            tt(WT[:, g, :], WTf[:, g, :], sgm[:], ALU.mult, ['WTf', 'sgm'], ['WT%d' % g])
                        S.barrier()
                    wzv = w_zd.rearrange("(kc p) n -> p kc n", p=128)
                    uT = sb(es, "uT", [128, 8, TOK], BF16)
                    vn = sb(es, "vn", [128, 8, 1024], BF16)
                    g1 = [sb(es, "g1_%d" % i, [128, 512], F32) for i in range(2)]
                    g2 = [sb(es, "g2_%d" % i, [128, 512], F32) for i in range(2)]
                    GC = 2.0 * math.sqrt(2.0 / math.pi)
                    gcount = [0]

                    def gelu_from_ps(bank, out_ap, pk, wk):
                        i = gcount[0] % 2
                        gcount[0] += 1
                        act(g1[i][:], bank, AF.Square, [pk], ['g1_%d' % i])
                        ts(g1[i][:], g1[i][:], 0.044715, 1.0, ALU.mult, ALU.add, ['g1_%d' % i], ['g1_%d' % i])
                        tt(g1[i][:], g1[i][:], bank, ALU.mult, ['g1_%d' % i, pk], ['g1_%d' % i])
                        act(g2[i][:], g1[i][:], AF.Sigmoid, ['g1_%d' % i], ['g2_%d' % i], scale=GC)
                        tt(out_ap, g2[i][:], bank, ALU.mult, ['g2_%d' % i, pk], [wk])
                    with ExitStack() as es2:
                        wz = [sb(es2, "wz%d" % i, [128, KC, 256], BF16) for i in range(3)]
                        n = 0
                        for c2 in range(4):
                            wb = c2 % 3
                            dma('pool', wz[wb][:], wzv[:, :, 256 * c2:256 * c2 + 256], [], ['wz%d' % wb])
                            for half in range(2):
                                ch = 2 * c2 + half
                                for th in range(2):
                                    bk = n % 4
                                    n += 1
                                    mms([(ps[:, bk, :], wz[wb][:, kc, 128 * half:128 * half + 128], hown[:, kc, 512 * th:512 * th + 512], kc == 0, kc == KC - 1) for kc in range(KC)],
                                        ['wz%d' % wb] + hk, ['ps%d' % bk])
                                    gelu_from_ps(ps[:, bk, :], uT[:, ch, 512 * th:512 * th + 512], 'ps%d' % bk, 'uT%d_%d' % (ch, th))
                        S.barrier()
                    with ExitStack() as es2:
                        wv2 = [sb(es2, "wv2_%d" % i, [128, KC, 512], BF16) for i in range(2)]
                        for i in range(2):
                            for hhf in range(2):
                                dma('pool', wv2[i][:, :, 256 * hhf:256 * hhf + 256], wzv[:, :, 1024 + 512 * i + 256 * hhf:1024 + 512 * i + 256 * hhf + 256], [], ['wv2_%d_%d' % (i, hhf)])
                        wvk = ['wv2_%d_%d' % (i, hhf) for i in range(2) for hhf in range(2)]
                        vg = [sb(es2, "vg%d" % i, [128, 1024], F32) for i in range(2)]
                        st = sb(es2, "lnst", [128, 16], F32)
                        junk = sb(es2, "lnjunk", [128, 1024], BF16)
                        n = 0
                        for t8 in range(8):
                            p = t8 % 2
                            for i in range(2):
                                bk = 4 + n % 4
                                n += 1
                                mms([(ps[:, bk, :], hown[:, kc, 128 * t8:128 * t8 + 128], wv2[i][:, kc, :], kc == 0, kc == KC - 1) for kc in range(KC)],
                                    wvk + hk, ['ps%d' % bk])
                                gelu_from_ps(ps[:, bk, :], vg[p][:, 512 * i:512 * i + 512], 'ps%d' % bk, 'vg%d_%d' % (p, i))
                            vk = ['vg%d_0' % p, 'vg%d_1' % p]
                            c0 = 8 * p
                            S.add('dve', lambda e, p=p, c0=c0: e.reduce_sum(out=st[:, c0:c0 + 1], in_=vg[p][:], axis=mybir.AxisListType.X), vk, ['st%d_0' % p])
                            act(junk[:], vg[p][:], AF.Square, vk, ['junk', 'st%d_1' % p], accum_out=st[:, c0 + 1:c0 + 2])
                            ts(st[:, c0 + 2:c0 + 3], st[:, c0:c0 + 1], 1.0 / 1024, None, ALU.mult, None, ['st%d_0' % p], ['st%d_2' % p])
                            tt(st[:, c0 + 3:c0 + 4], st[:, c0 + 2:c0 + 3], st[:, c0 + 2:c0 + 3], ALU.mult, ['st%d_2' % p], ['st%d_3' % p])
                            stt(st[:, c0 + 4:c0 + 5], st[:, c0 + 1:c0 + 2], 1.0 / 1024, st[:, c0 + 3:c0 + 4], ALU.mult, ALU.subtract, ['st%d_1' % p, 'st%d_3' % p], ['st%d_4' % p])
                            act(st[:, c0 + 5:c0 + 6], st[:, c0 + 4:c0 + 5], AF.Ln, ['st%d_4' % p], ['st%d_5' % p], bias=eps_t[:, 0:1])
                            act(st[:, c0 + 5:c0 + 6], st[:, c0 + 5:c0 + 6], AF.Exp, ['st%d_5' % p], ['st%d_5' % p], scale=-0.5)
                            ts(vg[p][:], vg[p][:], st[:, c0 + 2:c0 + 3], st[:, c0 + 5:c0 + 6], ALU.subtract, ALU.mult, vk + ['st%d_2' % p, 'st%d_5' % p], ['vgn%d' % p])
                            tt(vg[p][:], vg[p][:], lng[:, 0, :], ALU.mult, ['vgn%d' % p, 'lng'], ['vgn%d' % p])
                            tt(vn[:, t8, :], vg[p][:], lng[:, 1, :], ALU.add, ['vgn%d' % p, 'lng'], ['vn%d' % t8] + vk)
                        S.barrier()
                    n = 0
                    for ch in range(8):
                        g = ch // 2
                        for th in range(2):
                            bk = n % 4
                            n += 1
                            for t4 in range(4):
                                t8 = 4 * th + t4
                                mms([(ps[:, bk, 128 * t4:128 * t4 + 128], vn[:, t8, 128 * ch:128 * ch + 128], WT[:, g, :], True, True)],
                                    ['vn%d' % t8, 'WT%d' % g], ['ps%d_%d' % (bk, t4)])
                            i = n % 2
                            pk4 = ['ps%d_%d' % (bk, t4) for t4 in range(4)]
                            tt(g1[i][:], ps[:, bk, :], bsb[:, g, :], ALU.add, pk4 + ['bsb'], ['g1_%d' % i])
                            tt(odT[:, ch, 512 * th:512 * th + 512], g1[i][:], uT[:, ch, 512 * th:512 * th + 512], ALU.mult, ['g1_%d' % i, 'uT%d_%d' % (ch, th)], ['odT%d_%d' % (ch, th)] + pk4)
                    cp(hpk[:, 32:48].rearrange("p (k t) -> p k t", t=2), odT[:, :, TOK - 2:TOK], ['odT%d_1' % ch for ch in range(8)], ['hpk1'])
                    S.barrier()
                if 'odT' in dbg_out:
                    dma('sp', dbg_out['odT'], odT[:], [], ['dbg6'])

            if upto('ret'):
                with ExitStack() as es:
                    ocT = sb(es, "ocT", [128, 2, SEQ], BF16)
                    QKr = sb(es, "QKr", [128, 2, SEQ], BF16)
                    gate = sb(es, "gate", [128, 2, SEQ], BF16)
                    Vr = sb(es, "Vr", [128, 32, 256], BF16)
                    Ktm = sb(es, "Ktm", [128, 32, 128], BF16)
                    wC = sb(es, "wC", [128, KC, 1024], BF16)
                    hin = [sb(es, "hinC%d" % i, [128, KC, 512], BF16) for i in range(2)]
                    rc = sb(es, "retc", [128, 264], F32)
                    cs = [sb(es, "cs%d" % i, [128, 2, 512], F32) for i in range(2)]
                    cstm = [sb(es, "cstm%d" % i, [128, 4, 256], F32) for i in range(2)]
                    ra = sb(es, "ra", [128, 512], F32)
                    rb = sb(es, "rb", [128, 512], F32)
                    SD = [sb(es, "SD%d" % i, [128, 128], BF16) for i in range(2)]
                    Qs = [sb(es, "Qs%d" % i, [128, 128], BF16) for i in range(2)]
                    Sf = sb(es, "Sf", [128, 256], F32)
                    Sb = [sb(es, "Sb%d" % i, [128, 256], BF16) for i in range(2)]
                    yb = sb(es, "yb", [128, 2, 512], F32)
                    sqy = sb(es, "sqy", [128, 2, 512], BF16)
                    rsy = sb(es, "rsy", [128, 512], F32)
                    wcv = w_inC.rearrange("(kc p) n -> p kc n", p=128)
                    for i in range(4):
                        dma('pool', wC[:, :, 256 * i:256 * i + 256], wcv[:, :, 256 * i:256 * i + 256], [], ['wC%d' % i])
                    wck = ['wC%d' % i for i in range(4)]
                    dma('sp', rc[:], retc_d, [], ['retc'])
                    Dblk = rc[:, 0:128]
                    qdec = rc[:, 128:256]
                    kdec = rc[:, 256:257]
                    g128 = rc[:, 257:258]
                    retg = rc[:, 258:260]
                    S.add('dve', lambda e: e.memset(Sf[:], 0.0), [], ['Sf'])
                    ra2 = sb(es, "ra2", [128, 512], F32)
                    rk = [sb(es, "rk%d" % i, [128, 128], F32) for i in range(2)]

                    def loads(t8):
                        p = t8 % 2
                        r = t8 // 2
                        csl = slice(512 * (t8 % 2), 512 * (t8 % 2) + 512)
                        sl = slice(512 * t8, 512 * t8 + 512)
                        for q in range(2):
                            qq = 2 * (t8 % 2) + q
                            dma('sp', hin[p][:, 8 * q:8 * q + 8, :], cout2[qq][1024 * r:1024 * r + 1024, :].rearrange("(k p) t -> p k t", p=128), ['c2out%d' % qq], ['hinC%d_%d' % (p, q)])
                        dma('sp', cs[p][:, 0, :], cosT_d[:, sl], [], ['cs%d_0' % p])
                        dma('sp', cs[p][:, 1, :], sinT_d[:, sl], [], ['cs%d_1' % p])
                        dma('sp', cstm[p][:], cstm_d[:, 4 * t8:4 * t8 + 4, :], [], ['cstm%d' % p])

                    def inproj_groups(t8):
                        p = t8 % 2
                        sl = slice(512 * t8, 512 * t8 + 512)
                        hk = ['hinC%d_%d' % (p, q) for q in range(2)]
                        G = []

                        def g_qk(qk):
                            b0 = 2 * qk
                            for w in range(2):
                                g = 2 * qk + w
                                mms([(ps[:, b0 + w, :], wC[:, kc, 128 * g:128 * g + 128], hin[p][:, kc, :], kc == 0, kc == KC - 1) for kc in range(KC)], wck + hk, ['ps%d' % (b0 + w)])
                            tt(ra[:], ps[:, b0, :], cs[p][:, 0, :], ALU.mult, ['ps%d' % b0, 'cs%d_0' % p], ['ra'])
                            tt(rb[:], ps[:, b0 + 1, :], cs[p][:, 1, :], ALU.mult, ['ps%d' % (b0 + 1), 'cs%d_1' % p], ['rb'])
                            tt(QKr[:, qk, sl], ra[:], rb[:], ALU.add, ['ra', 'rb'], ['QKr%d_%d' % (qk, t8)])

                        def g_gate(hf):
                            g = 6 + hf
                            mms([(ps[:, hf, :], wC[:, kc, 128 * g:128 * g + 128], hin[p][:, kc, :], kc == 0, kc == KC - 1) for kc in range(KC)], wck + hk, ['ps%d' % hf])
                            act(gate[:, hf, sl], ps[:, hf, :], AF.Silu, ['ps%d' % hf], ['gate%d_%d' % (hf, t8)])

                        def g_v(sub):
                            blk = 4 * t8 + sub
                            tsl = slice(128 * sub, 128 * sub + 128)
                            mms([(ps[:, 2, 0:256], hin[p][:, kc, tsl], wC[:, kc, 512:768], kc == 0, kc == KC - 1) for kc in range(KC)], wck + hk, ['ps2'])
                            cp(Vr[:, blk, :], ps[:, 2, 0:256], ['ps2'], ['Vr%d' % blk], eng='act')

                        def g_k(sub):
                            blk = 4 * t8 + sub
                            tsl = slice(128 * sub, 128 * sub + 128)
                            mms([(ps[:, 3, 0:256], hin[p][:, kc, tsl], wC[:, kc, 256:512], kc == 0, kc == KC - 1) for kc in range(KC)], wck + hk, ['ps3'])
                            tt(rk[0][:], ps[:, 3, 0:128], cstm[p][:, sub, 0:128], ALU.mult, ['ps3', 'cstm%d' % p], ['rk0'])
                            tt(rk[1][:], ps[:, 3, 128:256], cstm[p][:, sub, 128:256], ALU.mult, ['ps3', 'cstm%d' % p], ['rk1'])
                            tt(rk[0][:], rk[0][:], rk[1][:], ALU.add, ['rk0', 'rk1'], ['rk0'])
                            ts(Ktm[:, blk, :], rk[0][:], kdec, None, ALU.mult, None, ['rk0', 'retc'], ['Ktm%d' % blk])
                        G.append(lambda: g_qk(0))
                        G.append(lambda: g_qk(1))
                        G.append(lambda: g_gate(0))
                        G.append(lambda: g_gate(1))
                        for sub in range(4):
                            G.append(lambda sub=sub: g_v(sub))
                            G.append(lambda sub=sub: g_k(sub))
                        return G

                    def rec_steps(t8):
                        R = []
                        for sub in range(4):
                            blk = 4 * t8 + sub
                            bsl = slice(128 * blk, 128 * blk + 128)
                            pb = blk % 2

                            def r_a(blk=blk, bsl=bsl, pb=pb):
                                mms([(ps[:, 4, 0:128], QKr[:, 1, bsl], QKr[:, 0, bsl], True, True)], ['QKr0_%d' % t8, 'QKr1_%d' % t8], ['ps4'])
                                tt(SD[pb][:], ps[:, 4, 0:128], Dblk, ALU.mult, ['ps4', 'retc'], ['SD%d' % pb])
                                if blk > 0:
                                    tt(Qs[pb][:], QKr[:, 0, bsl], qdec, ALU.mult, ['QKr0_%d' % t8, 'retc'], ['Qs%d' % pb])

                            def r_b(blk=blk, pb=pb, sub=sub):
                                lst = []
                                for hf in range(2):
                                    lst.append((ps[:, 5, 128 * hf:128 * hf + 128], Vr[:, blk, 128 * hf:128 * hf + 128], SD[pb][:], True, blk == 0))
                                rd = ['SD%d' % pb, 'Vr%d' % blk]
                                if blk > 0:
                                    for hf in range(2):
                                        lst.append((ps[:, 5, 128 * hf:128 * hf + 128], Sb[(blk - 1) % 2][:, 128 * hf:128 * hf + 128], Qs[pb][:], False, True))
                                    rd += ['Qs%d' % pb, 'Sb%d' % ((blk - 1) % 2)]
                                    lst = [lst[0], lst[2], lst[1], lst[3]]
                                mms(lst, rd, ['ps5'])
                                cp(yb[:, :, 128 * sub:128 * sub + 128], ps[:, 5, 0:256].rearrange("p (h t) -> p h t", h=2), ['ps5'], ['yb%d' % sub], eng='act')

                            def r_c(blk=blk):
                                mms([(ps[:, 6, 0:256], Ktm[:, blk, :], Vr[:, blk, :], True, True)], ['Ktm%d' % blk, 'Vr%d' % blk], ['ps6'])
                                stt(Sf[:], Sf[:], g128, ps[:, 6, 0:256], ALU.mult, ALU.add, ['Sf', 'ps6', 'retc'], ['Sf'])
                                cp(Sb[blk % 2][:], Sf[:], ['Sf'], ['Sb%d' % (blk % 2)])
                            R += [r_a, r_b, r_c]
                        return R

                    def norm_gate(t8):
                        sl = slice(512 * t8, 512 * t8 + 512)
                        ybk = ['yb%d' % sub for sub in range(4)]
                        act(sqy[:], yb[:], AF.Square, ybk, ['sqy'])
                        mms([(ps[:, 7, :], ones, sqy[:, hf, :], hf == 0, hf == 1) for hf in range(2)], ['sqy', 'tri'], ['ps7'])
                        rstd_from_ps(ps[:, 7, :], rsy[:], 256, ['ps7'], ['rsy'])
                        for hf in range(2):
                            stt(ra2[:], yb[:, hf, :], retg[:, hf:hf + 1], rsy[:], ALU.mult, ALU.mult, ybk + ['rsy', 'retc'], ['ra2'])
                            tt(ocT[:, hf, sl], ra2[:], gate[:, hf, sl], ALU.mult, ['ra2', 'gate%d_%d' % (hf, t8)], ['oT%d_%d' % (hf, t8)])
                        if t8 % 2 == 1 and upto('ag3'):
                            ag_slab(t8 // 2, cin3s, cout3a, ocT, 2, 'c3')

                    loads(0)
                    for t8 in range(9):
                        if t8 + 1 < 8:
                            loads(t8 + 1)
                        G = inproj_groups(t8) if t8 < 8 else []
                        R = rec_steps(t8 - 1) if t8 >= 1 else []
                        for k in range(max(len(G), len(R))):
                            if k < len(G):
                                G[k]()
                            if k < len(R):
                                R[k]()
                        if t8 >= 1:
                            norm_gate(t8 - 1)
                    if 'ocT' in dbg_out:
                        dma('sp', dbg_out['ocT'], ocT[:], ['oT%d_%d' % (hf, t8) for hf in range(2) for t8 in range(8)], ['dbg7'])
                    S.barrier()

            if upto('ag3'):
                dma('sp', cinH, hpk[:], ['hpk0', 'hpk1'], ['cinH'])
                S.add('pool', lambda e: e.collective_compute("AllGather", ALU.bypass, replica_groups=RG, ins=[cinH.opt()], outs=[coutH.opt()]),
                      ['cinH'], ['coutH'], dma='cc')
                dma('sp', hh[:], coutH[bass.ds(((rank + 3) % 4) * 128, 128), :], ['coutH'], ['hh'])
                ts(hh[:], hh[:], smallB[:, 3:4], None, ALU.mult, None, ['hh', 'smallB'], ['hh'])

            if upto('out1'):
                tok = ExitStack()
                xT = sb(tok, "xTb", [128, KC, TH], F32)
                aT = sb(tok, "aTb", [128, KC, TH], BF16)
                rs1 = sb(tok, "rs1b", [128, TH], F32)
                dma('sp', xT[:, :, 2:TH], x1s.rearrange("(kc p) t -> p kc t", p=128), ['x1s'], ['xTown'])
                cp(xT[:, :, 0:2], hh[:, 0:32].rearrange("p (k t) -> p k t", t=2), ['hh', 'xTown'], xTk)
                cv3 = cout3a.rearrange("(sk p) t -> p sk t", p=128)
                c3k = ['c3out%d' % t for t in range(4)]
                dma('sp', aT[:, 0:8, 2:TH], cv3[:, bass.ds(rank * 8, 8), :], c3k, ['aT%d' % kc for kc in range(8)])
                dma('sp', aT[:, 0:8, 0:2], cv3[:, bass.ds(((rank + 3) % 4) * 8, 8), TOK - 2:TOK], c3k, ['aTh3'])
                ts(aT[:, 0:8, 0:2], aT[:, 0:8, 0:2], smallB[:, 3:4], None, ALU.mult, None, ['aTh3', 'smallB'] + ['aT%d' % kc for kc in range(8)], ['aT%d' % kc for kc in range(8)])
                for ch in range(8):
                    cp(aT[:, 8 + ch, 2:TH], odT[:, ch, :], [], ['aT%d' % (8 + ch)], eng=('act' if ch % 2 else 'dve'))
                cp(aT[:, 8:16, 0:2], hh[:, 32:48].rearrange("p (k t) -> p k t", t=2), ['hh'] + ['aT%d' % (8 + ch) for ch in range(8)], ['aT%d' % (8 + ch) for ch in range(8)])
                S.barrier()
                out_proj(w_out1, 'o1')
                if 'xmid1' in dbg_out:
                    dma('sp', dbg_out['xmid1'].rearrange("(kc p) t -> p kc t", p=128), xT[:], xTk, ['dbg8'])
                if upto('ffn1'):
                    norm_to_aT(3, 0, TH)
                    conv_ffn(1)
                final_out(upto('final'))
                tok.close()
                S.barrier()
        S.emit()
    nc.used_inputs = used_inputs
    return nc


def _pp(v, n=KC):
    return np.ascontiguousarray(np.asarray(v, np.float32).reshape(n, 128).T)


def prep_inputs(inp):
    f = lambda k: np.asarray(inp[k], np.float32)
    x = f('x')
    cst = _consts()
    bidx = _bias_index()
    ab_in = f('ab_w_in')[0]
    ab_out = f('ab_w_out')[0]
    rel_bias = f('rel_bias')
    lam = f('diff_lambda')[0]
    subln = f('diff_subln_g')[0]
    gains = np.stack([_pp(f('norm_mix_g')[0]), _pp(f('norm_mix_g')[1]), _pp(f('norm_ffn_g')[0]), _pp(f('norm_ffn_g')[1]),
                      _pp(f('final_norm_g'))], axis=1)
    w_up_r, convp_r, w_down_r = [], [], []
    for l in range(2):
        wu = f('ffn_w_up')[l]
        w_up_r.append(np.ascontiguousarray(np.concatenate([wu[:, :DFF].reshape(D, NCH, 128), wu[:, DFF:].reshape(D, NCH, 128)], axis=2)))
        cw = f('ffn_conv_w')[l]
        cb = f('ffn_conv_b')[l]
        cp_ = np.zeros((128, NCH, 8), np.float32)
        for ag in range(2):
            for jj in range(3):
                cp_[:, :, 4 * ag + jj] = cw[jj, ag * DFF:(ag + 1) * DFF].reshape(NCH, 128).T
            cp_[:, :, 4 * ag + 3] = cb[ag * DFF:(ag + 1) * DFF].reshape(NCH, 128).T
        convp_r.append(cp_)
        w_down_r.append(np.ascontiguousarray(f('ffn_w_down')[l]))
    perm = []
    for r in range(4):
        for g in range(4):
            base = [128 * (2 * r), 128 * (2 * r + 1), 1024 + 256 * r, 1024 + 256 * r + 128][g]
            perm += list(range(base, base + 128))
    w_out0 = np.ascontiguousarray(ab_out[perm, :])
    cd_in = f('cd_w_in')[0]
    cd_out = f('cd_w_out')[0]
    sgw = f('sgu_w')[0]
    sguw = np.ascontiguousarray(sgw.transpose(2, 0, 1))
    jj = np.arange(128)[:, None]
    ii = np.arange(128)[None, :]
    sgum = ((jj // 64) <= (ii // 64)).astype(np.float32)
    lngb = np.ascontiguousarray(np.broadcast_to(np.stack([f('sgu_ln_g')[0], f('sgu_ln_b')[0]])[None], (128, 2, 1024)))
    sgub = np.ascontiguousarray(np.broadcast_to(np.tile(f('sgu_b')[0], (1, 4))[None], (128, 4, 512)))
    cos, sin = _rot_tables(None)
    cosT = np.ascontiguousarray(cos.T)
    sinT = np.ascontiguousarray(sin.T)
    cstm = np.ascontiguousarray(np.concatenate([cos, sin], -1).reshape(32, 128, 256).transpose(1, 0, 2))
    retg = f('ret_norm_g')[0]
    perm1 = list(range(2048))
    w_out1 = np.ascontiguousarray(cd_out[perm1, :])
    maps = []
    for c in range(8):
        b, j = divmod(c, 4)
        m = {}
        m['w_zd'] = np.ascontiguousarray(cd_in[:, 3072:5120])
        m['sguw'] = sguw
        m['sgum'] = sgum
        m['lngb'] = lngb
        m['sgub'] = sgub
        sw = (np.arange(128) + 64) % 128
        qcols = np.arange(128 * j, 128 * j + 128)
        kcols = 512 + qcols
        m['w_inC'] = np.ascontiguousarray(np.concatenate([cd_in[:, qcols], cd_in[:, qcols[sw]], cd_in[:, kcols], cd_in[:, kcols[sw]],
                                                          cd_in[:, 1024 + 256 * j:1024 + 256 * j + 256], cd_in[:, 2048 + 256 * j:2048 + 256 * j + 256]], axis=1))
        log_g = np.log(np.float32(1.0) - np.float32(2.0) ** np.float32(-5.0 - j)).astype(np.float32)
        sidx = np.arange(128, dtype=np.float32)
        rcst = np.zeros((128, 264), np.float32)
        sc = np.float32(128.0 ** -0.5)
        rcst[:, 0:128] = np.exp(log_g * np.abs(sidx[None, :] - sidx[:, None])) * ((jj // 64) <= (ii // 64)) * sc
        rcst[:, 128:256] = (np.exp(log_g * (sidx + 1.0)) * sc)[None, :]
        rcst[:, 256] = np.exp(log_g * (127.0 - sidx))
        rcst[:, 257] = np.exp(log_g * np.float32(128.0))
        rcst[:, 258] = retg[0:128]
        rcst[:, 259] = retg[128:256]
        m['retc'] = rcst
        m['cosT'] = cosT
        m['sinT'] = sinT
        m['cstm'] = cstm
        m['w_out1'] = w_out1
        m['xT_seq'] = np.ascontiguousarray(x[b].T)
        xo = np.zeros((D, TH), np.float32)
        xo[:, 2:] = x[b, TOK * j:TOK * j + TOK].T
        if j > 0:
            xo[:, 0:2] = x[b, TOK * j - 2:TOK * j].T
        m['xT_own'] = xo
        m['gains'] = gains
        m['tri'] = cst['tri']
        m['masks'] = cst['masks']
        h0, h1 = 2 * j, 2 * j + 1
        colsA = []
        for h in (h0, h1):
            colsA += list(range(128 * h, 128 * h + 128)) + list(range(1024 + 128 * h, 1024 + 128 * h + 128))
        for h in (h0, h1):
            colsA += list(range(2048 + 128 * h, 2048 + 128 * h + 128))
        m['w_inA'] = np.ascontiguousarray(ab_in[:, colsA])
        colsB = []
        for base in (3072, 4096):
            for cc in range(2):
                colsB += list(range(base + 256 * j + 128 * cc, base + 256 * j + 128 * cc + 128))
        colsB += list(range(5120 + 256 * j, 5120 + 256 * j + 256))
        m['w_inB'] = np.ascontiguousarray(ab_in[:, colsB])
        m['biasM'] = np.ascontiguousarray(rel_bias[bidx, j]).astype(np.float32)
        sB = np.zeros((128, 8), np.float32)
        sB[:, 0] = rel_bias[15, j]
        sB[:, 1] = subln[0:128]
        sB[:, 2] = subln[128:256]
        sB[:, 3] = 0.0 if j == 0 else 1.0
        m['smallB'] = sB
        m['lamv'] = np.ascontiguousarray(np.broadcast_to(lam.reshape(1, 512), (128, 512)))
        m['w_out0'] = w_out0
        for l in range(2):
            m['w_up%d' % l] = w_up_r[l]
            m['convp%d' % l] = convp_r[l]
            m['w_down%d' % l] = w_down_r[l]
        maps.append(m)
    return maps


_NC_CACHE = {}


def kernel(**inputs):
    maps = prep_inputs(inputs)
    if 'nc' not in _NC_CACHE:
        _NC_CACHE['nc'] = build()
    ui = _NC_CACHE['nc'].used_inputs
    maps = [{k: m[k] for k in ui} for m in maps]
    res = run_bass_kernel_spmd(_NC_CACHE['nc'], maps, core_ids=list(range(8)))
    out = np.zeros((NB, SEQ, D), np.float32)
    for c in range(8):
        b, j = divmod(c, 4)
        out[b, TOK * j:TOK * j + TOK, :] = res.results[c]['yT'].T
    return out
```
